# Optimizing a Trainium2 kernel written in Bass

```python
import math, functools
import jax, jax.numpy as jnp
from jax import lax
import numpy as np

D_MODEL = 4096
BATCH = 8
SEQ = 2048
DEPTH = 1
DEC_BATCH = 32
DEC_SEQ = 32
PAST_LEN = 2048

CHUNK = 64
GMLP_CHUNK = 128
D_A = D_MODEL // 2
G_A = 8
HEAD_DIM = 128
N_HEADS = D_MODEL // 256
N_KV_HEADS = N_HEADS // 4
ROT_DIM = HEAD_DIM // 4
N_IDX_HEADS = 16
IDX_DIM = 64
IDX_ROT = IDX_DIM // 4
TOPK_MAX = 256
QBLOCK = 64
ROPE_THETA = 500000.0
D_FF = 256 * ((8 * D_MODEL // 3 + 255) // 256)
CONV_WIDTH = 3
EPS = 1e-6
IN_SIZES = (D_A, D_A, N_HEADS * HEAD_DIM, N_KV_HEADS * HEAD_DIM, N_KV_HEADS * HEAD_DIM,
            N_IDX_HEADS * IDX_DIM, IDX_DIM, N_IDX_HEADS, D_MODEL, D_MODEL)
IN_COLS = sum(IN_SIZES)

kernel_name = "hybrid_gmlp_dsa_convffn_stream_step"


def rms_norm(x, g):
    xf = x.astype(jnp.float32)
    y = xf * lax.rsqrt(jnp.mean(xf * xf, axis=-1, keepdims=True) + EPS)
    return (y * g.astype(jnp.float32)).astype(x.dtype)


def partial_rope(x, pos, rot_dim):
    half = rot_dim // 2
    inv_freq = ROPE_THETA ** (-jnp.arange(half, dtype=jnp.float32) / half)
    ang = pos.astype(jnp.float32)[:, None] * inv_freq[None, :]
    cos = jnp.cos(ang)[None, :, None, :]
    sin = jnp.sin(ang)[None, :, None, :]
    xf = x.astype(jnp.float32)
    x1 = xf[..., :half]
    x2 = xf[..., half:rot_dim]
    out = jnp.concatenate([x1 * cos - x2 * sin, x2 * cos + x1 * sin, xf[..., rot_dim:]], axis=-1)
    return out.astype(x.dtype)


def project(h, w_in, pos):
    B, T, _ = h.shape
    offs = [int(o) for o in np.cumsum(IN_SIZES)[:-1]]
    u_a, v_a, q, k, v, q_idx, k_idx, w_idx, g_a, g_b = jnp.split(h @ w_in, offs, axis=-1)
    q = partial_rope(q.reshape(B, T, N_HEADS, HEAD_DIM), pos, ROT_DIM)
    k = partial_rope(k.reshape(B, T, N_KV_HEADS, HEAD_DIM), pos, ROT_DIM)
    v = v.reshape(B, T, N_KV_HEADS, HEAD_DIM)
    q_idx = partial_rope(q_idx.reshape(B, T, N_IDX_HEADS, IDX_DIM), pos, IDX_ROT)
    k_idx = partial_rope(k_idx[:, :, None, :], pos, IDX_ROT)[:, :, 0]
    w_idx = w_idx * (N_IDX_HEADS ** -0.5 * IDX_DIM ** -0.5)
    return u_a, v_a, q, k, v, q_idx, k_idx, w_idx, g_a, g_b


def gmlp_spatial(u, v, g_norm, ws, b):
    B, T, _ = v.shape
    vn = rms_norm(v, g_norm)
    n_c = -(-T // GMLP_CHUNK)
    tp = n_c * GMLP_CHUNK
    vp = jnp.pad(vn, ((0, 0), (0, tp - T), (0, 0))).reshape(B, n_c, GMLP_CHUNK, G_A, D_A // G_A)
    i = jnp.arange(GMLP_CHUNK)
    mask = (i[None, :] // CHUNK) <= (i[:, None] // CHUNK)
    wm = jnp.where(mask[None], ws, 0)
    s = jnp.einsum('gij,bcjgd->bcigd', wm, vp) + jnp.transpose(b)[None, None, :, :, None]
    s = s.reshape(B, tp, D_A)[:, :T]
    return u * s, vn


def dsa_attend(q, q_idx, w_idx, qpos, k_all, v_all, kidx_all, kpos, topk):
    B, Tq = q.shape[:2]
    admissible = (kpos[None, :] // CHUNK) <= (qpos[:, None] // CHUNK)
    rel = jax.nn.relu(jnp.einsum('bthd,bsd->bths', q_idx, kidx_all).astype(jnp.float32))
    score = jnp.einsum('bths,bth->bts', rel, w_idx.astype(jnp.float32))
    score = jnp.where(admissible[None], score, -jnp.inf)
    _, idx = lax.top_k(score, topk)
    k_sel = jax.vmap(lambda kk, ii: kk[ii])(k_all, idx)
    v_sel = jax.vmap(lambda vv, ii: vv[ii])(v_all, idx)
    valid = (kpos[idx] // CHUNK) <= (qpos[None, :, None] // CHUNK)
    qg = q.reshape(B, Tq, N_KV_HEADS, N_HEADS // N_KV_HEADS, HEAD_DIM)
    logits = jnp.einsum('btngd,btsnd->btngs', qg, k_sel).astype(jnp.float32) * (HEAD_DIM ** -0.5)
    logits = jnp.where(valid[:, :, None, None, :], logits, -jnp.inf)
    p = jax.nn.softmax(logits, axis=-1).astype(v_sel.dtype)
    out = jnp.einsum('btngs,btsnd->btngd', p, v_sel)
    return out.reshape(B, Tq, N_HEADS * HEAD_DIM)


def dsa_prompt(q, q_idx, w_idx, pos, k, v, k_idx, topk):
    B, S = q.shape[:2]
    nb = S // QBLOCK

    def blk(a):
        return jnp.swapaxes(a.reshape((B, nb, QBLOCK) + a.shape[2:]), 0, 1)

    def one(args):
        qb, qib, wib, posb = args
        return dsa_attend(qb, qib, wib, posb, k, v, k_idx, pos, topk)

    out = lax.map(one, (blk(q), blk(q_idx), blk(w_idx), pos.reshape(nb, QBLOCK)))
    return jnp.swapaxes(out, 0, 1).reshape(B, S, N_HEADS * HEAD_DIM)


def merge_branches(a_out, b_out, g_a, g_b, w_a, w_b, w_o):
    y = jax.nn.sigmoid(g_a) * (a_out @ w_a) + jax.nn.sigmoid(g_b) * (b_out @ w_b)
    return y @ w_o


def conv_ffn(h, buf, w_up, conv_w, conv_b, w_down):
    T = h.shape[1]
    z = h @ w_up
    zp = jnp.concatenate([buf.astype(z.dtype), z], axis=1)
    c = conv_b
    for tap in range(CONV_WIDTH):
        c = c + conv_w[tap] * zp[:, tap:tap + T]
    gate, up = jnp.split(c, 2, axis=-1)
    return (jax.nn.silu(gate) * up) @ w_down, zp[:, -(CONV_WIDTH - 1):]


def setup_inputs(seed: int = 0) -> dict:
    key = jax.random.key(seed)
    ks = jax.random.split(key, 24)
    f32 = jnp.float32

    def nrm(k, shape, scale=1.0):
        return jax.random.normal(k, shape, f32) * scale

    return {
        "x_prompt": nrm(ks[0], (BATCH, SEQ, D_MODEL)),
        "x_sample": nrm(ks[1], (DEC_BATCH, DEC_SEQ, D_MODEL)),
        "cache_k": nrm(ks[2], (DEPTH, DEC_BATCH, PAST_LEN, N_KV_HEADS, HEAD_DIM)),
        "cache_v": nrm(ks[3], (DEPTH, DEC_BATCH, PAST_LEN, N_KV_HEADS, HEAD_DIM)),
        "cache_kidx": nrm(ks[4], (DEPTH, DEC_BATCH, PAST_LEN, IDX_DIM)),
        "state_ffn_conv": nrm(ks[5], (DEPTH, DEC_BATCH, CONV_WIDTH - 1, 2 * D_FF)),
        "norm_attn_g": 1.0 + nrm(ks[6], (DEPTH, D_MODEL), 0.02),
        "w_in": nrm(ks[7], (DEPTH, D_MODEL, IN_COLS), D_MODEL ** -0.5),
        "gmlp_norm_g": 1.0 + nrm(ks[8], (DEPTH, D_A), 0.02),
        "gmlp_ws": nrm(ks[9], (DEPTH, G_A, GMLP_CHUNK, GMLP_CHUNK), GMLP_CHUNK ** -0.5),
        "gmlp_b": nrm(ks[10], (DEPTH, G_A, GMLP_CHUNK), 0.02),
        "w_branch_a": nrm(ks[11], (DEPTH, D_A, D_MODEL), D_A ** -0.5),
        "w_branch_b": nrm(ks[12], (DEPTH, N_HEADS * HEAD_DIM, D_MODEL), (N_HEADS * HEAD_DIM) ** -0.5),
        "w_out": nrm(ks[13], (DEPTH, D_MODEL, D_MODEL), D_MODEL ** -0.5),
        "norm_ffn_g": 1.0 + nrm(ks[14], (DEPTH, D_MODEL), 0.02),
        "w_up": nrm(ks[15], (DEPTH, D_MODEL, 2 * D_FF), D_MODEL ** -0.5),
        "conv_w": nrm(ks[16], (DEPTH, CONV_WIDTH, 2 * D_FF), CONV_WIDTH ** -0.5),
        "conv_b": nrm(ks[17], (DEPTH, 2 * D_FF), 0.02),
        "w_down": nrm(ks[18], (DEPTH, D_FF, D_MODEL), D_FF ** -0.5),
        "norm_final_g": 1.0 + nrm(ks[19], (D_MODEL,), 0.02),
    }


def reference(x_prompt, x_sample, cache_k, cache_v, cache_kidx, state_ffn_conv,
              norm_attn_g, w_in, gmlp_norm_g, gmlp_ws, gmlp_b, w_branch_a, w_branch_b,
              w_out, norm_ffn_g, w_up, conv_w, conv_b, w_down, norm_final_g):
    B, S, _ = x_prompt.shape
    DB, T, _ = x_sample.shape
    P = cache_k.shape[2]
    pos_p = jnp.arange(S, dtype=jnp.int32)
    pos_s = P + jnp.arange(T, dtype=jnp.int32)
    kpos_s = jnp.arange(P + T, dtype=jnp.int32)
    topk_p = min(TOPK_MAX, S // 4)
    topk_s = min(TOPK_MAX, (P + T) // 4)

    xp, xs = x_prompt, x_sample
    kp_l, vp_l, kip_l, cp_l = [], [], [], []
    ks_l, vs_l, kis_l, cs_l, gv_l = [], [], [], [], []
    for l in range(DEPTH):
        hp = rms_norm(xp, norm_attn_g[l])
        u_a, v_a, q, k, v, qi, ki, wi, ga, gb = project(hp, w_in[l], pos_p)
        a_out, _ = gmlp_spatial(u_a, v_a, gmlp_norm_g[l], gmlp_ws[l], gmlp_b[l])
        b_out = dsa_prompt(q, qi, wi, pos_p, k, v, ki, topk_p)
        xp = xp + merge_branches(a_out, b_out, ga, gb, w_branch_a[l], w_branch_b[l], w_out[l])
        buf0 = jnp.zeros((B, CONV_WIDTH - 1, 2 * D_FF), dtype=xp.dtype)
        f_out, cbuf_p = conv_ffn(rms_norm(xp, norm_ffn_g[l]), buf0, w_up[l], conv_w[l], conv_b[l], w_down[l])
        xp = xp + f_out
        kp_l.append(k); vp_l.append(v); kip_l.append(ki); cp_l.append(cbuf_p)

        hs = rms_norm(xs, norm_attn_g[l])
        u_a, v_a, q, k, v, qi, ki, wi, ga, gb = project(hs, w_in[l], pos_s)
        a_out, vn = gmlp_spatial(u_a, v_a, gmlp_norm_g[l], gmlp_ws[l], gmlp_b[l])
        k_all = jnp.concatenate([cache_k[l].astype(k.dtype), k], axis=1)
        v_all = jnp.concatenate([cache_v[l].astype(v.dtype), v], axis=1)
        ki_all = jnp.concatenate([cache_kidx[l].astype(ki.dtype), ki], axis=1)
        b_out = dsa_attend(q, qi, wi, pos_s, k_all, v_all, ki_all, kpos_s, topk_s)
        xs = xs + merge_branches(a_out, b_out, ga, gb, w_branch_a[l], w_branch_b[l], w_out[l])
        f_out, cbuf_s = conv_ffn(rms_norm(xs, norm_ffn_g[l]), state_ffn_conv[l], w_up[l], conv_w[l], conv_b[l], w_down[l])
        xs = xs + f_out
        ks_l.append(k); vs_l.append(v); kis_l.append(ki); cs_l.append(cbuf_s); gv_l.append(vn)

    y_prompt = rms_norm(xp, norm_final_g)
    y_sample = rms_norm(xs, norm_final_g)
    new_cache_k_prompt = jnp.stack(kp_l)
    new_cache_v_prompt = jnp.stack(vp_l)
    new_cache_kidx_prompt = jnp.stack(kip_l)
    new_state_ffn_conv_prompt = jnp.stack(cp_l)
    new_cache_k_sample = jnp.stack(ks_l)
    new_cache_v_sample = jnp.stack(vs_l)
    new_cache_kidx_sample = jnp.stack(kis_l)
    new_state_ffn_conv_sample = jnp.stack(cs_l)
    new_state_gmlp_v_sample = jnp.stack(gv_l)
    return (y_prompt, y_sample, new_cache_k_prompt, new_cache_v_prompt, new_cache_kidx_prompt,
            new_state_ffn_conv_prompt, new_cache_k_sample, new_cache_v_sample, new_cache_kidx_sample,
            new_state_ffn_conv_sample, new_state_gmlp_v_sample)
```

```python
import numpy as np
import ml_dtypes
from contextlib import ExitStack
import concourse.bass as bass
import concourse.mybir as mybir
from concourse.bass_utils import run_bass_kernel_spmd

F32 = mybir.dt.float32
BF16 = mybir.dt.bfloat16
ALU = mybir.AluOpType
AF = mybir.ActivationFunctionType
AX = mybir.AxisListType

ENGS = ("pe", "act", "dve", "pool", "sp")
NDMASEM = 16


class Buf:
    __slots__ = ("name", "w", "r", "rd")

    def __init__(self, name=""):
        self.name = name
        self.w = None
        self.r = {}
        self.rd = []


class Op:
    __slots__ = ("eng", "fn", "deps", "sig", "sem", "val", "dma")

    def __init__(self, eng, fn, dma):
        self.eng = eng
        self.fn = fn
        self.dma = dma
        self.deps = []
        self.sig = dma
        self.sem = None
        self.val = 0


class Prog:
    def __init__(self, nc):
        self.nc = nc
        self.q = {e: [] for e in ENGS}
        self.pend_dma = []

    def op(self, eng, fn, reads=(), writes=(), dma=False):
        o = Op(eng, fn, dma)
        deps = {}
        for b in reads:
            if b.w is not None:
                deps[id(b.w)] = b.w
        for b in writes:
            if b.w is not None:
                deps[id(b.w)] = b.w
            for d in b.r.values():
                deps[id(d)] = d
            for d in b.rd:
                deps[id(d)] = d
        for d in deps.values():
            if d is o:
                continue
            if eng == "pe" and d.eng == "pe" and not d.dma and not dma:
                continue
            d.sig = True
            o.deps.append(d)
        for b in reads:
            if dma:
                b.rd.append(o)
            else:
                b.r[eng] = o
        for b in writes:
            b.w = o
            b.r = {}
            b.rd = []
        self.q[eng].append(o)
        if dma:
            self.pend_dma.append(o)
        return o

    def dma(self, q, out, in_, reads=(), writes=(), **kw):
        return self.op(q, lambda e: e.dma_start(out=out, in_=in_, **kw), reads, writes, dma=True)

    def fence(self):
        lasts = list(self.pend_dma)
        self.pend_dma = []
        for e in ENGS:
            for o in reversed(self.q[e]):
                if o.fn is not None and not o.dma:
                    o.sig = True
                    lasts.append(o)
                    break
        for e in ENGS:
            b = Op(e, None, False)
            b.deps = list(lasts)
            self.q[e].append(b)

    def emit(self):
        nc = self.nc
        with ExitStack() as st:
            esem = {e: st.enter_context(nc.semaphore("s_" + e)) for e in ENGS}
            dsem = {e: [st.enter_context(nc.semaphore("d_%s%d" % (e, i))) for i in range(NDMASEM)]
                    for e in ("sp", "pool", "act")}
            for e in ENGS:
                cnt = 0
                dcnt = [0] * NDMASEM
                di = 0
                for o in self.q[e]:
                    if o.dma:
                        k = di % NDMASEM
                        di += 1
                        dcnt[k] += 16
                        o.sem = dsem[e][k]
                        o.val = dcnt[k]
                    elif o.sig:
                        cnt += 1
                        o.sem = esem[e]
                        o.val = cnt
            block = st.enter_context(nc.Block())

            def run(eng_obj, e):
                waited = {}
                for o in self.q[e]:
                    if o.fn is None:
                        best = {}
                        for d in o.deps:
                            if id(d.sem) not in best or best[id(d.sem)].val < d.val:
                                best[id(d.sem)] = d
                        o.deps = list(best.values())
                    for d in o.deps:
                        key = id(d.sem)
                        if waited.get(key, 0) < d.val:
                            eng_obj.wait_ge(d.sem, d.val)
                            waited[key] = d.val
                    if o.fn is None:
                        continue
                    if o.dma and o.val > 16 and waited.get(id(o.sem), 0) < o.val - 16:
                        eng_obj.wait_ge(o.sem, o.val - 16)
                        waited[id(o.sem)] = o.val - 16
                    ins = o.fn(eng_obj)
                    if o.dma:
                        ins.then_inc(o.sem, 16)
                    elif o.sig:
                        ins.then_inc(o.sem, 1)

            @block.tensor
            def _(t):
                run(t, "pe")

            @block.scalar
            def _(a):
                run(a, "act")

            @block.vector
            def _(v):
                run(v, "dve")

            @block.gpsimd
            def _(g):
                run(g, "pool")

            @block.sync
            def _(s):
                run(s, "sp")


D = 4096
NP = 2048
NS = 128
N = NP + NS
NT = N // 128
DA = 2048
DFF = 11008
NH = 16
NKV = 4
HD = 128
NIH = 16
IDD = 64
PAST = 2048
TOPK = 256
EPS = 1e-6
THETA = 500000.0
U0, VA0, Q0, K0, V0, QI0, KI0, WI0, GA0, GB0, INC = 0, 2048, 4096, 6144, 6656, 7168, 8192, 8256, 8272, 12368, 16464
PJ_VA, PJ_Q, PJ_K, PJ_V, PJ_QI, PJ_KI, PJ_WI, PJW = 0, 2048, 4096, 4608, 5120, 6144, 6208, 6224
ARENA_F32 = 52992
NEG = -1.0e30
TOKG = [(0, 512), (512, 512), (1024, 512), (1536, 512), (2048, 128)]


class Arena:
    def __init__(self, ap):
        self.ap = ap
        self.off = 0

    def mark(self):
        return self.off

    def reset(self, m=0):
        self.off = m

    def alloc(self, shape_free, dtype):
        n = 1
        for s in shape_free:
            n *= s
        bpe = 2 if dtype == BF16 else 4
        nbytes = (n * bpe + 31) // 32 * 32
        assert self.off + nbytes <= ARENA_F32 * 4, ("arena overflow", self.off, nbytes)
        a = self.ap[:, self.off // 4:(self.off + nbytes) // 4]
        self.off += nbytes
        if dtype == BF16:
            a = a.bitcast(BF16)
        a = a[:, 0:n]
        if len(shape_free) == 2:
            a = a.rearrange("p (a b) -> p a b", a=shape_free[0])
        elif len(shape_free) == 3:
            a = a.rearrange("p (a b c) -> p a b c", a=shape_free[0], b=shape_free[1])
        return a


def build_program(dbg=None):
    dbg = dbg or {}
    stop = dbg.get("stop")
    dbg_outs = set(dbg.get("outs", ()))
    only = dbg.get("only")
    cut = dbg.get("cut", 99)

    def want(name):
        return only is None or name in only
    nc = bass.Bass("TRN2", target_bir_lowering=False)
    P = Prog(nc)

    def din(name, shape, dt=F32):
        return nc.dram_tensor(name, list(shape), dt, kind="ExternalInput").ap()

    def dout(name, shape, dt=F32):
        return nc.dram_tensor(name, list(shape), dt, kind="ExternalOutput").ap()

    def dscr(name, shape, dt):
        kind = "ExternalOutput" if name in dbg_outs else "Internal"
        return nc.dram_tensor(name, list(shape), dt, kind=kind).ap()

    xin = din("xin", [N, D])
    ck = din("ck", [4, PAST, 512])
    cv = din("cv", [4, PAST, 512])
    cki = din("cki", [4, PAST, IDD])
    cst = din("cst", [8, 2 * DFF])
    g_attn = din("g_attn", [D])
    w_in = din("w_in", [D, INC])
    g_gmlp = din("g_gmlp", [DA])
    ws = din("ws", [8, 128, 128])
    gbias = din("gbias", [8, 128])
    w_a = din("w_a", [DA, D])
    w_b = din("w_b", [DA, D])
    w_o = din("w_o", [D, D])
    g_ffn = din("g_ffn", [D])
    w_up = din("w_up", [D, 2 * DFF])
    conv_w = din("conv_w", [3, 2 * DFF])
    conv_b = din("conv_b", [2 * DFF])
    w_down = din("w_down", [DFF, D])
    g_final = din("g_final", [D])
    c_ident = din("c_ident", [128, 128])
    c_csq = din("c_csq", [N, 32])
    c_csi = din("c_csi", [N, 16])
    c_bd = din("c_bd", [128, 128])
    o_y = dout("o_y", [N, D])
    o_k = dout("o_k", [N, 512])
    o_v = dout("o_v", [N, 512])
    o_ki = dout("o_ki", [N, IDD])
    o_cst = dout("o_cst", [10, 2 * DFF])
    o_vn = dout("o_vn", [NS, DA])
    out_bufs = []
    projT = dscr("projT", [N, PJW], F32)
    uT = dscr("uT", [DA, N], BF16)
    gaT = dscr("gaT", [D, N], BF16)
    gbT = dscr("gbT", [D, N], BF16)
    yT = dscr("yT", [D, N], BF16)
    x2 = dscr("x2", [N, D], F32)
    actT = dscr("actT", [DFF, N], BF16)
    fsc = dscr("fsc", [N, D], F32)
    wdc_t = dscr("wdc", [32, 128, (DFF // 128) * 128], BF16)
    wdc = [wdc_t[c].rearrange("p (k j) -> p k j", j=128) for c in range(32)]
    Bwdc = [Buf("wdc%d" % c) for c in range(32)]
    aTd = dscr("aTd", [DA, N], BF16)
    bTd = dscr("bTd", [DA, N], BF16)
    B_projT, B_uT, B_gaT, B_gbT, B_yT, B_x2, B_actT, B_fsc, B_aTd, B_bTd = [Buf(n) for n in
        ("projT", "uT", "gaT", "gbT", "yT", "x2", "actT", "fsc", "aTd", "bTd")]

    arena_t = nc.alloc_sbuf_tensor("arena", [128, ARENA_F32], F32)
    AR = Arena(arena_t[:])
    banks = [nc.alloc_psum_tensor("bank%d" % i, [128, 512], F32) for i in range(8)]
    Bbank = [Buf("bank%d" % i) for i in range(8)]
    bankctr = [0]

    def next_bank():
        i = bankctr[0] % 8
        bankctr[0] += 1
        return banks[i], Bbank[i]

    evctr = [0]

    def evac_eng():
        evctr[0] += 1
        return "act" if evctr[0] % 2 else "dve"

    def copy_op(eng, out, in_):
        if eng == "act":
            return lambda e: e.activation(out=out, in_=in_, func=AF.Copy)
        return lambda e: e.tensor_copy(out=out, in_=in_)

    identf = AR.alloc([128], F32)
    identb = AR.alloc([128], BF16)
    B_ident = Buf("ident")
    P.dma("sp", identf, c_ident[:, :], writes=[B_ident])
    P.op("dve", lambda e: e.tensor_copy(out=identb, in_=identf), [B_ident], [B_ident])
    base_mark = AR.mark()

    def norm_transpose(src, Bsrc, gvec, hT, B_hT):
        m = AR.mark()
        g_bc = AR.alloc([D], F32)
        xt = [AR.alloc([D], F32) for _ in range(2)]
        hb = AR.alloc([D], BF16)
        junk = AR.alloc([D], BF16)
        st = AR.alloc([3 * NT], F32)
        Bg, Bxt, Bhb, Bjunk = Buf("g"), [Buf("xt0"), Buf("xt1")], Buf("hb"), Buf("junk")
        Bst = [Buf("st%d" % t) for t in range(NT)]
        P.dma("sp", g_bc, gvec.partition_broadcast(128), writes=[Bg])
        for t in range(NT):
            x_t, Bx = xt[t % 2], Bxt[t % 2]
            P.dma("sp", x_t, src[t * 128:(t + 1) * 128, :], reads=[Bsrc], writes=[Bx])
            ss, sd, rs = st[:, 3 * t:3 * t + 1], st[:, 3 * t + 1:3 * t + 2], st[:, 3 * t + 2:3 * t + 3]
            P.op("act", lambda e, x_t=x_t, ss=ss: e.activation(out=junk, in_=x_t, func=AF.Square, accum_out=ss),
                 [Bx], [Bjunk, Bst[t]])
            P.op("act", lambda e, ss=ss, sd=sd: e.activation(out=sd, in_=ss, func=AF.Sqrt, scale=1.0 / D, bias=EPS),
                 [Bst[t]], [Bst[t]])
            P.op("dve", lambda e, sd=sd, rs=rs: e.reciprocal(out=rs, in_=sd), [Bst[t]], [Bst[t]])
            P.op("dve", lambda e, x_t=x_t, rs=rs: e.scalar_tensor_tensor(out=hb, in0=x_t, scalar=rs, in1=g_bc,
                                                                     op0=ALU.mult, op1=ALU.mult),
                 [Bx, Bst[t], Bg], [Bhb])
            for q4 in range(4):
                bank, Bb = next_bank()
                pb = bank[:].bitcast(BF16)
                for j in range(8):
                    kc = q4 * 8 + j
                    P.op("pe", lambda e, pb=pb, j=j, kc=kc: e.transpose(out=pb[:, j * 128:(j + 1) * 128],
                                                                      in_=hb[:, kc * 128:(kc + 1) * 128],
                                                                      identity=identb),
                         [Bhb, B_ident], [Bb])
                eng = evac_eng()
                o_ap = hT[:, q4 * 8:(q4 + 1) * 8, t * 128:(t + 1) * 128]
                i_ap = pb.rearrange("p (a b) -> p a b", a=8)
                P.op(eng, copy_op(eng, o_ap, i_ap), [Bb], [B_hT])
        P.fence()
        AR.reset(m)

    def gemm(xT, B_xT, KC, W, blocks, epilogue_T=None, epilogue_F=None, tok_groups=TOKG, tiles=range(NT),
             wcols=512):
        m = AR.mark()
        wb = [AR.alloc([KC, wcols], BF16) for _ in range(2)]
        Bwb = [Buf("wb0"), Buf("wb1")]
        Wv = W.rearrange("(kc p) c -> p kc c", p=128)

        def load(i):
            c0, ncols, mode, tag = blocks[i]
            P.dma("pool", wb[i % 2][:, :, 0:ncols], Wv[:, :, c0:c0 + ncols], writes=[Bwb[i % 2]])

        load(0)
        for i, (c0, ncols, mode, tag) in enumerate(blocks):
            if i + 1 < len(blocks):
                load(i + 1)
            w_i, Bw = wb[i % 2], Bwb[i % 2]
            if mode == "T":
                for t in tiles:
                    bank, Bb = next_bank()
                    for kc in range(KC):
                        P.op("pe", lambda e, bank=bank, kc=kc, t=t, w_i=w_i, ncols=ncols:
                             e.matmul(bank[:, 0:ncols], lhsT=xT[:, kc, t * 128:(t + 1) * 128],
                                      rhs=w_i[:, kc, 0:ncols], start=(kc == 0), stop=(kc == KC - 1)),
                             [B_xT, Bw], [Bb])
                    epilogue_T(tag, c0, ncols, t, bank, Bb)
            else:
                for cj in range(ncols // 128):
                    for (tk0, ntk) in tok_groups:
                        bank, Bb = next_bank()
                        for kc in range(KC):
                            P.op("pe", lambda e, bank=bank, kc=kc, cj=cj, tk0=tk0, ntk=ntk, w_i=w_i:
                                 e.matmul(bank[:, 0:ntk], lhsT=w_i[:, kc, cj * 128:(cj + 1) * 128],
                                          rhs=xT[:, kc, tk0:tk0 + ntk], start=(kc == 0), stop=(kc == KC - 1)),
                                 [B_xT, Bw], [Bb])
                        epilogue_F(tag, c0 + cj * 128, tk0, ntk, bank, Bb)
        AR.reset(m)

    hT = AR.alloc([32, N], BF16)
    B_hT = Buf("hT")
    B_xin = Buf("xin")
    if want("s1"):
        norm_transpose(xin, B_xin, g_attn, hT, B_hT)

    def s2():
        m = AR.mark()
        stg = [AR.alloc([512], F32) for _ in range(2)]
        stgb = [AR.alloc([512], BF16) for _ in range(2)]
        Bstg = [Buf("stg0"), Buf("stg1")]
        Bstgb = [Buf("stgb0"), Buf("stgb1")]
        ctr = [0, 0]

        def epi_T(tag, c0, ncols, t, bank, Bb):
            i = ctr[0] % 2
            ctr[0] += 1
            eng = evac_eng()
            P.op(eng, copy_op(eng, stg[i][:, 0:ncols], bank[:, 0:ncols]), [Bb], [Bstg[i]])
            pc = c0 - VA0
            P.dma("sp", projT[t * 128:(t + 1) * 128, pc:pc + ncols], stg[i][:, 0:ncols], reads=[Bstg[i]],
                  writes=[B_projT])

        def epi_F(tag, c, tk0, ntk, bank, Bb):
            i = ctr[1] % 2
            ctr[1] += 1
            if tag == "u":
                eng = evac_eng()
                P.op(eng, copy_op(eng, stgb[i][:, 0:ntk], bank[:, 0:ntk]), [Bb], [Bstgb[i]])
                dst, Bd, r0 = uT, B_uT, c - U0
            else:
                P.op("act", lambda e, i=i, ntk=ntk, bank=bank: e.activation(out=stgb[i][:, 0:ntk], in_=bank[:, 0:ntk],
                                                                          func=AF.Sigmoid), [Bb], [Bstgb[i]])
                if tag == "ga":
                    dst, Bd, r0 = gaT, B_gaT, c - GA0
                else:
                    dst, Bd, r0 = gbT, B_gbT, c - GB0
            P.dma("sp", dst[r0:r0 + 128, tk0:tk0 + ntk], stgb[i][:, 0:ntk], reads=[Bstgb[i]], writes=[Bd])

        blocks = []
        for j in range(4):
            blocks.append((VA0 + 512 * j, 512, "T", "va"))
        for j in range(8):
            blocks.append((Q0 + 512 * j, 512, "T", "qkv"))
        blocks.append((KI0, 80, "T", "kiwi"))
        for j in range(4):
            blocks.append((U0 + 512 * j, 512, "F", "u"))
        for j in range(8):
            blocks.append((GA0 + 512 * j, 512, "F", "ga"))
        for j in range(8):
            blocks.append((GB0 + 512 * j, 512, "F", "gb"))
        gemm(hT, B_hT, 32, w_in, blocks, epi_T, epi_F)
        P.fence()
        AR.reset(m)

    if want("s2"):
        s2()
    if stop == "s2":
        return finish(nc, P, out_bufs)

    def bias4(bias_bc, bi, q4):
        b = bias_bc[:, bi, q4 * 256:(q4 + 1) * 256].rearrange("p (g i) -> p g i", g=2)
        return b.unsqueeze(2).to_broadcast([128, 2, 2, 128])

    def s3():
        AR.reset(base_mark)
        gn_bc = AR.alloc([DA], F32)
        Bgn = Buf("gn")
        P.dma("sp", gn_bc, g_gmlp.partition_broadcast(128), writes=[Bgn])
        wn = AR.alloc([8, 128], F32)
        wns = AR.alloc([8, 128], F32)
        wnb = AR.alloc([8, 128], BF16)
        wnsb = AR.alloc([8, 128], BF16)
        wmT = AR.alloc([8, 128], BF16)
        wmTs = AR.alloc([8, 128], BF16)
        bias_bc = AR.alloc([2, 1024], F32)
        stmp = AR.alloc([512], F32)
        Bstmp = Buf("stmp")
        Bwn, Bwns, Bwmt, Bbr = Buf("wn"), Buf("wns"), Buf("wmT"), Buf("br")
        P.dma("sp", wn, ws.rearrange("g i j -> i g j"), writes=[Bwn])
        P.op("dve", lambda e: e.memset(wn[0:64, :, 64:128], 0.0), [], [Bwn])
        P.op("dve", lambda e: e.tensor_copy(out=wnb, in_=wn), [Bwn], [Bwn])
        P.op("pool", lambda e: e.memset(wns, 0.0), [], [Bwns])
        for s_ in range(4):
            P.dma("sp", wns[32 * s_:32 * s_ + 32, :, 32 * s_:32 * s_ + 32],
                  ws[:, 0:32, 0:32].rearrange("g i j -> i g j"), writes=[Bwns])
        P.op("pool", lambda e: e.tensor_copy(out=wnsb, in_=wns), [Bwns], [Bwns])
        for (src_b, dst, Bs) in ((wnb, wmT, Bwn), (wnsb, wmTs, Bwns)):
            bank, Bb = next_bank()
            pb = bank[:].bitcast(BF16)
            for g in range(8):
                P.op("pe", lambda e, pb=pb, g=g, src_b=src_b: e.transpose(out=pb[:, g * 128:(g + 1) * 128],
                                                                        in_=src_b[:, g, :], identity=identb),
                     [Bs, B_ident], [Bb])
            P.op("dve", lambda e, pb=pb, dst=dst: e.tensor_copy(out=dst, in_=pb.rearrange("p (a b) -> p a b", a=8)),
                 [Bb], [Bwmt])
        P.dma("sp", bias_bc[:, 0, :], gbias.rearrange("g i -> (g i)").partition_broadcast(128), writes=[Bbr])
        for s_ in range(4):
            P.op("dve", lambda e, s_=s_: e.tensor_copy(
                out=bias_bc[:, 1, :].rearrange("p (g i) -> p g i", g=8)[:, :, 32 * s_:32 * s_ + 32],
                in_=bias_bc[:, 0, :].rearrange("p (g i) -> p g i", g=8)[:, :, 0:32]), [Bbr], [Bbr])
        if cut == 1:
            P.fence()
            return

        va = [AR.alloc([DA], F32) for _ in range(2)]
        Bva = [Buf("va0"), Buf("va1")]
        vnb = AR.alloc([DA], BF16)
        vnf = AR.alloc([DA], F32)
        junk = AR.alloc([DA], BF16)
        Bvnb, Bvnf, Bjunk = Buf("vnb"), Buf("vnf"), Buf("junk3")
        st = AR.alloc([3 * NT], F32)
        Bst = [Buf("st3_%d" % t) for t in range(NT)]
        uTs = [AR.alloc([16, 512], BF16) for _ in range(2)]
        BuTs = [Buf("uTs0"), Buf("uTs1")]
        aTg = [AR.alloc([16, 512], BF16) for _ in range(2)]
        BaTg = [Buf("aTg0"), Buf("aTg1")]
        uTv = uT.rearrange("(dc p) n -> p dc n", p=128)
        aTv = aTd.rearrange("(dc p) n -> p dc n", p=128)
        for gi, (tk0, ntk) in enumerate(TOKG):
            u_g, Bu = uTs[gi % 2], BuTs[gi % 2]
            a_g, Ba = aTg[gi % 2], BaTg[gi % 2]
            P.dma("sp", u_g[:, :, 0:ntk], uTv[:, :, tk0:tk0 + ntk], reads=[B_uT], writes=[Bu])
            for tl in range(ntk // 128):
                t = tk0 // 128 + tl
                v_t, Bv = va[t % 2], Bva[t % 2]
                P.dma("sp", v_t, projT[t * 128:(t + 1) * 128, PJ_VA:PJ_VA + DA], reads=[B_projT], writes=[Bv])
                ss, sd, rs = st[:, 3 * t:3 * t + 1], st[:, 3 * t + 1:3 * t + 2], st[:, 3 * t + 2:3 * t + 3]
                P.op("act", lambda e, v_t=v_t, ss=ss: e.activation(out=junk, in_=v_t, func=AF.Square, accum_out=ss),
                     [Bv], [Bjunk, Bst[t]])
                P.op("act", lambda e, ss=ss, sd=sd: e.activation(out=sd, in_=ss, func=AF.Sqrt, scale=1.0 / DA, bias=EPS),
                     [Bst[t]], [Bst[t]])
                P.op("dve", lambda e, sd=sd, rs=rs: e.reciprocal(out=rs, in_=sd), [Bst[t]], [Bst[t]])
                P.op("dve", lambda e, v_t=v_t, rs=rs: e.scalar_tensor_tensor(out=vnb, in0=v_t, scalar=rs, in1=gn_bc,
                                                                         op0=ALU.mult, op1=ALU.mult),
                     [Bv, Bst[t], Bgn], [Bvnb])
                if t == NT - 1:
                    P.op("dve", lambda e, v_t=v_t, rs=rs: e.scalar_tensor_tensor(out=vnf, in0=v_t, scalar=rs, in1=gn_bc,
                                                                             op0=ALU.mult, op1=ALU.mult),
                         [Bv, Bst[t], Bgn], [Bvnf])
                    ob = Buf("o_vn")
                    out_bufs.append(ob)
                    P.dma("sp", o_vn[:, :], vnf, reads=[Bvnf], writes=[ob])
                wm_t = wmTs if t == NT - 1 else wmT
                bi = 1 if t == NT - 1 else 0
                for q4 in range(4 if cut > 2 else 0):
                    bank, Bb = next_bank()
                    for j in range(4):
                        dc = q4 * 4 + j
                        g = dc // 2
                        P.op("pe", lambda e, bank=bank, j=j, dc=dc, g=g, wm_t=wm_t:
                             e.matmul(bank[:, j * 128:(j + 1) * 128], lhsT=vnb[:, dc * 128:(dc + 1) * 128],
                                      rhs=wm_t[:, g, :], start=True, stop=True),
                             [Bvnb, Bwmt], [Bb])
                    if cut == 3:
                        continue
                    P.op("dve", lambda e, bank=bank, q4=q4, bi=bi: e.tensor_tensor(
                        out=stmp.rearrange("p (g r i) -> p g r i", g=2, r=2),
                        in0=bank[:, 0:512].rearrange("p (g r i) -> p g r i", g=2, r=2),
                        in1=bias4(bias_bc, bi, q4), op=ALU.add),
                        [Bb, Bbr], [Bstmp])
                    if cut == 4:
                        continue
                    P.op("dve", lambda e, q4=q4, tl=tl, a_g=a_g, u_g=u_g:
                         e.tensor_tensor(out=a_g[:, q4 * 4:(q4 + 1) * 4, tl * 128:(tl + 1) * 128],
                                         in0=stmp.rearrange("p (a b) -> p a b", a=4),
                                         in1=u_g[:, q4 * 4:(q4 + 1) * 4, tl * 128:(tl + 1) * 128], op=ALU.mult),
                         [Bstmp, Bu], [Ba])
            P.dma("sp", aTv[:, :, tk0:tk0 + ntk], a_g[:, :, 0:ntk], reads=[Ba], writes=[B_aTd])
        P.fence()

    if want("s3"):
        s3()
    if stop == "s3":
        return finish(nc, P, out_bufs)

    def s4():
        AR.reset(base_mark)
        SCALE = float(HD) ** -0.5
        MNEG = -30000.0
        WSC = float(NIH) ** -0.5 * float(IDD) ** -0.5
        RAWW = PJW - PJ_Q
        KT0, Vb0, kiT20 = AR.alloc([4, NP], BF16), AR.alloc([16, 512], BF16), AR.alloc([NP], BF16)
        m_s = AR.mark()
        KT1, Vb1, kiT21 = AR.alloc([4, NP], BF16), AR.alloc([16, 512], BF16), AR.alloc([NP], BF16)
        ckb = AR.alloc([16, 512], BF16)
        ckib = AR.alloc([16, 128], BF16)
        m_e = AR.mark()
        AR.reset(m_s)
        raw_b, qkb_b, qib_b, kib_b, wsm_b = (AR.alloc([RAWW], F32), AR.alloc([2560], BF16), AR.alloc([1024], BF16),
                                             AR.alloc([128], BF16), AR.alloc([16 * 6], F32))
        assert AR.mark() <= m_e
        AR.reset(m_e)
        KT, Vb, kiT2 = [KT0, KT1], [Vb0, Vb1], [kiT20, kiT21]
        BKT = [Buf("KT0"), Buf("KT1")]
        BVb = [Buf("Vb0"), Buf("Vb1")]
        BkiT = [Buf("kiT0"), Buf("kiT1")]
        Bckb, Bckib = Buf("ckb"), Buf("ckib")
        csq = AR.alloc([NT, 32], F32)
        csi = AR.alloc([NT, 16], F32)
        bd = AR.alloc([128], F32)
        Bcs = Buf("cs")
        P.dma("sp", csq, c_csq.rearrange("(t p) c -> p t c", p=128), writes=[Bcs])
        P.dma("sp", csi, c_csi.rearrange("(t p) c -> p t c", p=128), writes=[Bcs])
        P.dma("sp", bd, c_bd[:, :], writes=[Bcs])
        ones_b = AR.alloc([128], BF16)
        diagT = AR.alloc([128], BF16)
        P.op("dve", lambda e: e.memset(ones_b, 1.0), [], [Bcs])
        P.op("dve", lambda e: e.memset(diagT, 0.0), [], [Bcs])
        P.op("dve", lambda e: e.memset(diagT[0:64, 64:128], MNEG), [], [Bcs])
        I4 = AR.alloc([4, 128], BF16)
        for h_ in range(4):
            P.op("dve", lambda e, h_=h_: e.tensor_copy(out=I4[:, h_, :], in_=identb), [B_ident], [Bcs])
        raw2 = [AR.alloc([RAWW], F32), raw_b]
        Braw2 = [Buf("raw0"), Buf("raw1")]
        tmp = [AR.alloc([20 * 16], F32) for _ in range(4)]
        Btmp = Buf("ropetmp")
        qkb2 = [AR.alloc([2560], BF16), qkb_b]
        qib2 = [AR.alloc([1024], BF16), qib_b]
        kib2 = [AR.alloc([128], BF16), kib_b]
        Bqkb2, Bqib2, Bkib2 = [Buf("qkb0"), Buf("qkb1")], [Buf("qib0"), Buf("qib1")], [Buf("kib0"), Buf("kib1")]
        qT2 = [AR.alloc([16, 128], BF16) for _ in range(2)]
        qiT = AR.alloc([8, 128], BF16)
        kTs = AR.alloc([4, 128], BF16)
        Vnew = AR.alloc([512], BF16)
        kiTs = AR.alloc([128], BF16)
        BqT2 = [Buf("qT0"), Buf("qT1")]
        BqiT, BkTs, BVnew, BkiTs = Buf("qiT"), Buf("kTs"), Buf("Vnew"), Buf("kiTs")
        wsm2 = [AR.alloc([16 * 6], F32), wsm_b]
        Bw2 = [Buf("wsm0"), Buf("wsm1")]
        wsm, Bw = wsm2[0], Bw2[0]
        S = AR.alloc([2080], F32)
        mb2 = [AR.alloc([2080], BF16) for _ in range(2)]
        Bmb2 = [Buf("mb0"), Buf("mb1")]
        mx8 = AR.alloc([8], F32)
        BS, Bmx = Buf("S"), Buf("mx8")
        rtmp = [AR.alloc([512], F32) for _ in range(2)]
        Brtmp = [Buf("rtmp%d" % i) for i in range(2)]
        rtb = [AR.alloc([512], BF16) for _ in range(16)]
        Brtb = [Buf("rtb%d" % i) for i in range(16)]
        dg = AR.alloc([16, 128], BF16)
        Bdg = Buf("dg")
        PT = [AR.alloc([512], BF16) for _ in range(5)]
        BPT = [Buf("PT%d" % i) for i in range(5)]
        osb = [AR.alloc([512], F32) for _ in range(2)]
        dsb = [AR.alloc([512], F32) for _ in range(2)]
        Bosb = [Buf("osb0"), Buf("osb1")]
        Bdsb = [Buf("dsb0"), Buf("dsb1")]
        nrm = [0]

        bTg = [AR.alloc([16, 128], BF16) for _ in range(2)]
        BbTg = [Buf("bTg0"), Buf("bTg1")]
        snew = AR.alloc([128], F32)
        snew2 = AR.alloc([128], F32)
        mnew = AR.alloc([128], BF16)
        Bsnew, Bmnew = Buf("snew"), Buf("mnew")
        bTv = bTd.rearrange("(dc p) n -> p dc n", p=128)
        stb = [(banks[i], Bbank[i]) for i in (0, 1, 2, 6, 7)]
        sc_bank, Bsc = banks[3], Bbank[3]
        ob_ = [(banks[4], Bbank[4]), (banks[4], Bbank[4])]
        db_ = [(banks[5], Bbank[5]), (banks[5], Bbank[5])]
        LOOK = 4
        ctr = {"st": 0, "pt": 0, "rt": 0, "mm": 0}

        def nb():
            ctr["st"] += 1
            return stb[ctr["st"] % 5]

        def prep_a(t):
            raw, Braw = raw2[t % 2], Braw2[t % 2]
            qkb, Bqkb, qib, Bqib, kib, Bkib = qkb2[t % 2], Bqkb2[t % 2], qib2[t % 2], Bqib2[t % 2], kib2[t % 2], Bkib2[t % 2]
            wsm, Bw = wsm2[t % 2], Bw2[t % 2]
            P.dma("sp", raw, projT[t * 128:(t + 1) * 128, PJ_Q:PJW], reads=[B_projT], writes=[Braw])
            for (o0, nh, hd, half, cs_t) in ((0, 20, 128, 16, csq), (3072, 17, 64, 8, csi)):
                v3 = raw[:, o0:o0 + nh * hd].rearrange("p (h d) -> p h d", h=nh)
                x1, x2_ = v3[:, :, 0:half], v3[:, :, half:2 * half]
                cos = cs_t[:, t, 0:half].unsqueeze(1).to_broadcast([128, nh, half])
                sin = cs_t[:, t, half:2 * half].unsqueeze(1).to_broadcast([128, nh, half])
                tt = [tm[:, 0:nh * half].rearrange("p (h d) -> p h d", h=nh) for tm in tmp]
                P.op("pool", lambda e, tt=tt, x1=x1, cos=cos: e.tensor_tensor(out=tt[0], in0=x1, in1=cos, op=ALU.mult),
                     [Braw, Bcs], [Btmp])
                P.op("pool", lambda e, tt=tt, x2_=x2_, sin=sin: e.tensor_tensor(out=tt[1], in0=x2_, in1=sin, op=ALU.mult),
                     [Braw, Bcs], [Btmp])
                P.op("pool", lambda e, tt=tt, x2_=x2_, cos=cos: e.tensor_tensor(out=tt[2], in0=x2_, in1=cos, op=ALU.mult),
                     [Braw, Bcs], [Btmp])
                P.op("pool", lambda e, tt=tt, x1=x1, sin=sin: e.tensor_tensor(out=tt[3], in0=x1, in1=sin, op=ALU.mult),
                     [Braw, Bcs], [Btmp])
                P.op("pool", lambda e, tt=tt, x1=x1: e.tensor_tensor(out=x1, in0=tt[0], in1=tt[1], op=ALU.subtract),
                     [Btmp], [Braw])
                P.op("pool", lambda e, tt=tt, x2_=x2_: e.tensor_tensor(out=x2_, in0=tt[2], in1=tt[3], op=ALU.add),
                     [Btmp], [Braw])
            for (dst, c0, w) in ((o_k, 2048, 512), (o_v, 2560, 512), (o_ki, 4096, 64)):
                ob = Buf("ok")
                out_bufs.append(ob)
                P.dma("sp", dst[t * 128:(t + 1) * 128, :], raw[:, c0:c0 + w], reads=[Braw], writes=[ob])
            P.op("act", lambda e: e.activation(out=qkb, in_=raw[:, 0:2560], func=AF.Copy), [Braw], [Bqkb])
            wi = raw[:, 4160:4176]
            P.op("act", lambda e: e.activation(out=wsm[:, 0:16], in_=wi, func=AF.Abs, scale=WSC), [Braw], [Bw])
            P.op("act", lambda e: e.activation(out=wsm[:, 16:32], in_=wi, func=AF.Sign), [Braw], [Bw])
            P.op("pool", lambda e: e.tensor_tensor(out=qib.rearrange("p (h d) -> p h d", h=16),
                                                  in0=raw[:, 3072:4096].rearrange("p (h d) -> p h d", h=16),
                                                  in1=wsm[:, 0:16].unsqueeze(2).to_broadcast([128, 16, 64]),
                                                  op=ALU.mult), [Braw, Bw], [Bqib])
            P.op("pool", lambda e: e.tensor_copy(out=kib.rearrange("p (a d) -> p a d", a=2),
                                                in_=raw[:, 4096:4160].unsqueeze(1).to_broadcast([128, 2, 64])),
                 [Braw], [Bkib])

        def prep_b(t, kv):
            raw, Braw = raw2[t % 2], Braw2[t % 2]
            qkb, Bqkb, qib, Bqib, kib, Bkib = qkb2[t % 2], Bqkb2[t % 2], qib2[t % 2], Bqib2[t % 2], kib2[t % 2], Bkib2[t % 2]
            sample = (t == NT - 1)
            for b0 in (0, 8, 16):
                nb_ = min(8, 20 - b0)
                bank, Bb = nb()
                pb = bank[:].bitcast(BF16)
                for j in range(nb_):
                    hh = b0 + j
                    P.op("pe", lambda e, pb=pb, j=j, hh=hh: e.transpose(out=pb[:, j * 128:(j + 1) * 128],
                                                                      in_=qkb[:, hh * 128:(hh + 1) * 128],
                                                                      identity=identb), [Bqkb, B_ident], [Bb])
                if b0 < 16:
                    P.op("act", lambda e, pb=pb, b0=b0, t=t: e.activation(out=qT2[t % 2][:, b0:b0 + 8, :],
                                                                         in_=pb.rearrange("p (a b) -> p a b", a=8),
                                                                         func=AF.Copy), [Bb], [BqT2[t % 2]])
                else:
                    src3 = pb[:, 0:512].rearrange("p (a b) -> p a b", a=4)
                    if sample:
                        P.op("act", lambda e, src3=src3: e.activation(out=kTs, in_=src3, func=AF.Copy), [Bb], [BkTs])
                    else:
                        P.op("act", lambda e, src3=src3, kv=kv, t=t: e.activation(
                            out=KT[kv][:, :, t * 128:(t + 1) * 128], in_=src3, func=AF.Copy), [Bb], [BKT[kv]])
            bank, Bb = nb()
            pb = bank[:].bitcast(BF16)
            for j in range(8):
                P.op("pe", lambda e, pb=pb, j=j: e.transpose(out=pb[:, j * 128:(j + 1) * 128],
                                                            in_=qib[:, j * 128:(j + 1) * 128], identity=identb),
                     [Bqib, B_ident], [Bb])
            P.op("dve", lambda e, pb=pb: e.tensor_copy(out=qiT, in_=pb.rearrange("p (a b) -> p a b", a=8)),
                 [Bb], [BqiT])
            bank, Bb = nb()
            pb = bank[:].bitcast(BF16)
            P.op("pe", lambda e, pb=pb: e.transpose(out=pb[:, 0:128], in_=kib, identity=identb), [Bkib, B_ident], [Bb])
            if sample:
                P.op("dve", lambda e, pb=pb: e.tensor_copy(out=kiTs, in_=pb[:, 0:128]), [Bb], [BkiTs])
                P.op("dve", lambda e: e.tensor_copy(out=Vnew, in_=raw[:, 2560:3072]), [Braw], [BVnew])
            else:
                P.op("dve", lambda e, pb=pb, kv=kv, t=t: e.tensor_copy(out=kiT2[kv][:, t * 128:(t + 1) * 128],
                                                                     in_=pb[:, 0:128]), [Bb], [BkiT[kv]])
                P.op("dve", lambda e, kv=kv, t=t: e.tensor_copy(out=Vb[kv][:, t, :], in_=raw[:, 2560:3072]),
                     [Braw], [BVb[kv]])


        def prep_tile(t, kv):
            prep_a(t)
            prep_b(t, kv)

        def score_block(kiT_ap, BkiTb, k0, w, sg_ap_fn, first, dst, Bdst, pe_acc=False):
            zb = [None] * 16

            def emit_z(h):
                bank, Bb = nb()
                zb[h] = (bank, Bb)
                r0 = (h % 2) * 64
                P.op("pe", lambda e, bank=bank, h=h, r0=r0, k0=k0, w=w:
                     e.matmul(bank[:, 0:w], lhsT=qiT[r0:r0 + 64, h // 2, :], rhs=kiT_ap[r0:r0 + 64, k0:k0 + w],
                              start=True, stop=True), [BqiT, BkiTb], [Bb])

            LZ = 4 if pe_acc else 3
            for h in range(LZ):
                emit_z(h)
            for h in range(16):
                if h + LZ < 16:
                    emit_z(h + LZ)
                bank, Bb = zb[h]
                if pe_acc:
                    P.op("act", lambda e, bank=bank, h=h, w=w: e.activation(out=rtb[h][:, 0:w], in_=bank[:, 0:w],
                                                                          func=AF.Relu), [Bb], [Brtb[h]])
                    P.op("pe", lambda e, w=w, h=h: e.matmul(sc_bank[:, 0:w], lhsT=dg[:, h, :], rhs=rtb[h][:, 0:w],
                                                           start=(h == 0), stop=(h == 15)), [Bdg, Brtb[h]], [Bsc])
                    continue
                ctr["rt"] += 1
                ri = ctr["rt"] % 2
                P.op("act", lambda e, bank=bank, ri=ri, w=w: e.activation(out=rtmp[ri][:, 0:w], in_=bank[:, 0:w],
                                                                        func=AF.Relu), [Bb], [Brtmp[ri]])
                sg = sg_ap_fn(h)
                if first and h == 0:
                    P.op("dve", lambda e, ri=ri, w=w, sg=sg, dst=dst: e.tensor_scalar(
                        out=dst[:, 0:w], in0=rtmp[ri][:, 0:w], scalar1=sg, scalar2=None, op0=ALU.mult),
                        [Brtmp[ri], Bw], [Bdst])
                else:
                    P.op("dve", lambda e, ri=ri, w=w, sg=sg, dst=dst: e.scalar_tensor_tensor(
                        out=dst[:, 0:w], in0=rtmp[ri][:, 0:w], scalar=sg, in1=dst[:, 0:w], op0=ALU.mult, op1=ALU.add),
                        [Brtmp[ri], Bw, Bdst], [Bdst])
            if pe_acc:
                P.op("act", lambda e, w=w, dst=dst: e.activation(out=dst[:, 0:w], in_=sc_bank[:, 0:w], func=AF.Copy),
                     [Bsc], [Bdst])

        def topk_mask(L, first_chunk_cut, mi):
            mb, Bm = mb2[mi], Bmb2[mi]
            if first_chunk_cut:
                P.op("dve", lambda e: e.memset(S[0:64, L - 64:L], -3.0e30), [], [BS])
            for r in range(TOPK // 8):
                P.op("dve", lambda e: e.max(out=mx8, in_=S[:, 0:L]), [BS], [Bmx])
                P.op("dve", lambda e: e.match_replace(out=S[:, 0:L], in_to_replace=mx8, in_values=S[:, 0:L],
                                                      imm_value=NEG), [BS, Bmx], [BS])
            P.op("dve", lambda e: e.tensor_scalar(out=mb[:, 0:L], in0=S[:, 0:L], scalar1=-5.0e29, scalar2=MNEG,
                                                  op0=ALU.is_gt, op1=ALU.mult), [BS], [Bm])
            if first_chunk_cut:
                P.op("dve", lambda e: e.memset(mb[0:64, L - 64:L], MNEG), [], [Bm])

        mulctr = [0]

        def attn_group(steps, o_flat, d_ap, Bo, Bd, out_ap, Bout):
            nst = len(steps)
            stbank = [None] * nst

            def emit_st(i):
                s_ = steps[i]
                bank, Bb = nb()
                stbank[i] = (bank, Bb)
                hasb = s_["bias"] is not None
                P.op("pe", lambda e, bank=bank, s_=s_, hasb=hasb: e.matmul(s_["st_out"](bank), lhsT=s_["lhsT_k"],
                                                                         rhs=s_["rhs_q"], start=True, stop=not hasb),
                     [s_["Bk"], s_["Bq"]], [Bb])
                if hasb:
                    P.op("pe", lambda e, bank=bank, s_=s_: e.matmul(s_["st_out"](bank), lhsT=s_["bias"],
                                                                  rhs=s_["bias_rhs"], start=False, stop=True),
                         [s_["Bbias"], Bcs], [Bb])

            for i in range(min(LOOK, nst)):
                emit_st(i)
            for i in range(nst):
                if i + LOOK < nst:
                    emit_st(i + LOOK)
                s_ = steps[i]
                bank, Bb = stbank[i]
                w = s_["st_w"]
                ctr["pt"] += 1
                pi = ctr["pt"] % 5
                pt = PT[pi][:, 0:w]
                P.op("act", lambda e, bank=bank, pt=pt, w=w: e.activation(out=pt, in_=bank[:, 0:w], func=AF.Exp,
                                                                        scale=SCALE), [Bb], [BPT[pi]])
                P.op("pe", lambda e, s_=s_, pt=pt, i=i: e.matmul(s_["pv_out"], lhsT=s_["lhsT_v"], rhs=s_["pv_rhs"](pt),
                                                              start=(i == 0), stop=(i == nst - 1)),
                     [s_["Bv"], BPT[pi]], [Bo])
                P.op("pe", lambda e, s_=s_, pt=pt, i=i: e.matmul(s_["pd_out"], lhsT=ones_b, rhs=s_["pv_rhs"](pt),
                                                              start=(i == 0), stop=(i == nst - 1)),
                     [Bcs, BPT[pi]], [Bd])
            wtot = steps[0]["st_w"]
            nrm[0] += 1
            k = nrm[0] % 2
            rv = steps[0]["rec_view"]
            P.op("act", lambda e, k=k: e.activation(out=osb[k][:, 0:wtot], in_=o_flat, func=AF.Copy), [Bo], [Bosb[k]])
            P.op("act", lambda e, k=k: e.activation(out=dsb[k][:, 0:wtot], in_=d_ap, func=AF.Ln), [Bd], [Bdsb[k]])
            P.op("act", lambda e, k=k: e.activation(out=dsb[k][:, 0:wtot], in_=dsb[k][:, 0:wtot], func=AF.Exp,
                                                    scale=-1.0), [Bdsb[k]], [Bdsb[k]])
            P.op("pool", lambda e, k=k: e.tensor_tensor(out=out_ap, in0=rv(osb[k][:, 0:wtot]), in1=rv(dsb[k][:, 0:wtot]),
                                                        op=ALU.mult), [Bosb[k], Bdsb[k]], [Bout])

        def score_topk(t):
            L = 128 * (t + 1)
            if t >= 2:
                wsm_t, Bw_t = wsm2[t % 2], Bw2[t % 2]
                for h in range(16):
                    P.op("pool", lambda e, h=h, wsm_t=wsm_t: e.tensor_scalar(out=dg[:, h, :], in0=identf,
                                                                            scalar1=wsm_t[:, 16 + h:17 + h],
                                                                            scalar2=None, op0=ALU.mult),
                         [Bw_t, B_ident], [Bdg])
                for k0 in range(0, L, 512):
                    w = min(512, L - k0)
                    score_block(kiT2[0], BkiT[0], k0, w, None, True, S[:, k0:k0 + w], BS, pe_acc=True)
                topk_mask(L, True, t % 2)

        def attention_prompt(t):
            use_topk = t >= 2
            b_g, Bbg = bTg[t % 2], BbTg[t % 2]
            for n in range(4):
                (o_bank, Bo), (d_bank, Bd) = ob_[n % 2], db_[n % 2]
                steps = []
                for kt in range(t + 1):
                    if use_topk:
                        mk, Bmk = mb2[t % 2][:, kt * 128:(kt + 1) * 128], Bmb2[t % 2]
                    else:
                        mk, Bmk = (diagT, Bcs) if kt == t else (None, None)
                    steps.append(dict(
                        bias=mk, Bbias=Bmk, bias_rhs=I4,
                        lhsT_k=KT[0][:, n, kt * 128:(kt + 1) * 128], Bk=BKT[0],
                        rhs_q=qT2[t % 2][:, 4 * n:4 * n + 4, :], Bq=BqT2[t % 2], st_w=512,
                        st_out=lambda bank: bank[:, 0:512].rearrange("p (h q) -> p h q", h=4),
                        lhsT_v=Vb[0][:, kt, n * 128:(n + 1) * 128], Bv=BVb[0],
                        pv_out=o_bank[:, 0:512], pd_out=d_bank[:, 0:512], pv_rhs=lambda pt: pt,
                        rec_view=lambda r: r.rearrange("p (h q) -> p h q", h=4)))
                attn_group(steps, o_bank[:, 0:512], d_bank[:, 0:512], Bo, Bd,
                           b_g[:, 4 * n:4 * n + 4, :], Bbg)
            P.dma("sp", bTv[:, :, t * 128:(t + 1) * 128], b_g, reads=[Bbg], writes=[B_bTd])

        prep_a(0)
        prep_a(1)
        prep_b(0, 0)
        for t in range(16):
            if t + 1 < 16:
                prep_b(t + 1, 0)
                score_topk(t + 1)
            if t + 2 < 16:
                prep_a(t + 2)
            attention_prompt(t)

        P.fence()
        t = NT - 1
        prep_tile(t, 0)
        for s_ in range(4):
            P.op("dve", lambda e, s_=s_: e.tensor_scalar(out=wsm[:, 32 + 16 * s_:48 + 16 * s_], in0=wsm[:, 16:32],
                                                         scalar1=bd[:, 32 * s_:32 * s_ + 1], scalar2=None,
                                                         op0=ALU.mult), [Bw, Bcs], [Bw])
        for s_ in range(4):
            kv = s_ % 2
            ckv = cki[s_].rearrange("(kt p) c -> p kt c", p=128)
            P.dma("pool", ckib[:, :, 0:64], ckv, writes=[Bckib])
            P.dma("pool", ckib[:, :, 64:128], ckv, writes=[Bckib])
            for b0 in (0, 8):
                bank, Bb = nb()
                pb = bank[:].bitcast(BF16)
                for j in range(8):
                    P.op("pe", lambda e, pb=pb, j=j, b0=b0: e.transpose(out=pb[:, j * 128:(j + 1) * 128],
                                                                      in_=ckib[:, b0 + j, :], identity=identb),
                         [Bckib, B_ident], [Bb])
                P.op("dve", lambda e, pb=pb, b0=b0, kv=kv: e.tensor_copy(out=kiT2[kv][:, b0 * 128:(b0 + 8) * 128],
                                                                       in_=pb), [Bb], [BkiT[kv]])
            for k0 in range(0, PAST, 512):
                score_block(kiT2[kv], BkiT[kv], k0, 512,
                            lambda h, s_=s_: wsm[:, 32 + 16 * s_ + h:33 + 16 * s_ + h], s_ == 0, S[:, k0:k0 + 512], BS)
        score_block(kiTs, BkiTs, 0, 128, lambda h: wsm[:, 16 + h:17 + h], True, snew, Bsnew)
        P.op("dve", lambda e: e.tensor_tensor(out=snew2, in0=snew, in1=bd, op=ALU.mult), [Bsnew, Bcs], [Bsnew])
        P.op("dve", lambda e: e.tensor_reduce(out=S[:, PAST:PAST + 32], in_=snew2.rearrange("p (s j) -> p j s", s=4),
                                              axis=AX.X, op=ALU.add), [Bsnew], [BS])
        topk_mask(PAST + 32, False, 0)
        mbs, Bmbs = mb2[0], Bmb2[0]
        m3 = mnew.rearrange("p (s j) -> p s j", s=4)
        P.op("dve", lambda e: e.tensor_scalar(out=snew.rearrange("p (s j) -> p s j", s=4),
                                              in0=mbs[:, PAST:PAST + 32].unsqueeze(1).to_broadcast([128, 4, 32]),
                                              scalar1=-MNEG, scalar2=None, op0=ALU.add), [Bmbs], [Bsnew])
        P.op("dve", lambda e: e.tensor_tensor(out=snew2, in0=snew, in1=bd, op=ALU.mult), [Bsnew, Bcs], [Bsnew])
        P.op("dve", lambda e: e.tensor_scalar(out=mnew, in0=snew2, scalar1=MNEG, scalar2=None, op0=ALU.add),
             [Bsnew], [Bmnew])
        b_g, Bbg = bTg[0], BbTg[0]
        for s_ in range(4):
            kv = s_ % 2
            P.dma("pool", ckb, ck[s_].rearrange("(kt p) c -> p kt c", p=128), writes=[Bckb])
            P.dma("pool", Vb[kv], cv[s_].rearrange("(kt p) c -> p kt c", p=128), writes=[BVb[kv]])
            for kt2 in range(8):
                bank, Bb = nb()
                pb = bank[:].bitcast(BF16)
                for j in range(8):
                    ktl, n = j // 4, j % 4
                    kt = kt2 * 2 + ktl
                    P.op("pe", lambda e, pb=pb, j=j, kt=kt, n=n: e.transpose(out=pb[:, j * 128:(j + 1) * 128],
                                                                           in_=ckb[:, kt, n * 128:(n + 1) * 128],
                                                                           identity=identb), [Bckb, B_ident], [Bb])
                eng = evac_eng()
                o_ap = KT[kv][:, :, kt2 * 256:(kt2 + 1) * 256].rearrange("p n (k c) -> p n k c", k=2)
                i_ap = pb.rearrange("p (k n c) -> p n k c", k=2, n=4)
                P.op(eng, copy_op(eng, o_ap, i_ap), [Bb], [BKT[kv]])
            qs = slice(32 * s_, 32 * s_ + 32)
            for n in range(4):
                (o_bank, Bo), (d_bank, Bd) = ob_[n % 2], db_[n % 2]
                o3 = o_bank[:, 0:128].rearrange("p (h q) -> p h q", h=4)
                steps = []
                for kt in range(17):
                    if kt < 16:
                        lk, Bk_, lv, Bv_ = KT[kv][:, n, kt * 128:(kt + 1) * 128], BKT[kv], \
                            Vb[kv][:, kt, n * 128:(n + 1) * 128], BVb[kv]
                    else:
                        lk, Bk_, lv, Bv_ = kTs[:, n, :], BkTs, Vnew[:, n * 128:(n + 1) * 128], BVnew
                    steps.append(dict(
                        lhsT_k=lk, Bk=Bk_, rhs_q=qT2[0][:, 4 * n:4 * n + 4, qs], Bq=BqT2[0], st_w=128,
                        st_out=lambda bank: bank[:, 0:128].rearrange("p (h q) -> p h q", h=4),
                        bias=(mbs[:, kt * 128:(kt + 1) * 128] if kt < 16 else mnew),
                        Bbias=(Bmbs if kt < 16 else Bmnew), bias_rhs=I4[:, :, qs], lhsT_v=lv, Bv=Bv_,
                        pv_out=o_bank[:, 0:128], pd_out=d_bank[:, 0:128], pv_rhs=lambda pt: pt,
                        rec_view=lambda r: r.rearrange("p (h q) -> p h q", h=4)))
                attn_group(steps, o_bank[:, 0:128], d_bank[:, 0:128], Bo, Bd, b_g[:, 4 * n:4 * n + 4, qs], Bbg)
        P.dma("sp", bTv[:, :, NP:N], b_g, reads=[Bbg], writes=[B_bTd])
        P.fence()

    if want("s4"):
        s4()
    if stop == "s4":
        return finish(nc, P, out_bufs)

    def s5():
        AR.reset(base_mark)
        aT = AR.alloc([16, N], BF16)
        bT = AR.alloc([16, N], BF16)
        BaT, BbT = Buf("aT"), Buf("bT")
        aTv = aTd.rearrange("(dc p) n -> p dc n", p=128)
        bTv = bTd.rearrange("(dc p) n -> p dc n", p=128)
        for q in range(4):
            P.dma("sp", aT[:, 4 * q:4 * q + 4, :], aTv[:, 4 * q:4 * q + 4, :], reads=[B_aTd], writes=[BaT])
            P.dma("sp", bT[:, 4 * q:4 * q + 4, :], bTv[:, 4 * q:4 * q + 4, :], reads=[B_bTd], writes=[BbT])
        WC = 256
        wa = [AR.alloc([16, WC], BF16) for _ in range(2)]
        wb_ = [AR.alloc([16, WC], BF16) for _ in range(2)]
        Bwa, Bwb = [Buf("wa0"), Buf("wa1")], [Buf("wbb0"), Buf("wbb1")]
        ga = [AR.alloc([512], BF16) for _ in range(2)]
        gb_ = [AR.alloc([512], BF16) for _ in range(2)]
        Bga, Bgb = [Buf("ga0"), Buf("ga1")], [Buf("gb0"), Buf("gb1")]
        tA = [AR.alloc([512], F32) for _ in range(2)]
        tB = [AR.alloc([512], F32) for _ in range(2)]
        BtA, BtB = [Buf("tA0"), Buf("tA1")], [Buf("tB0"), Buf("tB1")]
        ys = [AR.alloc([512], BF16) for _ in range(2)]
        Bys = [Buf("ys0"), Buf("ys1")]
        wav = w_a.rearrange("(kc p) c -> p kc c", p=128)
        wbv = w_b.rearrange("(kc p) c -> p kc c", p=128)
        nblk = D // WC

        def load(i):
            P.dma("pool", wa[i % 2], wav[:, :, i * WC:(i + 1) * WC], writes=[Bwa[i % 2]])
            P.dma("pool", wb_[i % 2], wbv[:, :, i * WC:(i + 1) * WC], writes=[Bwb[i % 2]])

        load(0)
        it = 0
        for i in range(nblk):
            if i + 1 < nblk:
                load(i + 1)
            for cj in range(WC // 128):
                c = i * WC + cj * 128
                for (tk0, ntk) in TOKG:
                    k = it % 2
                    it += 1
                    P.dma("sp", ga[k][:, 0:ntk], gaT[c:c + 128, tk0:tk0 + ntk], reads=[B_gaT], writes=[Bga[k]])
                    P.dma("sp", gb_[k][:, 0:ntk], gbT[c:c + 128, tk0:tk0 + ntk], reads=[B_gbT], writes=[Bgb[k]])
                    bankA, BbA = next_bank()
                    for kc in range(16):
                        P.op("pe", lambda e, bankA=bankA, kc=kc, cj=cj, tk0=tk0, ntk=ntk, i=i:
                             e.matmul(bankA[:, 0:ntk], lhsT=wa[i % 2][:, kc, cj * 128:(cj + 1) * 128],
                                      rhs=aT[:, kc, tk0:tk0 + ntk], start=(kc == 0), stop=(kc == 15)),
                             [Bwa[i % 2], BaT], [BbA])
                    bankB, BbB = next_bank()
                    for kc in range(16):
                        P.op("pe", lambda e, bankB=bankB, kc=kc, cj=cj, tk0=tk0, ntk=ntk, i=i:
                             e.matmul(bankB[:, 0:ntk], lhsT=wb_[i % 2][:, kc, cj * 128:(cj + 1) * 128],
                                      rhs=bT[:, kc, tk0:tk0 + ntk], start=(kc == 0), stop=(kc == 15)),
                             [Bwb[i % 2], BbT], [BbB])
                    P.op("dve", lambda e, k=k, ntk=ntk, bankA=bankA: e.tensor_tensor(
                        out=tA[k][:, 0:ntk], in0=bankA[:, 0:ntk], in1=ga[k][:, 0:ntk], op=ALU.mult),
                        [BbA, Bga[k]], [BtA[k]])
                    P.op("dve", lambda e, k=k, ntk=ntk, bankB=bankB: e.tensor_tensor(
                        out=tB[k][:, 0:ntk], in0=bankB[:, 0:ntk], in1=gb_[k][:, 0:ntk], op=ALU.mult),
                        [BbB, Bgb[k]], [BtB[k]])
                    P.op("pool", lambda e, k=k, ntk=ntk: e.tensor_tensor(
                        out=ys[k][:, 0:ntk], in0=tA[k][:, 0:ntk], in1=tB[k][:, 0:ntk], op=ALU.add),
                        [BtA[k], BtB[k]], [Bys[k]])
                    P.dma("sp", yT[c:c + 128, tk0:tk0 + ntk], ys[k][:, 0:ntk], reads=[Bys[k]], writes=[B_yT])
        P.fence()

    if want("s5"):
        s5()
    if stop == "s5":
        return finish(nc, P, out_bufs)

    def s6():
        AR.reset(base_mark)
        yTs = AR.alloc([32, N], BF16)
        ByTs = Buf("yTs")
        yTv = yT.rearrange("(kc p) n -> p kc n", p=128)
        for q in range(8):
            P.dma("sp", yTs[:, 4 * q:4 * q + 4, :], yTv[:, 4 * q:4 * q + 4, :], reads=[B_yT], writes=[ByTs])
        xs = [AR.alloc([512], F32) for _ in range(2)]
        Bxs = [Buf("xs0"), Buf("xs1")]
        ctr = [0]

        def epi_T(tag, c0, ncols, t, bank, Bb):
            i = ctr[0] % 2
            ctr[0] += 1
            P.dma("sp", xs[i], xin[t * 128:(t + 1) * 128, c0:c0 + ncols], reads=[B_xin], writes=[Bxs[i]])
            P.op("dve", lambda e, i=i, bank=bank: e.tensor_tensor(out=xs[i], in0=bank[:, 0:512], in1=xs[i], op=ALU.add),
                 [Bb, Bxs[i]], [Bxs[i]])
            P.dma("sp", x2[t * 128:(t + 1) * 128, c0:c0 + ncols], xs[i], reads=[Bxs[i]], writes=[B_x2])

        gemm(yTs, ByTs, 32, w_o, [(512 * j, 512, "T", "o") for j in range(8)], epilogue_T=epi_T)
        P.fence()

    if want("s6"):
        s6()
    if stop == "s6":
        return finish(nc, P, out_bufs)

    AR.reset(base_mark)
    hT = AR.alloc([32, N], BF16)
    B_hT = Buf("hT2")
    if want("s7"):
        norm_transpose(x2, B_x2, g_ffn, hT, B_hT)

    def s8():
        m = AR.mark()
        NCH = 2 * DFF // 128
        par = AR.alloc([NCH, 12], F32)
        Bpar = Buf("par")
        Zl = AR.alloc([NCH, 10], F32)
        BZl = Buf("Zl")
        rowsrc = AR.alloc([2048], F32)
        Brow = Buf("rowsrc")
        for p0 in range(0, 2 * DFF, 2048):
            wdt = min(2048, 2 * DFF - p0)
            P.dma("sp", rowsrc[0:8, 0:wdt], cst[:, p0:p0 + wdt], writes=[Brow])
            P.dma("sp", rowsrc[8:11, 0:wdt], conv_w[:, p0:p0 + wdt], writes=[Brow])
            P.dma("sp", rowsrc[11:12, 0:wdt], conv_b[p0:p0 + wdt].unsqueeze(0), writes=[Brow])
            bank, Bb = next_bank()
            nch = wdt // 128
            for j in range(nch):
                P.op("pe", lambda e, bank=bank, j=j: e.matmul(bank[:, j * 12:(j + 1) * 12],
                                                            lhsT=rowsrc[0:12, j * 128:(j + 1) * 128],
                                                            rhs=identf[0:12, 0:12], start=True, stop=True),
                     [Brow, B_ident], [Bb])
            c0 = p0 // 128
            P.op("dve", lambda e, bank=bank, c0=c0, nch=nch: e.tensor_copy(
                out=par[:, c0:c0 + nch, :], in_=bank[:, 0:nch * 12].rearrange("p (a b) -> p a b", a=nch)),
                [Bb], [Bpar])
        WC = 256
        wb = [AR.alloc([32, WC], BF16) for _ in range(2)]
        Bwb = [Buf("wu0"), Buf("wu1")]
        zb = {k: [AR.alloc([516], F32) for _ in range(2)] for k in ("g", "u")}
        Bzb = {k: [Buf("z%s0" % k), Buf("z%s1" % k)] for k in ("g", "u")}
        cg = AR.alloc([512], F32)
        cu = AR.alloc([512], F32)
        Bcg, Bcu = Buf("cg"), Buf("cu")
        ast = [AR.alloc([512], BF16) for _ in range(2)]
        Bast = [Buf("ast0"), Buf("ast1")]
        wv = w_up.rearrange("(kc p) c -> p kc c", p=128)
        nblk = DFF // 128

        def load(i):
            P.dma("pool", wb[i % 2][:, :, 0:128], wv[:, :, i * 128:(i + 1) * 128], writes=[Bwb[i % 2]])
            P.dma("pool", wb[i % 2][:, :, 128:256], wv[:, :, DFF + i * 128:DFF + (i + 1) * 128], writes=[Bwb[i % 2]])

        load(0)
        it = 0
        for i in range(nblk):
            if i + 1 < nblk:
                load(i + 1)
            w_i, Bw = wb[i % 2], Bwb[i % 2]
            for gi, (tk0, ntk) in enumerate(TOKG):
                bk = {}
                for (k, off) in (("g", 0), ("u", 128)):
                    bank, Bb = next_bank()
                    bk[k] = (bank, Bb)
                    for kc in range(32):
                        P.op("pe", lambda e, bank=bank, kc=kc, off=off, tk0=tk0, ntk=ntk, w_i=w_i:
                             e.matmul(bank[:, 0:ntk], lhsT=w_i[:, kc, off:off + 128], rhs=hT[:, kc, tk0:tk0 + ntk],
                                      start=(kc == 0), stop=(kc == 31)), [Bw, B_hT], [Bb])
                zi = gi % 2
                outc = {}
                for (k, cacc, Bc, ch) in (("g", cg, Bcg, i), ("u", cu, Bcu, nblk + i)):
                    bank, Bb = bk[k]
                    z, Bz = zb[k][zi], Bzb[k][zi]
                    zp, Bzp = zb[k][1 - zi], Bzb[k][1 - zi]
                    w0, w1, w2, bia = par[:, ch, 8:9], par[:, ch, 9:10], par[:, ch, 10:11], par[:, ch, 11:12]
                    sample = (gi == 4)
                    if not sample:
                        P.op("act", lambda e, z=z, bank=bank, ntk=ntk: e.activation(out=z[:, 2:2 + ntk], in_=bank[:, 0:ntk],
                                                                                  func=AF.Copy), [Bb], [Bz])
                        if gi == 0:
                            P.op("pool", lambda e, z=z: e.memset(z[:, 0:2], 0.0), [], [Bz])
                        else:
                            P.op("pool", lambda e, z=z, zp=zp: e.tensor_copy(out=z[:, 0:2], in_=zp[:, 512:514]),
                                 [Bzp], [Bz])
                        P.op("act", lambda e, cacc=cacc, bank=bank, ntk=ntk, w2=w2, bia=bia: e.activation(
                            out=cacc[:, 0:ntk], in_=bank[:, 0:ntk], func=AF.Identity, scale=w2, bias=bia),
                            [Bb, Bpar], [Bc])
                        P.op("dve", lambda e, cacc=cacc, z=z, ntk=ntk, w1=w1: e.scalar_tensor_tensor(
                            out=cacc[:, 0:ntk], in0=z[:, 1:1 + ntk], scalar=w1, in1=cacc[:, 0:ntk],
                            op0=ALU.mult, op1=ALU.add), [Bz, Bc, Bpar], [Bc])
                        P.op("dve", lambda e, cacc=cacc, z=z, ntk=ntk, w0=w0: e.scalar_tensor_tensor(
                            out=cacc[:, 0:ntk], in0=z[:, 0:ntk], scalar=w0, in1=cacc[:, 0:ntk],
                            op0=ALU.mult, op1=ALU.add), [Bz, Bc, Bpar], [Bc])
                        if gi == 3:
                            P.op("pool", lambda e, z=z, ch=ch: e.tensor_copy(out=Zl[:, ch, 0:2], in_=z[:, 512:514]),
                                 [Bz], [BZl])
                    else:
                        z3 = z[:, 0:136].rearrange("p (s c) -> p s c", s=4)
                        P.op("act", lambda e, z3=z3, bank=bank: e.activation(
                            out=z3[:, :, 2:34], in_=bank[:, 0:128].rearrange("p (s c) -> p s c", s=4), func=AF.Copy),
                            [Bb], [Bz])
                        P.op("pool", lambda e, z3=z3, ch=ch: e.tensor_copy(
                            out=z3[:, :, 0:2], in_=par[:, ch, 0:8].rearrange("p (s r) -> p s r", s=4)), [Bpar], [Bz])
                        c3 = cacc[:, 0:128].rearrange("p (s c) -> p s c", s=4)
                        P.op("act", lambda e, cacc=cacc, bank=bank, w2=w2, bia=bia: e.activation(
                            out=cacc[:, 0:128], in_=bank[:, 0:128], func=AF.Identity, scale=w2, bias=bia),
                            [Bb, Bpar], [Bc])
                        P.op("dve", lambda e, c3=c3, z3=z3, w1=w1: e.scalar_tensor_tensor(
                            out=c3, in0=z3[:, :, 1:33], scalar=w1, in1=c3, op0=ALU.mult, op1=ALU.add),
                            [Bz, Bc, Bpar], [Bc])
                        P.op("dve", lambda e, c3=c3, z3=z3, w0=w0: e.scalar_tensor_tensor(
                            out=c3, in0=z3[:, :, 0:32], scalar=w0, in1=c3, op0=ALU.mult, op1=ALU.add),
                            [Bz, Bc, Bpar], [Bc])
                        P.op("pool", lambda e, z3=z3, ch=ch: e.tensor_copy(
                            out=Zl[:, ch, 2:10].rearrange("p (s r) -> p s r", s=4), in_=z3[:, :, 32:34]), [Bz], [BZl])
                P.op("act", lambda e, ntk=ntk: e.activation(out=cg[:, 0:ntk], in_=cg[:, 0:ntk], func=AF.Silu),
                     [Bcg], [Bcg])
                k = it % 2
                it += 1
                P.op("pool", lambda e, k=k, ntk=ntk: e.tensor_tensor(out=ast[k][:, 0:ntk], in0=cg[:, 0:ntk],
                                                                     in1=cu[:, 0:ntk], op=ALU.mult),
                     [Bcg, Bcu], [Bast[k]])
                P.dma("sp", actT[i * 128:(i + 1) * 128, tk0:tk0 + ntk], ast[k][:, 0:ntk], reads=[Bast[k]],
                      writes=[B_actT])
        zo = [rowsrc[:, 0:512], rowsrc[:, 512:1024]]
        Bzo = [Brow, Brow]
        for q in range(NCH // 4):
            bank, Bb = next_bank()
            for j in range(4):
                ch = q * 4 + j
                P.op("pe", lambda e, bank=bank, j=j, ch=ch: e.matmul(bank[0:10, j * 128:(j + 1) * 128], lhsT=Zl[:, ch, :],
                                                                   rhs=identf, start=True, stop=True),
                     [BZl, B_ident], [Bb])
            P.op("dve", lambda e, bank=bank, q=q: e.tensor_copy(out=zo[q % 2][0:10, :], in_=bank[0:10, :]),
                 [Bb], [Bzo[q % 2]])
            ob = Buf("ocst")
            out_bufs.append(ob)
            P.dma("sp", o_cst[:, q * 512:(q + 1) * 512], zo[q % 2][0:10, :], reads=[Bzo[q % 2]], writes=[ob])
        P.fence()
        AR.reset(m)

    if want("s8"):
        s8()
    if stop == "s8":
        return finish(nc, P, out_bufs)

    def s9():
        AR.reset(base_mark)
        KC = DFF // 128
        TB = [(0, 768), (768, 768), (1536, 640)]
        acT = AR.alloc([KC, 768], BF16)
        BacT = Buf("acT")
        wb = [AR.alloc([KC, 128], BF16) for _ in range(2)]
        Bwb = [Buf("wd0"), Buf("wd1")]
        fT = [AR.alloc([384], BF16) for _ in range(2)]
        BfT = [Buf("fT0"), Buf("fT1")]
        fst = [AR.alloc([6, 512], F32) for _ in range(2)]
        Bfst = [Buf("fst0"), Buf("fst1")]
        acv = actT.rearrange("(kc p) n -> p kc n", p=128)
        wv = w_down.rearrange("(kc p) c -> p kc c", p=128)
        li = [0]

        def load():
            i = li[0]
            c = i % 32
            if i < 32:
                P.dma("pool", wb[i % 2], wv[:, :, c * 128:(c + 1) * 128], writes=[Bwb[i % 2]])
                P.dma("sp", wdc[c], wb[i % 2], reads=[Bwb[i % 2]], writes=[Bwdc[c]])
            else:
                P.dma("sp", wb[i % 2], wdc[c], reads=[Bwdc[c]], writes=[Bwb[i % 2]])
            li[0] += 1

        ci = 0
        fi = 0
        load()
        for bi_, (tb0, ntb) in enumerate(TB):
            for q0 in range(0, KC, 8):
                q1 = min(KC, q0 + 8)
                P.dma("sp", acT[:, q0:q1, 0:ntb], acv[:, q0:q1, tb0:tb0 + ntb], reads=[B_actT], writes=[BacT])
            ntl = ntb // 128
            for c in range(32):
                if li[0] < 32 * len(TB):
                    load()
                w_i, Bw = wb[ci % 2], Bwb[ci % 2]
                ci += 1
                f_s, Bf = fst[(c // 4 + 8 * bi_) % 2], Bfst[(c // 4 + 8 * bi_) % 2]
                for s0 in range(0, ntb, 384):
                    ns = min(384, ntb - s0)
                    bank, Bb = next_bank()
                    for kc in range(KC):
                        P.op("pe", lambda e, bank=bank, kc=kc, s0=s0, ns=ns, w_i=w_i:
                             e.matmul(bank[:, 0:ns], lhsT=w_i[:, kc, :], rhs=acT[:, kc, s0:s0 + ns],
                                      start=(kc == 0), stop=(kc == KC - 1)), [Bw, BacT], [Bb])
                    k = fi % 2
                    fi += 1
                    P.op("act", lambda e, k=k, bank=bank, ns=ns: e.activation(out=fT[k][:, 0:ns], in_=bank[:, 0:ns],
                                                                            func=AF.Copy), [Bb], [BfT[k]])
                    bank2, Bb2 = next_bank()
                    pb = bank2[:].bitcast(BF16)
                    for j in range(ns // 128):
                        P.op("pe", lambda e, pb=pb, j=j, k=k: e.transpose(out=pb[:, j * 128:(j + 1) * 128],
                                                                        in_=fT[k][:, j * 128:(j + 1) * 128],
                                                                        identity=identb), [BfT[k], B_ident], [Bb2])
                    tl0 = s0 // 128
                    nj = ns // 128
                    P.op("dve", lambda e, pb=pb, f_s=f_s, tl0=tl0, nj=nj, c=c: e.tensor_copy(
                        out=f_s[:, tl0:tl0 + nj, (c % 4) * 128:(c % 4) * 128 + 128],
                        in_=pb[:, 0:nj * 128].rearrange("p (a b) -> p a b", a=nj)), [Bb2], [Bf])
                if c % 4 == 3:
                    cb = c // 4
                    P.dma("sp", fsc[tb0:tb0 + ntb, cb * 512:(cb + 1) * 512].rearrange("(j p) c -> p j c", p=128),
                          f_s[:, 0:ntl, :], reads=[Bf], writes=[B_fsc])
        P.fence()

    if want("s9"):
        s9()
    if stop == "s9":
        return finish(nc, P, out_bufs)

    def s10():
        AR.reset(base_mark)
        g_bc = AR.alloc([D], F32)
        Bg = Buf("gfin")
        P.dma("sp", g_bc, g_final.partition_broadcast(128), writes=[Bg])
        xa = [AR.alloc([D], F32) for _ in range(2)]
        fa = [AR.alloc([D], F32) for _ in range(2)]
        Bxa, Bfa = [Buf("xa0"), Buf("xa1")], [Buf("fa0"), Buf("fa1")]
        junk = AR.alloc([D], BF16)
        Bjunk = Buf("junk10")
        st = AR.alloc([3 * NT], F32)
        Bst = [Buf("st10_%d" % t) for t in range(NT)]
        for t in range(NT):
            x_t, Bx, f_t, Bf = xa[t % 2], Bxa[t % 2], fa[t % 2], Bfa[t % 2]
            P.dma("sp", x_t, x2[t * 128:(t + 1) * 128, :], reads=[B_x2], writes=[Bx])
            P.dma("sp", f_t, fsc[t * 128:(t + 1) * 128, :], reads=[B_fsc], writes=[Bf])
            P.op("pool", lambda e, x_t=x_t, f_t=f_t: e.tensor_tensor(out=x_t, in0=x_t, in1=f_t, op=ALU.add),
                 [Bx, Bf], [Bx])
            ss, sd, rs = st[:, 3 * t:3 * t + 1], st[:, 3 * t + 1:3 * t + 2], st[:, 3 * t + 2:3 * t + 3]
            P.op("act", lambda e, x_t=x_t, ss=ss: e.activation(out=junk, in_=x_t, func=AF.Square, accum_out=ss),
                 [Bx], [Bjunk, Bst[t]])
            P.op("act", lambda e, ss=ss, sd=sd: e.activation(out=sd, in_=ss, func=AF.Sqrt, scale=1.0 / D, bias=EPS),
                 [Bst[t]], [Bst[t]])
            P.op("dve", lambda e, sd=sd, rs=rs: e.reciprocal(out=rs, in_=sd), [Bst[t]], [Bst[t]])
            P.op("dve", lambda e, x_t=x_t, f_t=f_t, rs=rs: e.scalar_tensor_tensor(out=f_t, in0=x_t, scalar=rs, in1=g_bc,
                                                                              op0=ALU.mult, op1=ALU.mult),
                 [Bx, Bst[t], Bg, Bf], [Bf])
            ob = Buf("oy")
            out_bufs.append(ob)
            P.dma("sp", o_y[t * 128:(t + 1) * 128, :], f_t, reads=[Bf], writes=[ob])

    if want("s10"):
        s10()
    return finish(nc, P, out_bufs)


def finish(nc, P, out_bufs):
    P.fence()
    P.emit()
    return nc


def _rope_table(pos, half):
    inv = (np.float32(THETA) ** (-(np.arange(half, dtype=np.float32)) / np.float32(half))).astype(np.float32)
    ang = pos.astype(np.float32)[:, None] * inv[None, :]
    return np.concatenate([np.cos(ang), np.sin(ang)], axis=1).astype(np.float32)


def _consts():
    pos = np.concatenate([np.arange(NP), np.tile(PAST + np.arange(32), 4)]).astype(np.int32)
    bd = np.kron(np.eye(4, dtype=np.float32), np.ones((32, 32), np.float32))
    return {
        "c_ident": np.eye(128, dtype=np.float32),
        "c_csq": _rope_table(pos, 16),
        "c_csi": _rope_table(pos, 8),
        "c_bd": bd,
    }


def make_in_maps(inp, cores=range(8)):
    f = lambda a: np.ascontiguousarray(np.asarray(a, dtype=np.float32))
    shared = {
        "g_attn": f(inp["norm_attn_g"][0]), "w_in": f(inp["w_in"][0]), "g_gmlp": f(inp["gmlp_norm_g"][0]),
        "ws": f(inp["gmlp_ws"][0]), "gbias": f(inp["gmlp_b"][0]), "w_a": f(inp["w_branch_a"][0]),
        "w_b": f(inp["w_branch_b"][0]), "w_o": f(inp["w_out"][0]), "g_ffn": f(inp["norm_ffn_g"][0]),
        "w_up": f(inp["w_up"][0]), "conv_w": f(inp["conv_w"][0]), "conv_b": f(inp["conv_b"][0]),
        "w_down": f(inp["w_down"][0]), "g_final": f(inp["norm_final_g"]),
    }
    shared.update(_consts())
    maps = []
    for c in cores:
        sl = slice(4 * c, 4 * c + 4)
        xs = np.asarray(inp["x_sample"][sl], dtype=np.float32).reshape(NS, D)
        m = dict(shared)
        m["xin"] = np.ascontiguousarray(np.concatenate([np.asarray(inp["x_prompt"][c], dtype=np.float32), xs], axis=0))
        m["ck"] = f(np.asarray(inp["cache_k"][0, sl]).reshape(4, PAST, 512))
        m["cv"] = f(np.asarray(inp["cache_v"][0, sl]).reshape(4, PAST, 512))
        m["cki"] = f(inp["cache_kidx"][0, sl])
        m["cst"] = f(np.asarray(inp["state_ffn_conv"][0, sl]).reshape(8, 2 * DFF))
        maps.append(m)
    return maps


_NC_CACHE = {}


def kernel(**inputs):
    if "nc" not in _NC_CACHE:
        _NC_CACHE["nc"] = build_program()
    nc = _NC_CACHE["nc"]
    maps = make_in_maps(inputs)
    res = run_bass_kernel_spmd(nc, maps, core_ids=list(range(8)))
    R = res.results
    cat = lambda name: [np.asarray(r[name], dtype=np.float32) for r in R]
    y = cat("o_y"); k = cat("o_k"); v = cat("o_v"); ki = cat("o_ki"); cs = cat("o_cst"); vn = cat("o_vn")
    y_prompt = np.stack([a[:NP] for a in y])
    y_sample = np.concatenate([a[NP:].reshape(4, 32, D) for a in y])
    kp = np.stack([a[:NP].reshape(NP, 4, 128) for a in k])[None]
    vp = np.stack([a[:NP].reshape(NP, 4, 128) for a in v])[None]
    kip = np.stack([a[:NP] for a in ki])[None]
    csp = np.stack([a[0:2] for a in cs])[None]
    ks = np.concatenate([a[NP:].reshape(4, 32, 4, 128) for a in k])[None]
    vs = np.concatenate([a[NP:].reshape(4, 32, 4, 128) for a in v])[None]
    kis = np.concatenate([a[NP:].reshape(4, 32, IDD) for a in ki])[None]
    css = np.concatenate([a[2:10].reshape(4, 2, 2 * DFF) for a in cs])[None]
    gv = np.concatenate([a.reshape(4, 32, DA) for a in vn])[None]
    return (y_prompt, y_sample, kp, vp, kip, csp, ks, vs, kis, css, gv)
```

```python
import numpy as np
import ml_dtypes
from contextlib import ExitStack
import concourse.bass as bass
import concourse.mybir as mybir
from concourse.bass_utils import run_bass_kernel_spmd

F32 = mybir.dt.float32
BF16 = mybir.dt.bfloat16
ALU = mybir.AluOpType
AF = mybir.ActivationFunctionType
AX = mybir.AxisListType

ENGS = ("pe", "act", "dve", "pool", "sp")
NDMASEM = 16


class Buf:
    __slots__ = ("name", "w", "r", "rd")

    def __init__(self, name=""):
        self.name = name
        self.w = None
        self.r = {}
        self.rd = []


class Op:
    __slots__ = ("eng", "fn", "deps", "sig", "sem", "val", "dma")

    def __init__(self, eng, fn, dma):
        self.eng = eng
        self.fn = fn
        self.dma = dma
        self.deps = []
        self.sig = dma
        self.sem = None
        self.val = 0


class Prog:
    def __init__(self, nc):
        self.nc = nc
        self.q = {e: [] for e in ENGS}
        self.pend_dma = []

    def op(self, eng, fn, reads=(), writes=(), dma=False):
        o = Op(eng, fn, dma)
        deps = {}
        for b in reads:
            if b.w is not None:
                deps[id(b.w)] = b.w
        for b in writes:
            if b.w is not None:
                deps[id(b.w)] = b.w
            for d in b.r.values():
                deps[id(d)] = d
            for d in b.rd:
                deps[id(d)] = d
        for d in deps.values():
            if d is o:
                continue
            if eng == "pe" and d.eng == "pe" and not d.dma and not dma:
                continue
            d.sig = True
            o.deps.append(d)
        for b in reads:
            if dma:
                b.rd.append(o)
            else:
                b.r[eng] = o
        for b in writes:
            b.w = o
            b.r = {}
            b.rd = []
        self.q[eng].append(o)
        if dma:
            self.pend_dma.append(o)
        return o

    def dma(self, q, out, in_, reads=(), writes=(), **kw):
        return self.op(q, lambda e: e.dma_start(out=out, in_=in_, **kw), reads, writes, dma=True)

    def fence(self):
        lasts = list(self.pend_dma)
        self.pend_dma = []
        for e in ENGS:
            for o in reversed(self.q[e]):
                if o.fn is not None and not o.dma:
                    o.sig = True
                    lasts.append(o)
                    break
        for e in ENGS:
            b = Op(e, None, False)
            b.deps = list(lasts)
            self.q[e].append(b)

    def emit(self):
        nc = self.nc
        with ExitStack() as st:
            esem = {e: st.enter_context(nc.semaphore("s_" + e)) for e in ENGS}
            dsem = {e: [st.enter_context(nc.semaphore("d_%s%d" % (e, i))) for i in range(NDMASEM)]
                    for e in ("sp", "pool", "act")}
            for e in ENGS:
                cnt = 0
                dcnt = [0] * NDMASEM
                di = 0
                for o in self.q[e]:
                    if o.dma:
                        k = di % NDMASEM
                        di += 1
                        dcnt[k] += 16
                        o.sem = dsem[e][k]
                        o.val = dcnt[k]
                    elif o.sig:
                        cnt += 1
                        o.sem = esem[e]
                        o.val = cnt
            block = st.enter_context(nc.Block())

            def run(eng_obj, e):
                waited = {}
                for o in self.q[e]:
                    if o.fn is None:
                        best = {}
                        for d in o.deps:
                            if id(d.sem) not in best or best[id(d.sem)].val < d.val:
                                best[id(d.sem)] = d
                        o.deps = list(best.values())
                    for d in o.deps:
                        key = id(d.sem)
                        if waited.get(key, 0) < d.val:
                            eng_obj.wait_ge(d.sem, d.val)
                            waited[key] = d.val
                    if o.fn is None:
                        continue
                    if o.dma and o.val > 16 and waited.get(id(o.sem), 0) < o.val - 16:
                        eng_obj.wait_ge(o.sem, o.val - 16)
                        waited[id(o.sem)] = o.val - 16
                    ins = o.fn(eng_obj)
                    if o.dma:
                        ins.then_inc(o.sem, 16)
                    elif o.sig:
                        ins.then_inc(o.sem, 1)

            @block.tensor
            def _(t):
                run(t, "pe")

            @block.scalar
            def _(a):
                run(a, "act")

            @block.vector
            def _(v):
                run(v, "dve")

            @block.gpsimd
            def _(g):
                run(g, "pool")

            @block.sync
            def _(s):
                run(s, "sp")


D = 4096
NP = 2048
NS = 128
N = NP + NS
NT = N // 128
DA = 2048
DFF = 11008
NH = 16
NKV = 4
HD = 128
NIH = 16
IDD = 64
PAST = 2048
TOPK = 256
EPS = 1e-6
THETA = 500000.0
U0, VA0, Q0, K0, V0, QI0, KI0, WI0, GA0, GB0, INC = 0, 2048, 4096, 6144, 6656, 7168, 8192, 8256, 8272, 12368, 16464
PJ_VA, PJ_Q, PJ_K, PJ_V, PJ_QI, PJ_KI, PJ_WI, PJW = 0, 2048, 4096, 4608, 5120, 6144, 6208, 6224
ARENA_F32 = 52992
NEG = -1.0e30
TOKG = [(0, 512), (512, 512), (1024, 512), (1536, 512), (2048, 128)]


class Arena:
    def __init__(self, ap):
        self.ap = ap
        self.off = 0

    def mark(self):
        return self.off

    def reset(self, m=0):
        self.off = m

    def alloc(self, shape_free, dtype):
        n = 1
        for s in shape_free:
            n *= s
        bpe = 2 if dtype == BF16 else 4
        nbytes = (n * bpe + 31) // 32 * 32
        assert self.off + nbytes <= ARENA_F32 * 4, ("arena overflow", self.off, nbytes)
        a = self.ap[:, self.off // 4:(self.off + nbytes) // 4]
        self.off += nbytes
        if dtype == BF16:
            a = a.bitcast(BF16)
        a = a[:, 0:n]
        if len(shape_free) == 2:
            a = a.rearrange("p (a b) -> p a b", a=shape_free[0])
        elif len(shape_free) == 3:
            a = a.rearrange("p (a b c) -> p a b c", a=shape_free[0], b=shape_free[1])
        return a


def build_program(dbg=None):
    dbg = dbg or {}
    stop = dbg.get("stop")
    dbg_outs = set(dbg.get("outs", ()))
    only = dbg.get("only")
    cut = dbg.get("cut", 99)

    def want(name):
        return only is None or name in only
    nc = bass.Bass("TRN2", target_bir_lowering=False)
    P = Prog(nc)

    def din(name, shape, dt=F32):
        return nc.dram_tensor(name, list(shape), dt, kind="ExternalInput").ap()

    def dout(name, shape, dt=F32):
        return nc.dram_tensor(name, list(shape), dt, kind="ExternalOutput").ap()

    def dscr(name, shape, dt):
        kind = "ExternalOutput" if name in dbg_outs else "Internal"
        return nc.dram_tensor(name, list(shape), dt, kind=kind).ap()

    xin = din("xin", [N, D])
    ck = din("ck", [4, PAST, 512])
    cv = din("cv", [4, PAST, 512])
    cki = din("cki", [4, PAST, IDD])
    cst = din("cst", [8, 2 * DFF])
    g_attn = din("g_attn", [D])
    w_in = din("w_in", [D, INC])
    g_gmlp = din("g_gmlp", [DA])
    ws = din("ws", [8, 128, 128])
    gbias = din("gbias", [8, 128])
    w_a = din("w_a", [DA, D])
    w_b = din("w_b", [DA, D])
    w_o = din("w_o", [D, D])
    g_ffn = din("g_ffn", [D])
    w_up = din("w_up", [D, 2 * DFF])
    conv_w = din("conv_w", [3, 2 * DFF])
    conv_b = din("conv_b", [2 * DFF])
    w_down = din("w_down", [DFF, D])
    g_final = din("g_final", [D])
    c_ident = din("c_ident", [128, 128])
    c_csq = din("c_csq", [N, 32])
    c_csi = din("c_csi", [N, 16])
    c_bd = din("c_bd", [128, 128])
    o_y = dout("o_y", [N, D])
    o_k = dout("o_k", [N, 512])
    o_v = dout("o_v", [N, 512])
    o_ki = dout("o_ki", [N, IDD])
    o_cst = dout("o_cst", [10, 2 * DFF])
    o_vn = dout("o_vn", [NS, DA])
    out_bufs = []
    projT = dscr("projT", [N, PJW], F32)
    uT = dscr("uT", [DA, N], BF16)
    gaT = dscr("gaT", [D, N], BF16)
    gbT = dscr("gbT", [D, N], BF16)
    yT = dscr("yT", [D, N], BF16)
    x2 = dscr("x2", [N, D], F32)
    actT = dscr("actT", [DFF, N], BF16)
    fsc = dscr("fsc", [N, D], F32)
    wdc_t = dscr("wdc", [32, 128, (DFF // 128) * 128], BF16)
    wdc = [wdc_t[c].rearrange("p (k j) -> p k j", j=128) for c in range(32)]
    Bwdc = [Buf("wdc%d" % c) for c in range(32)]
    aTd = dscr("aTd", [DA, N], BF16)
    bTd = dscr("bTd", [DA, N], BF16)
    B_projT, B_uT, B_gaT, B_gbT, B_yT, B_x2, B_actT, B_fsc, B_aTd, B_bTd = [Buf(n) for n in
        ("projT", "uT", "gaT", "gbT", "yT", "x2", "actT", "fsc", "aTd", "bTd")]

    arena_t = nc.alloc_sbuf_tensor("arena", [128, ARENA_F32], F32)
    AR = Arena(arena_t[:])
    banks = [nc.alloc_psum_tensor("bank%d" % i, [128, 512], F32) for i in range(8)]
    Bbank = [Buf("bank%d" % i) for i in range(8)]
    bankctr = [0]

    def next_bank():
        i = bankctr[0] % 8
        bankctr[0] += 1
        return banks[i], Bbank[i]

    evctr = [0]

    def evac_eng():
        evctr[0] += 1
        return "act" if evctr[0] % 2 else "dve"

    def copy_op(eng, out, in_):
        if eng == "act":
            return lambda e: e.activation(out=out, in_=in_, func=AF.Copy)
        return lambda e: e.tensor_copy(out=out, in_=in_)

    identf = AR.alloc([128], F32)
    identb = AR.alloc([128], BF16)
    B_ident = Buf("ident")
    P.dma("sp", identf, c_ident[:, :], writes=[B_ident])
    P.op("dve", lambda e: e.tensor_copy(out=identb, in_=identf), [B_ident], [B_ident])
    base_mark = AR.mark()

    def norm_transpose(src, Bsrc, gvec, hT, B_hT):
        m = AR.mark()
        g_bc = AR.alloc([D], F32)
        xt = [AR.alloc([D], F32) for _ in range(2)]
        hb = AR.alloc([D], BF16)
        junk = AR.alloc([D], BF16)
        st = AR.alloc([3 * NT], F32)
        Bg, Bxt, Bhb, Bjunk = Buf("g"), [Buf("xt0"), Buf("xt1")], Buf("hb"), Buf("junk")
        Bst = [Buf("st%d" % t) for t in range(NT)]
        P.dma("sp", g_bc, gvec.partition_broadcast(128), writes=[Bg])
        for t in range(NT):
            x_t, Bx = xt[t % 2], Bxt[t % 2]
            P.dma("sp", x_t, src[t * 128:(t + 1) * 128, :], reads=[Bsrc], writes=[Bx])
            ss, sd, rs = st[:, 3 * t:3 * t + 1], st[:, 3 * t + 1:3 * t + 2], st[:, 3 * t + 2:3 * t + 3]
            P.op("act", lambda e, x_t=x_t, ss=ss: e.activation(out=junk, in_=x_t, func=AF.Square, accum_out=ss),
                 [Bx], [Bjunk, Bst[t]])
            P.op("act", lambda e, ss=ss, sd=sd: e.activation(out=sd, in_=ss, func=AF.Sqrt, scale=1.0 / D, bias=EPS),
                 [Bst[t]], [Bst[t]])
            P.op("dve", lambda e, sd=sd, rs=rs: e.reciprocal(out=rs, in_=sd), [Bst[t]], [Bst[t]])
            P.op("dve", lambda e, x_t=x_t, rs=rs: e.scalar_tensor_tensor(out=hb, in0=x_t, scalar=rs, in1=g_bc,
                                                                     op0=ALU.mult, op1=ALU.mult),
                 [Bx, Bst[t], Bg], [Bhb])
            for q4 in range(4):
                bank, Bb = next_bank()
                pb = bank[:].bitcast(BF16)
                for j in range(8):
                    kc = q4 * 8 + j
                    P.op("pe", lambda e, pb=pb, j=j, kc=kc: e.transpose(out=pb[:, j * 128:(j + 1) * 128],
                                                                      in_=hb[:, kc * 128:(kc + 1) * 128],
                                                                      identity=identb),
                         [Bhb, B_ident], [Bb])
                eng = evac_eng()
                o_ap = hT[:, q4 * 8:(q4 + 1) * 8, t * 128:(t + 1) * 128]
                i_ap = pb.rearrange("p (a b) -> p a b", a=8)
                P.op(eng, copy_op(eng, o_ap, i_ap), [Bb], [B_hT])
        P.fence()
        AR.reset(m)

    def gemm(xT, B_xT, KC, W, blocks, epilogue_T=None, epilogue_F=None, tok_groups=TOKG, tiles=range(NT),
             wcols=512):
        m = AR.mark()
        wb = [AR.alloc([KC, wcols], BF16) for _ in range(2)]
        Bwb = [Buf("wb0"), Buf("wb1")]
        Wv = W.rearrange("(kc p) c -> p kc c", p=128)

        def load(i):
            c0, ncols, mode, tag = blocks[i]
            P.dma("pool", wb[i % 2][:, :, 0:ncols], Wv[:, :, c0:c0 + ncols], writes=[Bwb[i % 2]])

        load(0)
        for i, (c0, ncols, mode, tag) in enumerate(blocks):
            if i + 1 < len(blocks):
                load(i + 1)
            w_i, Bw = wb[i % 2], Bwb[i % 2]
            if mode == "T":
                for t in tiles:
                    bank, Bb = next_bank()
                    for kc in range(KC):
                        P.op("pe", lambda e, bank=bank, kc=kc, t=t, w_i=w_i, ncols=ncols:
                             e.matmul(bank[:, 0:ncols], lhsT=xT[:, kc, t * 128:(t + 1) * 128],
                                      rhs=w_i[:, kc, 0:ncols], start=(kc == 0), stop=(kc == KC - 1)),
                             [B_xT, Bw], [Bb])
                    epilogue_T(tag, c0, ncols, t, bank, Bb)
            else:
                for cj in range(ncols // 128):
                    for (tk0, ntk) in tok_groups:
                        bank, Bb = next_bank()
                        for kc in range(KC):
                            P.op("pe", lambda e, bank=bank, kc=kc, cj=cj, tk0=tk0, ntk=ntk, w_i=w_i:
                                 e.matmul(bank[:, 0:ntk], lhsT=w_i[:, kc, cj * 128:(cj + 1) * 128],
                                          rhs=xT[:, kc, tk0:tk0 + ntk], start=(kc == 0), stop=(kc == KC - 1)),
                                 [B_xT, Bw], [Bb])
                        epilogue_F(tag, c0 + cj * 128, tk0, ntk, bank, Bb)
        AR.reset(m)

    hT = AR.alloc([32, N], BF16)
    B_hT = Buf("hT")
    B_xin = Buf("xin")
    if want("s1"):
        norm_transpose(xin, B_xin, g_attn, hT, B_hT)

    def s2():
        m = AR.mark()
        stg = [AR.alloc([512], F32) for _ in range(2)]
        stgb = [AR.alloc([512], BF16) for _ in range(2)]
        Bstg = [Buf("stg0"), Buf("stg1")]
        Bstgb = [Buf("stgb0"), Buf("stgb1")]
        ctr = [0, 0]

        def epi_T(tag, c0, ncols, t, bank, Bb):
            i = ctr[0] % 2
            ctr[0] += 1
            eng = evac_eng()
            P.op(eng, copy_op(eng, stg[i][:, 0:ncols], bank[:, 0:ncols]), [Bb], [Bstg[i]])
            pc = c0 - VA0
            P.dma("sp", projT[t * 128:(t + 1) * 128, pc:pc + ncols], stg[i][:, 0:ncols], reads=[Bstg[i]],
                  writes=[B_projT])

        def epi_F(tag, c, tk0, ntk, bank, Bb):
            i = ctr[1] % 2
            ctr[1] += 1
            if tag == "u":
                eng = evac_eng()
                P.op(eng, copy_op(eng, stgb[i][:, 0:ntk], bank[:, 0:ntk]), [Bb], [Bstgb[i]])
                dst, Bd, r0 = uT, B_uT, c - U0
            else:
                P.op("act", lambda e, i=i, ntk=ntk, bank=bank: e.activation(out=stgb[i][:, 0:ntk], in_=bank[:, 0:ntk],
                                                                          func=AF.Sigmoid), [Bb], [Bstgb[i]])
                if tag == "ga":
                    dst, Bd, r0 = gaT, B_gaT, c - GA0
                else:
                    dst, Bd, r0 = gbT, B_gbT, c - GB0
            P.dma("sp", dst[r0:r0 + 128, tk0:tk0 + ntk], stgb[i][:, 0:ntk], reads=[Bstgb[i]], writes=[Bd])

        blocks = []
        for j in range(4):
            blocks.append((VA0 + 512 * j, 512, "T", "va"))
        for j in range(8):
            blocks.append((Q0 + 512 * j, 512, "T", "qkv"))
        blocks.append((KI0, 80, "T", "kiwi"))
        for j in range(4):
            blocks.append((U0 + 512 * j, 512, "F", "u"))
        for j in range(8):
            blocks.append((GA0 + 512 * j, 512, "F", "ga"))
        for j in range(8):
            blocks.append((GB0 + 512 * j, 512, "F", "gb"))
        gemm(hT, B_hT, 32, w_in, blocks, epi_T, epi_F)
        P.fence()
        AR.reset(m)

    if want("s2"):
        s2()
    if stop == "s2":
        return finish(nc, P, out_bufs)

    def bias4(bias_bc, bi, q4):
        b = bias_bc[:, bi, q4 * 256:(q4 + 1) * 256].rearrange("p (g i) -> p g i", g=2)
        return b.unsqueeze(2).to_broadcast([128, 2, 2, 128])

    def s3():
        AR.reset(base_mark)
        gn_bc = AR.alloc([DA], F32)
        Bgn = Buf("gn")
        P.dma("sp", gn_bc, g_gmlp.partition_broadcast(128), writes=[Bgn])
        wn = AR.alloc([8, 128], F32)
        wns = AR.alloc([8, 128], F32)
        wnb = AR.alloc([8, 128], BF16)
        wnsb = AR.alloc([8, 128], BF16)
        wmT = AR.alloc([8, 128], BF16)
        wmTs = AR.alloc([8, 128], BF16)
        bias_bc = AR.alloc([2, 1024], F32)
        stmp = AR.alloc([512], F32)
        Bstmp = Buf("stmp")
        Bwn, Bwns, Bwmt, Bbr = Buf("wn"), Buf("wns"), Buf("wmT"), Buf("br")
        P.dma("sp", wn, ws.rearrange("g i j -> i g j"), writes=[Bwn])
        P.op("dve", lambda e: e.memset(wn[0:64, :, 64:128], 0.0), [], [Bwn])
        P.op("dve", lambda e: e.tensor_copy(out=wnb, in_=wn), [Bwn], [Bwn])
        P.op("pool", lambda e: e.memset(wns, 0.0), [], [Bwns])
        for s_ in range(4):
            P.dma("sp", wns[32 * s_:32 * s_ + 32, :, 32 * s_:32 * s_ + 32],
                  ws[:, 0:32, 0:32].rearrange("g i j -> i g j"), writes=[Bwns])
        P.op("pool", lambda e: e.tensor_copy(out=wnsb, in_=wns), [Bwns], [Bwns])
        for (src_b, dst, Bs) in ((wnb, wmT, Bwn), (wnsb, wmTs, Bwns)):
            bank, Bb = next_bank()
            pb = bank[:].bitcast(BF16)
            for g in range(8):
                P.op("pe", lambda e, pb=pb, g=g, src_b=src_b: e.transpose(out=pb[:, g * 128:(g + 1) * 128],
                                                                        in_=src_b[:, g, :], identity=identb),
                     [Bs, B_ident], [Bb])
            P.op("dve", lambda e, pb=pb, dst=dst: e.tensor_copy(out=dst, in_=pb.rearrange("p (a b) -> p a b", a=8)),
                 [Bb], [Bwmt])
        P.dma("sp", bias_bc[:, 0, :], gbias.rearrange("g i -> (g i)").partition_broadcast(128), writes=[Bbr])
        for s_ in range(4):
            P.op("dve", lambda e, s_=s_: e.tensor_copy(
                out=bias_bc[:, 1, :].rearrange("p (g i) -> p g i", g=8)[:, :, 32 * s_:32 * s_ + 32],
                in_=bias_bc[:, 0, :].rearrange("p (g i) -> p g i", g=8)[:, :, 0:32]), [Bbr], [Bbr])
        if cut == 1:
            P.fence()
            return

        va = [AR.alloc([DA], F32) for _ in range(2)]
        Bva = [Buf("va0"), Buf("va1")]
        vnb = AR.alloc([DA], BF16)
        vnf = AR.alloc([DA], F32)
        junk = AR.alloc([DA], BF16)
        Bvnb, Bvnf, Bjunk = Buf("vnb"), Buf("vnf"), Buf("junk3")
        st = AR.alloc([3 * NT], F32)
        Bst = [Buf("st3_%d" % t) for t in range(NT)]
        uTs = [AR.alloc([16, 512], BF16) for _ in range(2)]
        BuTs = [Buf("uTs0"), Buf("uTs1")]
        aTg = [AR.alloc([16, 512], BF16) for _ in range(2)]
        BaTg = [Buf("aTg0"), Buf("aTg1")]
        uTv = uT.rearrange("(dc p) n -> p dc n", p=128)
        aTv = aTd.rearrange("(dc p) n -> p dc n", p=128)
        for gi, (tk0, ntk) in enumerate(TOKG):
            u_g, Bu = uTs[gi % 2], BuTs[gi % 2]
            a_g, Ba = aTg[gi % 2], BaTg[gi % 2]
            P.dma("sp", u_g[:, :, 0:ntk], uTv[:, :, tk0:tk0 + ntk], reads=[B_uT], writes=[Bu])
            for tl in range(ntk // 128):
                t = tk0 // 128 + tl
                v_t, Bv = va[t % 2], Bva[t % 2]
                P.dma("sp", v_t, projT[t * 128:(t + 1) * 128, PJ_VA:PJ_VA + DA], reads=[B_projT], writes=[Bv])
                ss, sd, rs = st[:, 3 * t:3 * t + 1], st[:, 3 * t + 1:3 * t + 2], st[:, 3 * t + 2:3 * t + 3]
                P.op("act", lambda e, v_t=v_t, ss=ss: e.activation(out=junk, in_=v_t, func=AF.Square, accum_out=ss),
                     [Bv], [Bjunk, Bst[t]])
                P.op("act", lambda e, ss=ss, sd=sd: e.activation(out=sd, in_=ss, func=AF.Sqrt, scale=1.0 / DA, bias=EPS),
                     [Bst[t]], [Bst[t]])
                P.op("dve", lambda e, sd=sd, rs=rs: e.reciprocal(out=rs, in_=sd), [Bst[t]], [Bst[t]])
                P.op("dve", lambda e, v_t=v_t, rs=rs: e.scalar_tensor_tensor(out=vnb, in0=v_t, scalar=rs, in1=gn_bc,
                                                                         op0=ALU.mult, op1=ALU.mult),
                     [Bv, Bst[t], Bgn], [Bvnb])
                if t == NT - 1:
                    P.op("dve", lambda e, v_t=v_t, rs=rs: e.scalar_tensor_tensor(out=vnf, in0=v_t, scalar=rs, in1=gn_bc,
                                                                             op0=ALU.mult, op1=ALU.mult),
                         [Bv, Bst[t], Bgn], [Bvnf])
                    ob = Buf("o_vn")
                    out_bufs.append(ob)
                    P.dma("sp", o_vn[:, :], vnf, reads=[Bvnf], writes=[ob])
                wm_t = wmTs if t == NT - 1 else wmT
                bi = 1 if t == NT - 1 else 0
                for q4 in range(4 if cut > 2 else 0):
                    bank, Bb = next_bank()
                    for j in range(4):
                        dc = q4 * 4 + j
                        g = dc // 2
                        P.op("pe", lambda e, bank=bank, j=j, dc=dc, g=g, wm_t=wm_t:
                             e.matmul(bank[:, j * 128:(j + 1) * 128], lhsT=vnb[:, dc * 128:(dc + 1) * 128],
                                      rhs=wm_t[:, g, :], start=True, stop=True),
                             [Bvnb, Bwmt], [Bb])
                    if cut == 3:
                        continue
                    P.op("dve", lambda e, bank=bank, q4=q4, bi=bi: e.tensor_tensor(
                        out=stmp.rearrange("p (g r i) -> p g r i", g=2, r=2),
                        in0=bank[:, 0:512].rearrange("p (g r i) -> p g r i", g=2, r=2),
                        in1=bias4(bias_bc, bi, q4), op=ALU.add),
                        [Bb, Bbr], [Bstmp])
                    if cut == 4:
                        continue
                    P.op("dve", lambda e, q4=q4, tl=tl, a_g=a_g, u_g=u_g:
                         e.tensor_tensor(out=a_g[:, q4 * 4:(q4 + 1) * 4, tl * 128:(tl + 1) * 128],
                                         in0=stmp.rearrange("p (a b) -> p a b", a=4),
                                         in1=u_g[:, q4 * 4:(q4 + 1) * 4, tl * 128:(tl + 1) * 128], op=ALU.mult),
                         [Bstmp, Bu], [Ba])
            P.dma("sp", aTv[:, :, tk0:tk0 + ntk], a_g[:, :, 0:ntk], reads=[Ba], writes=[B_aTd])
        P.fence()

    if want("s3"):
        s3()
    if stop == "s3":
        return finish(nc, P, out_bufs)

    def s4():
        AR.reset(base_mark)
        SCALE = float(HD) ** -0.5
        MNEG = -30000.0
        WSC = float(NIH) ** -0.5 * float(IDD) ** -0.5
        RAWW = PJW - PJ_Q
        KT0, Vb0, kiT20 = AR.alloc([4, NP], BF16), AR.alloc([16, 512], BF16), AR.alloc([NP], BF16)
        m_s = AR.mark()
        KT1, Vb1, kiT21 = AR.alloc([4, NP], BF16), AR.alloc([16, 512], BF16), AR.alloc([NP], BF16)
        ckb = AR.alloc([16, 512], BF16)
        ckib = AR.alloc([16, 128], BF16)
        m_e = AR.mark()
        AR.reset(m_s)
        raw_b, qkb_b, qib_b, kib_b, wsm_b = (AR.alloc([RAWW], F32), AR.alloc([2560], BF16), AR.alloc([1024], BF16),
                                             AR.alloc([128], BF16), AR.alloc([16 * 6], F32))
        assert AR.mark() <= m_e
        AR.reset(m_e)
        KT, Vb, kiT2 = [KT0, KT1], [Vb0, Vb1], [kiT20, kiT21]
        BKT = [Buf("KT0"), Buf("KT1")]
        BVb = [Buf("Vb0"), Buf("Vb1")]
        BkiT = [Buf("kiT0"), Buf("kiT1")]
        Bckb, Bckib = Buf("ckb"), Buf("ckib")
        csq = AR.alloc([NT, 32], F32)
        csi = AR.alloc([NT, 16], F32)
        bd = AR.alloc([128], F32)
        Bcs = Buf("cs")
        P.dma("sp", csq, c_csq.rearrange("(t p) c -> p t c", p=128), writes=[Bcs])
        P.dma("sp", csi, c_csi.rearrange("(t p) c -> p t c", p=128), writes=[Bcs])
        P.dma("sp", bd, c_bd[:, :], writes=[Bcs])
        ones_b = AR.alloc([128], BF16)
        diagT = AR.alloc([128], BF16)
        P.op("dve", lambda e: e.memset(ones_b, 1.0), [], [Bcs])
        P.op("dve", lambda e: e.memset(diagT, 0.0), [], [Bcs])
        P.op("dve", lambda e: e.memset(diagT[0:64, 64:128], MNEG), [], [Bcs])
        I4 = AR.alloc([4, 128], BF16)
        for h_ in range(4):
            P.op("dve", lambda e, h_=h_: e.tensor_copy(out=I4[:, h_, :], in_=identb), [B_ident], [Bcs])
        raw2 = [AR.alloc([RAWW], F32), raw_b]
        Braw2 = [Buf("raw0"), Buf("raw1")]
        tmp = [AR.alloc([20 * 16], F32) for _ in range(4)]
        Btmp = Buf("ropetmp")
        qkb2 = [AR.alloc([2560], BF16), qkb_b]
        qib2 = [AR.alloc([1024], BF16), qib_b]
        kib2 = [AR.alloc([128], BF16), kib_b]
        Bqkb2, Bqib2, Bkib2 = [Buf("qkb0"), Buf("qkb1")], [Buf("qib0"), Buf("qib1")], [Buf("kib0"), Buf("kib1")]
        qT2 = [AR.alloc([16, 128], BF16) for _ in range(2)]
        qiT = AR.alloc([8, 128], BF16)
        kTs = AR.alloc([4, 128], BF16)
        Vnew = AR.alloc([512], BF16)
        kiTs = AR.alloc([128], BF16)
        BqT2 = [Buf("qT0"), Buf("qT1")]
        BqiT, BkTs, BVnew, BkiTs = Buf("qiT"), Buf("kTs"), Buf("Vnew"), Buf("kiTs")
        wsm2 = [AR.alloc([16 * 6], F32), wsm_b]
        Bw2 = [Buf("wsm0"), Buf("wsm1")]
        wsm, Bw = wsm2[0], Bw2[0]
        S = AR.alloc([2080], F32)
        mb2 = [AR.alloc([2080], BF16) for _ in range(2)]
        Bmb2 = [Buf("mb0"), Buf("mb1")]
        mx8 = AR.alloc([8], F32)
        BS, Bmx = Buf("S"), Buf("mx8")
        rtmp = [AR.alloc([512], F32) for _ in range(2)]
        Brtmp = [Buf("rtmp%d" % i) for i in range(2)]
        rtb = [AR.alloc([512], BF16) for _ in range(16)]
        Brtb = [Buf("rtb%d" % i) for i in range(16)]
        dg = AR.alloc([16, 128], BF16)
        Bdg = Buf("dg")
        PT = [AR.alloc([512], BF16) for _ in range(5)]
        BPT = [Buf("PT%d" % i) for i in range(5)]
        osb = [AR.alloc([512], F32) for _ in range(2)]
        dsb = [AR.alloc([512], F32) for _ in range(2)]
        Bosb = [Buf("osb0"), Buf("osb1")]
        Bdsb = [Buf("dsb0"), Buf("dsb1")]
        nrm = [0]

        bTg = [AR.alloc([16, 128], BF16) for _ in range(2)]
        BbTg = [Buf("bTg0"), Buf("bTg1")]
        snew = AR.alloc([128], F32)
        snew2 = AR.alloc([128], F32)
        mnew = AR.alloc([128], BF16)
        Bsnew, Bmnew = Buf("snew"), Buf("mnew")
        bTv = bTd.rearrange("(dc p) n -> p dc n", p=128)
        stb = [(banks[i], Bbank[i]) for i in (0, 1, 2, 6, 7)]
        sc_bank, Bsc = banks[3], Bbank[3]
        ob_ = [(banks[4], Bbank[4]), (banks[4], Bbank[4])]
        db_ = [(banks[5], Bbank[5]), (banks[5], Bbank[5])]
        LOOK = 4
        ctr = {"st": 0, "pt": 0, "rt": 0, "mm": 0}

        def nb():
            ctr["st"] += 1
            return stb[ctr["st"] % 5]

        def prep_a(t):
            raw, Braw = raw2[t % 2], Braw2[t % 2]
            qkb, Bqkb, qib, Bqib, kib, Bkib = qkb2[t % 2], Bqkb2[t % 2], qib2[t % 2], Bqib2[t % 2], kib2[t % 2], Bkib2[t % 2]
            wsm, Bw = wsm2[t % 2], Bw2[t % 2]
            P.dma("sp", raw, projT[t * 128:(t + 1) * 128, PJ_Q:PJW], reads=[B_projT], writes=[Braw])
            for (o0, nh, hd, half, cs_t) in ((0, 20, 128, 16, csq), (3072, 17, 64, 8, csi)):
                v3 = raw[:, o0:o0 + nh * hd].rearrange("p (h d) -> p h d", h=nh)
                x1, x2_ = v3[:, :, 0:half], v3[:, :, half:2 * half]
                cos = cs_t[:, t, 0:half].unsqueeze(1).to_broadcast([128, nh, half])
                sin = cs_t[:, t, half:2 * half].unsqueeze(1).to_broadcast([128, nh, half])
                tt = [tm[:, 0:nh * half].rearrange("p (h d) -> p h d", h=nh) for tm in tmp]
                P.op("pool", lambda e, tt=tt, x1=x1, cos=cos: e.tensor_tensor(out=tt[0], in0=x1, in1=cos, op=ALU.mult),
                     [Braw, Bcs], [Btmp])
                P.op("pool", lambda e, tt=tt, x2_=x2_, sin=sin: e.tensor_tensor(out=tt[1], in0=x2_, in1=sin, op=ALU.mult),
                     [Braw, Bcs], [Btmp])
                P.op("pool", lambda e, tt=tt, x2_=x2_, cos=cos: e.tensor_tensor(out=tt[2], in0=x2_, in1=cos, op=ALU.mult),
                     [Braw, Bcs], [Btmp])
                P.op("pool", lambda e, tt=tt, x1=x1, sin=sin: e.tensor_tensor(out=tt[3], in0=x1, in1=sin, op=ALU.mult),
                     [Braw, Bcs], [Btmp])
                P.op("pool", lambda e, tt=tt, x1=x1: e.tensor_tensor(out=x1, in0=tt[0], in1=tt[1], op=ALU.subtract),
                     [Btmp], [Braw])
                P.op("pool", lambda e, tt=tt, x2_=x2_: e.tensor_tensor(out=x2_, in0=tt[2], in1=tt[3], op=ALU.add),
                     [Btmp], [Braw])
            for (dst, c0, w) in ((o_k, 2048, 512), (o_v, 2560, 512), (o_ki, 4096, 64)):
                ob = Buf("ok")
                out_bufs.append(ob)
                P.dma("sp", dst[t * 128:(t + 1) * 128, :], raw[:, c0:c0 + w], reads=[Braw], writes=[ob])
            P.op("act", lambda e: e.activation(out=qkb, in_=raw[:, 0:2560], func=AF.Copy), [Braw], [Bqkb])
            wi = raw[:, 4160:4176]
            P.op("act", lambda e: e.activation(out=wsm[:, 0:16], in_=wi, func=AF.Abs, scale=WSC), [Braw], [Bw])
            P.op("act", lambda e: e.activation(out=wsm[:, 16:32], in_=wi, func=AF.Sign), [Braw], [Bw])
            P.op("pool", lambda e: e.tensor_tensor(out=qib.rearrange("p (h d) -> p h d", h=16),
                                                  in0=raw[:, 3072:4096].rearrange("p (h d) -> p h d", h=16),
                                                  in1=wsm[:, 0:16].unsqueeze(2).to_broadcast([128, 16, 64]),
                                                  op=ALU.mult), [Braw, Bw], [Bqib])
            P.op("pool", lambda e: e.tensor_copy(out=kib.rearrange("p (a d) -> p a d", a=2),
                                                in_=raw[:, 4096:4160].unsqueeze(1).to_broadcast([128, 2, 64])),
                 [Braw], [Bkib])

        def prep_b(t, kv):
            raw, Braw = raw2[t % 2], Braw2[t % 2]
            qkb, Bqkb, qib, Bqib, kib, Bkib = qkb2[t % 2], Bqkb2[t % 2], qib2[t % 2], Bqib2[t % 2], kib2[t % 2], Bkib2[t % 2]
            sample = (t == NT - 1)
            for b0 in (0, 8, 16):
                nb_ = min(8, 20 - b0)
                bank, Bb = nb()
                pb = bank[:].bitcast(BF16)
                for j in range(nb_):
                    hh = b0 + j
                    P.op("pe", lambda e, pb=pb, j=j, hh=hh: e.transpose(out=pb[:, j * 128:(j + 1) * 128],
                                                                      in_=qkb[:, hh * 128:(hh + 1) * 128],
                                                                      identity=identb), [Bqkb, B_ident], [Bb])
                if b0 < 16:
                    P.op("act", lambda e, pb=pb, b0=b0, t=t: e.activation(out=qT2[t % 2][:, b0:b0 + 8, :],
                                                                         in_=pb.rearrange("p (a b) -> p a b", a=8),
                                                                         func=AF.Copy), [Bb], [BqT2[t % 2]])
                else:
                    src3 = pb[:, 0:512].rearrange("p (a b) -> p a b", a=4)
                    if sample:
                        P.op("act", lambda e, src3=src3: e.activation(out=kTs, in_=src3, func=AF.Copy), [Bb], [BkTs])
                    else:
                        P.op("act", lambda e, src3=src3, kv=kv, t=t: e.activation(
                            out=KT[kv][:, :, t * 128:(t + 1) * 128], in_=src3, func=AF.Copy), [Bb], [BKT[kv]])
            bank, Bb = nb()
            pb = bank[:].bitcast(BF16)
            for j in range(8):
                P.op("pe", lambda e, pb=pb, j=j: e.transpose(out=pb[:, j * 128:(j + 1) * 128],
                                                            in_=qib[:, j * 128:(j + 1) * 128], identity=identb),
                     [Bqib, B_ident], [Bb])
            P.op("dve", lambda e, pb=pb: e.tensor_copy(out=qiT, in_=pb.rearrange("p (a b) -> p a b", a=8)),
                 [Bb], [BqiT])
            bank, Bb = nb()
            pb = bank[:].bitcast(BF16)
            P.op("pe", lambda e, pb=pb: e.transpose(out=pb[:, 0:128], in_=kib, identity=identb), [Bkib, B_ident], [Bb])
            if sample:
                P.op("dve", lambda e, pb=pb: e.tensor_copy(out=kiTs, in_=pb[:, 0:128]), [Bb], [BkiTs])
                P.op("dve", lambda e: e.tensor_copy(out=Vnew, in_=raw[:, 2560:3072]), [Braw], [BVnew])
            else:
                P.op("dve", lambda e, pb=pb, kv=kv, t=t: e.tensor_copy(out=kiT2[kv][:, t * 128:(t + 1) * 128],
                                                                     in_=pb[:, 0:128]), [Bb], [BkiT[kv]])
                P.op("dve", lambda e, kv=kv, t=t: e.tensor_copy(out=Vb[kv][:, t, :], in_=raw[:, 2560:3072]),
                     [Braw], [BVb[kv]])


        def prep_tile(t, kv):
            prep_a(t)
            prep_b(t, kv)

        def score_block(kiT_ap, BkiTb, k0, w, sg_ap_fn, first, dst, Bdst, pe_acc=False):
            zb = [None] * 16

            def emit_z(h):
                bank, Bb = nb()
                zb[h] = (bank, Bb)
                r0 = (h % 2) * 64
                P.op("pe", lambda e, bank=bank, h=h, r0=r0, k0=k0, w=w:
                     e.matmul(bank[:, 0:w], lhsT=qiT[r0:r0 + 64, h // 2, :], rhs=kiT_ap[r0:r0 + 64, k0:k0 + w],
                              start=True, stop=True), [BqiT, BkiTb], [Bb])

            LZ = 4 if pe_acc else 3
            for h in range(LZ):
                emit_z(h)
            for h in range(16):
                if h + LZ < 16:
                    emit_z(h + LZ)
                bank, Bb = zb[h]
                if pe_acc:
                    P.op("act", lambda e, bank=bank, h=h, w=w: e.activation(out=rtb[h][:, 0:w], in_=bank[:, 0:w],
                                                                          func=AF.Relu), [Bb], [Brtb[h]])
                    P.op("pe", lambda e, w=w, h=h: e.matmul(sc_bank[:, 0:w], lhsT=dg[:, h, :], rhs=rtb[h][:, 0:w],
                                                           start=(h == 0), stop=(h == 15)), [Bdg, Brtb[h]], [Bsc])
                    continue
                ctr["rt"] += 1
                ri = ctr["rt"] % 2
                P.op("act", lambda e, bank=bank, ri=ri, w=w: e.activation(out=rtmp[ri][:, 0:w], in_=bank[:, 0:w],
                                                                        func=AF.Relu), [Bb], [Brtmp[ri]])
                sg = sg_ap_fn(h)
                if first and h == 0:
                    P.op("dve", lambda e, ri=ri, w=w, sg=sg, dst=dst: e.tensor_scalar(
                        out=dst[:, 0:w], in0=rtmp[ri][:, 0:w], scalar1=sg, scalar2=None, op0=ALU.mult),
                        [Brtmp[ri], Bw], [Bdst])
                else:
                    P.op("dve", lambda e, ri=ri, w=w, sg=sg, dst=dst: e.scalar_tensor_tensor(
                        out=dst[:, 0:w], in0=rtmp[ri][:, 0:w], scalar=sg, in1=dst[:, 0:w], op0=ALU.mult, op1=ALU.add),
                        [Brtmp[ri], Bw, Bdst], [Bdst])
            if pe_acc:
                P.op("act", lambda e, w=w, dst=dst: e.activation(out=dst[:, 0:w], in_=sc_bank[:, 0:w], func=AF.Copy),
                     [Bsc], [Bdst])

        def topk_mask(L, first_chunk_cut, mi):
            mb, Bm = mb2[mi], Bmb2[mi]
            if first_chunk_cut:
                P.op("dve", lambda e: e.memset(S[0:64, L - 64:L], -3.0e30), [], [BS])
            for r in range(TOPK // 8):
                P.op("dve", lambda e: e.max(out=mx8, in_=S[:, 0:L]), [BS], [Bmx])
                P.op("dve", lambda e: e.match_replace(out=S[:, 0:L], in_to_replace=mx8, in_values=S[:, 0:L],
                                                      imm_value=NEG), [BS, Bmx], [BS])
            P.op("dve", lambda e: e.tensor_scalar(out=mb[:, 0:L], in0=S[:, 0:L], scalar1=-5.0e29, scalar2=MNEG,
                                                  op0=ALU.is_gt, op1=ALU.mult), [BS], [Bm])
            if first_chunk_cut:
                P.op("dve", lambda e: e.memset(mb[0:64, L - 64:L], MNEG), [], [Bm])

        mulctr = [0]

        def attn_group(steps, o_flat, d_ap, Bo, Bd, out_ap, Bout):
            nst = len(steps)
            stbank = [None] * nst

            def emit_st(i):
                s_ = steps[i]
                bank, Bb = nb()
                stbank[i] = (bank, Bb)
                hasb = s_["bias"] is not None
                P.op("pe", lambda e, bank=bank, s_=s_, hasb=hasb: e.matmul(s_["st_out"](bank), lhsT=s_["lhsT_k"],
                                                                         rhs=s_["rhs_q"], start=True, stop=not hasb),
                     [s_["Bk"], s_["Bq"]], [Bb])
                if hasb:
                    P.op("pe", lambda e, bank=bank, s_=s_: e.matmul(s_["st_out"](bank), lhsT=s_["bias"],
                                                                  rhs=s_["bias_rhs"], start=False, stop=True),
                         [s_["Bbias"], Bcs], [Bb])

            for i in range(min(LOOK, nst)):
                emit_st(i)
            for i in range(nst):
                if i + LOOK < nst:
                    emit_st(i + LOOK)
                s_ = steps[i]
                bank, Bb = stbank[i]
                w = s_["st_w"]
                ctr["pt"] += 1
                pi = ctr["pt"] % 5
                pt = PT[pi][:, 0:w]
                P.op("act", lambda e, bank=bank, pt=pt, w=w: e.activation(out=pt, in_=bank[:, 0:w], func=AF.Exp,
                                                                        scale=SCALE), [Bb], [BPT[pi]])
                P.op("pe", lambda e, s_=s_, pt=pt, i=i: e.matmul(s_["pv_out"], lhsT=s_["lhsT_v"], rhs=s_["pv_rhs"](pt),
                                                              start=(i == 0), stop=(i == nst - 1)),
                     [s_["Bv"], BPT[pi]], [Bo])
                P.op("pe", lambda e, s_=s_, pt=pt, i=i: e.matmul(s_["pd_out"], lhsT=ones_b, rhs=s_["pv_rhs"](pt),
                                                              start=(i == 0), stop=(i == nst - 1)),
                     [Bcs, BPT[pi]], [Bd])
            wtot = steps[0]["st_w"]
            nrm[0] += 1
            k = nrm[0] % 2
            rv = steps[0]["rec_view"]
            P.op("act", lambda e, k=k: e.activation(out=osb[k][:, 0:wtot], in_=o_flat, func=AF.Copy), [Bo], [Bosb[k]])
            P.op("act", lambda e, k=k: e.activation(out=dsb[k][:, 0:wtot], in_=d_ap, func=AF.Ln), [Bd], [Bdsb[k]])
            P.op("act", lambda e, k=k: e.activation(out=dsb[k][:, 0:wtot], in_=dsb[k][:, 0:wtot], func=AF.Exp,
                                                    scale=-1.0), [Bdsb[k]], [Bdsb[k]])
            P.op("pool", lambda e, k=k: e.tensor_tensor(out=out_ap, in0=rv(osb[k][:, 0:wtot]), in1=rv(dsb[k][:, 0:wtot]),
                                                        op=ALU.mult), [Bosb[k], Bdsb[k]], [Bout])

        def score_topk(t):
            L = 128 * (t + 1)
            if t >= 2:
                wsm_t, Bw_t = wsm2[t % 2], Bw2[t % 2]
                for h in range(16):
                    P.op("pool", lambda e, h=h, wsm_t=wsm_t: e.tensor_scalar(out=dg[:, h, :], in0=identf,
                                                                            scalar1=wsm_t[:, 16 + h:17 + h],
                                                                            scalar2=None, op0=ALU.mult),
                         [Bw_t, B_ident], [Bdg])
                for k0 in range(0, L, 512):
                    w = min(512, L - k0)
                    score_block(kiT2[0], BkiT[0], k0, w, None, True, S[:, k0:k0 + w], BS, pe_acc=True)
                topk_mask(L, True, t % 2)

        def attention_prompt(t):
            use_topk = t >= 2
            b_g, Bbg = bTg[t % 2], BbTg[t % 2]
            for n in range(4):
                (o_bank, Bo), (d_bank, Bd) = ob_[n % 2], db_[n % 2]
                steps = []
                for kt in range(t + 1):
                    if use_topk:
                        mk, Bmk = mb2[t % 2][:, kt * 128:(kt + 1) * 128], Bmb2[t % 2]
                    else:
                        mk, Bmk = (diagT, Bcs) if kt == t else (None, None)
                    steps.append(dict(
                        bias=mk, Bbias=Bmk, bias_rhs=I4,
                        lhsT_k=KT[0][:, n, kt * 128:(kt + 1) * 128], Bk=BKT[0],
                        rhs_q=qT2[t % 2][:, 4 * n:4 * n + 4, :], Bq=BqT2[t % 2], st_w=512,
                        st_out=lambda bank: bank[:, 0:512].rearrange("p (h q) -> p h q", h=4),
                        lhsT_v=Vb[0][:, kt, n * 128:(n + 1) * 128], Bv=BVb[0],
                        pv_out=o_bank[:, 0:512], pd_out=d_bank[:, 0:512], pv_rhs=lambda pt: pt,
                        rec_view=lambda r: r.rearrange("p (h q) -> p h q", h=4)))
                attn_group(steps, o_bank[:, 0:512], d_bank[:, 0:512], Bo, Bd,
                           b_g[:, 4 * n:4 * n + 4, :], Bbg)
            P.dma("sp", bTv[:, :, t * 128:(t + 1) * 128], b_g, reads=[Bbg], writes=[B_bTd])

        prep_a(0)
        prep_a(1)
        prep_b(0, 0)
        for t in range(16):
            if t + 1 < 16:
                prep_b(t + 1, 0)
                score_topk(t + 1)
            if t + 2 < 16:
                prep_a(t + 2)
            attention_prompt(t)

        P.fence()
        t = NT - 1
        prep_tile(t, 0)
        for s_ in range(4):
            P.op("dve", lambda e, s_=s_: e.tensor_scalar(out=wsm[:, 32 + 16 * s_:48 + 16 * s_], in0=wsm[:, 16:32],
                                                         scalar1=bd[:, 32 * s_:32 * s_ + 1], scalar2=None,
                                                         op0=ALU.mult), [Bw, Bcs], [Bw])
        for s_ in range(4):
            kv = s_ % 2
            ckv = cki[s_].rearrange("(kt p) c -> p kt c", p=128)
            P.dma("pool", ckib[:, :, 0:64], ckv, writes=[Bckib])
            P.dma("pool", ckib[:, :, 64:128], ckv, writes=[Bckib])
            for b0 in (0, 8):
                bank, Bb = nb()
                pb = bank[:].bitcast(BF16)
                for j in range(8):
                    P.op("pe", lambda e, pb=pb, j=j, b0=b0: e.transpose(out=pb[:, j * 128:(j + 1) * 128],
                                                                      in_=ckib[:, b0 + j, :], identity=identb),
                         [Bckib, B_ident], [Bb])
                P.op("dve", lambda e, pb=pb, b0=b0, kv=kv: e.tensor_copy(out=kiT2[kv][:, b0 * 128:(b0 + 8) * 128],
                                                                       in_=pb), [Bb], [BkiT[kv]])
            for k0 in range(0, PAST, 512):
                score_block(kiT2[kv], BkiT[kv], k0, 512,
                            lambda h, s_=s_: wsm[:, 32 + 16 * s_ + h:33 + 16 * s_ + h], s_ == 0, S[:, k0:k0 + 512], BS)
        score_block(kiTs, BkiTs, 0, 128, lambda h: wsm[:, 16 + h:17 + h], True, snew, Bsnew)
        P.op("dve", lambda e: e.tensor_tensor(out=snew2, in0=snew, in1=bd, op=ALU.mult), [Bsnew, Bcs], [Bsnew])
        P.op("dve", lambda e: e.tensor_reduce(out=S[:, PAST:PAST + 32], in_=snew2.rearrange("p (s j) -> p j s", s=4),
                                              axis=AX.X, op=ALU.add), [Bsnew], [BS])
        topk_mask(PAST + 32, False, 0)
        mbs, Bmbs = mb2[0], Bmb2[0]
        m3 = mnew.rearrange("p (s j) -> p s j", s=4)
        P.op("dve", lambda e: e.tensor_scalar(out=snew.rearrange("p (s j) -> p s j", s=4),
                                              in0=mbs[:, PAST:PAST + 32].unsqueeze(1).to_broadcast([128, 4, 32]),
                                              scalar1=-MNEG, scalar2=None, op0=ALU.add), [Bmbs], [Bsnew])
        P.op("dve", lambda e: e.tensor_tensor(out=snew2, in0=snew, in1=bd, op=ALU.mult), [Bsnew, Bcs], [Bsnew])
        P.op("dve", lambda e: e.tensor_scalar(out=mnew, in0=snew2, scalar1=MNEG, scalar2=None, op0=ALU.add),
             [Bsnew], [Bmnew])
        b_g, Bbg = bTg[0], BbTg[0]
        for s_ in range(4):
            kv = s_ % 2
            P.dma("pool", ckb, ck[s_].rearrange("(kt p) c -> p kt c", p=128), writes=[Bckb])
            P.dma("pool", Vb[kv], cv[s_].rearrange("(kt p) c -> p kt c", p=128), writes=[BVb[kv]])
            for kt2 in range(8):
                bank, Bb = nb()
                pb = bank[:].bitcast(BF16)
                for j in range(8):
                    ktl, n = j // 4, j % 4
                    kt = kt2 * 2 + ktl
                    P.op("pe", lambda e, pb=pb, j=j, kt=kt, n=n: e.transpose(out=pb[:, j * 128:(j + 1) * 128],
                                                                           in_=ckb[:, kt, n * 128:(n + 1) * 128],
                                                                           identity=identb), [Bckb, B_ident], [Bb])
                eng = evac_eng()
                o_ap = KT[kv][:, :, kt2 * 256:(kt2 + 1) * 256].rearrange("p n (k c) -> p n k c", k=2)
                i_ap = pb.rearrange("p (k n c) -> p n k c", k=2, n=4)
                P.op(eng, copy_op(eng, o_ap, i_ap), [Bb], [BKT[kv]])
            qs = slice(32 * s_, 32 * s_ + 32)
            for n in range(4):
                (o_bank, Bo), (d_bank, Bd) = ob_[n % 2], db_[n % 2]
                o3 = o_bank[:, 0:128].rearrange("p (h q) -> p h q", h=4)
                steps = []
                for kt in range(17):
                    if kt < 16:
                        lk, Bk_, lv, Bv_ = KT[kv][:, n, kt * 128:(kt + 1) * 128], BKT[kv], \
                            Vb[kv][:, kt, n * 128:(n + 1) * 128], BVb[kv]
                    else:
                        lk, Bk_, lv, Bv_ = kTs[:, n, :], BkTs, Vnew[:, n * 128:(n + 1) * 128], BVnew
                    steps.append(dict(
                        lhsT_k=lk, Bk=Bk_, rhs_q=qT2[0][:, 4 * n:4 * n + 4, qs], Bq=BqT2[0], st_w=128,
                        st_out=lambda bank: bank[:, 0:128].rearrange("p (h q) -> p h q", h=4),
                        bias=(mbs[:, kt * 128:(kt + 1) * 128] if kt < 16 else mnew),
                        Bbias=(Bmbs if kt < 16 else Bmnew), bias_rhs=I4[:, :, qs], lhsT_v=lv, Bv=Bv_,
                        pv_out=o_bank[:, 0:128], pd_out=d_bank[:, 0:128], pv_rhs=lambda pt: pt,
                        rec_view=lambda r: r.rearrange("p (h q) -> p h q", h=4)))
                attn_group(steps, o_bank[:, 0:128], d_bank[:, 0:128], Bo, Bd, b_g[:, 4 * n:4 * n + 4, qs], Bbg)
        P.dma("sp", bTv[:, :, NP:N], b_g, reads=[Bbg], writes=[B_bTd])
        P.fence()

    if want("s4"):
        s4()
    if stop == "s4":
        return finish(nc, P, out_bufs)

    def s5():
        AR.reset(base_mark)
        aT = AR.alloc([16, N], BF16)
        bT = AR.alloc([16, N], BF16)
        BaT, BbT = Buf("aT"), Buf("bT")
        aTv = aTd.rearrange("(dc p) n -> p dc n", p=128)
        bTv = bTd.rearrange("(dc p) n -> p dc n", p=128)
        for q in range(4):
            P.dma("sp", aT[:, 4 * q:4 * q + 4, :], aTv[:, 4 * q:4 * q + 4, :], reads=[B_aTd], writes=[BaT])
            P.dma("sp", bT[:, 4 * q:4 * q + 4, :], bTv[:, 4 * q:4 * q + 4, :], reads=[B_bTd], writes=[BbT])
        WC = 256
        wa = [AR.alloc([16, WC], BF16) for _ in range(2)]
        wb_ = [AR.alloc([16, WC], BF16) for _ in range(2)]
        Bwa, Bwb = [Buf("wa0"), Buf("wa1")], [Buf("wbb0"), Buf("wbb1")]
        ga = [AR.alloc([512], BF16) for _ in range(2)]
        gb_ = [AR.alloc([512], BF16) for _ in range(2)]
        Bga, Bgb = [Buf("ga0"), Buf("ga1")], [Buf("gb0"), Buf("gb1")]
        tA = [AR.alloc([512], F32) for _ in range(2)]
        tB = [AR.alloc([512], F32) for _ in range(2)]
        BtA, BtB = [Buf("tA0"), Buf("tA1")], [Buf("tB0"), Buf("tB1")]
        ys = [AR.alloc([512], BF16) for _ in range(2)]
        Bys = [Buf("ys0"), Buf("ys1")]
        wav = w_a.rearrange("(kc p) c -> p kc c", p=128)
        wbv = w_b.rearrange("(kc p) c -> p kc c", p=128)
        nblk = D // WC

        def load(i):
            P.dma("pool", wa[i % 2], wav[:, :, i * WC:(i + 1) * WC], writes=[Bwa[i % 2]])
            P.dma("pool", wb_[i % 2], wbv[:, :, i * WC:(i + 1) * WC], writes=[Bwb[i % 2]])

        load(0)
        it = 0
        for i in range(nblk):
            if i + 1 < nblk:
                load(i + 1)
            for cj in range(WC // 128):
                c = i * WC + cj * 128
                for (tk0, ntk) in TOKG:
                    k = it % 2
                    it += 1
                    P.dma("sp", ga[k][:, 0:ntk], gaT[c:c + 128, tk0:tk0 + ntk], reads=[B_gaT], writes=[Bga[k]])
                    P.dma("sp", gb_[k][:, 0:ntk], gbT[c:c + 128, tk0:tk0 + ntk], reads=[B_gbT], writes=[Bgb[k]])
                    bankA, BbA = next_bank()
                    for kc in range(16):
                        P.op("pe", lambda e, bankA=bankA, kc=kc, cj=cj, tk0=tk0, ntk=ntk, i=i:
                             e.matmul(bankA[:, 0:ntk], lhsT=wa[i % 2][:, kc, cj * 128:(cj + 1) * 128],
                                      rhs=aT[:, kc, tk0:tk0 + ntk], start=(kc == 0), stop=(kc == 15)),
                             [Bwa[i % 2], BaT], [BbA])
                    bankB, BbB = next_bank()
                    for kc in range(16):
                        P.op("pe", lambda e, bankB=bankB, kc=kc, cj=cj, tk0=tk0, ntk=ntk, i=i:
                             e.matmul(bankB[:, 0:ntk], lhsT=wb_[i % 2][:, kc, cj * 128:(cj + 1) * 128],
                                      rhs=bT[:, kc, tk0:tk0 + ntk], start=(kc == 0), stop=(kc == 15)),
                             [Bwb[i % 2], BbT], [BbB])
                    P.op("dve", lambda e, k=k, ntk=ntk, bankA=bankA: e.tensor_tensor(
                        out=tA[k][:, 0:ntk], in0=bankA[:, 0:ntk], in1=ga[k][:, 0:ntk], op=ALU.mult),
                        [BbA, Bga[k]], [BtA[k]])
                    P.op("dve", lambda e, k=k, ntk=ntk, bankB=bankB: e.tensor_tensor(
                        out=tB[k][:, 0:ntk], in0=bankB[:, 0:ntk], in1=gb_[k][:, 0:ntk], op=ALU.mult),
                        [BbB, Bgb[k]], [BtB[k]])
                    P.op("pool", lambda e, k=k, ntk=ntk: e.tensor_tensor(
                        out=ys[k][:, 0:ntk], in0=tA[k][:, 0:ntk], in1=tB[k][:, 0:ntk], op=ALU.add),
                        [BtA[k], BtB[k]], [Bys[k]])
                    P.dma("sp", yT[c:c + 128, tk0:tk0 + ntk], ys[k][:, 0:ntk], reads=[Bys[k]], writes=[B_yT])
        P.fence()

    if want("s5"):
        s5()
    if stop == "s5":
        return finish(nc, P, out_bufs)

    def s6():
        AR.reset(base_mark)
        yTs = AR.alloc([32, N], BF16)
        ByTs = Buf("yTs")
        yTv = yT.rearrange("(kc p) n -> p kc n", p=128)
        for q in range(8):
            P.dma("sp", yTs[:, 4 * q:4 * q + 4, :], yTv[:, 4 * q:4 * q + 4, :], reads=[B_yT], writes=[ByTs])
        xs = [AR.alloc([512], F32) for _ in range(2)]
        Bxs = [Buf("xs0"), Buf("xs1")]
        ctr = [0]

        def epi_T(tag, c0, ncols, t, bank, Bb):
            i = ctr[0] % 2
            ctr[0] += 1
            P.dma("sp", xs[i], xin[t * 128:(t + 1) * 128, c0:c0 + ncols], reads=[B_xin], writes=[Bxs[i]])
            P.op("dve", lambda e, i=i, bank=bank: e.tensor_tensor(out=xs[i], in0=bank[:, 0:512], in1=xs[i], op=ALU.add),
                 [Bb, Bxs[i]], [Bxs[i]])
            P.dma("sp", x2[t * 128:(t + 1) * 128, c0:c0 + ncols], xs[i], reads=[Bxs[i]], writes=[B_x2])

        gemm(yTs, ByTs, 32, w_o, [(512 * j, 512, "T", "o") for j in range(8)], epilogue_T=epi_T)
        P.fence()

    if want("s6"):
        s6()
    if stop == "s6":
        return finish(nc, P, out_bufs)

    AR.reset(base_mark)
    hT = AR.alloc([32, N], BF16)
    B_hT = Buf("hT2")
    if want("s7"):
        norm_transpose(x2, B_x2, g_ffn, hT, B_hT)

    def s8():
        m = AR.mark()
        NCH = 2 * DFF // 128
        par = AR.alloc([NCH, 12], F32)
        Bpar = Buf("par")
        Zl = AR.alloc([NCH, 10], F32)
        BZl = Buf("Zl")
        rowsrc = AR.alloc([2048], F32)
        Brow = Buf("rowsrc")
        for p0 in range(0, 2 * DFF, 2048):
            wdt = min(2048, 2 * DFF - p0)
            P.dma("sp", rowsrc[0:8, 0:wdt], cst[:, p0:p0 + wdt], writes=[Brow])
            P.dma("sp", rowsrc[8:11, 0:wdt], conv_w[:, p0:p0 + wdt], writes=[Brow])
            P.dma("sp", rowsrc[11:12, 0:wdt], conv_b[p0:p0 + wdt].unsqueeze(0), writes=[Brow])
            bank, Bb = next_bank()
            nch = wdt // 128
            for j in range(nch):
                P.op("pe", lambda e, bank=bank, j=j: e.matmul(bank[:, j * 12:(j + 1) * 12],
                                                            lhsT=rowsrc[0:12, j * 128:(j + 1) * 128],
                                                            rhs=identf[0:12, 0:12], start=True, stop=True),
                     [Brow, B_ident], [Bb])
            c0 = p0 // 128
            P.op("dve", lambda e, bank=bank, c0=c0, nch=nch: e.tensor_copy(
                out=par[:, c0:c0 + nch, :], in_=bank[:, 0:nch * 12].rearrange("p (a b) -> p a b", a=nch)),
                [Bb], [Bpar])
        WC = 256
        wb = [AR.alloc([32, WC], BF16) for _ in range(2)]
        Bwb = [Buf("wu0"), Buf("wu1")]
        zb = {k: [AR.alloc([516], F32) for _ in range(2)] for k in ("g", "u")}
        Bzb = {k: [Buf("z%s0" % k), Buf("z%s1" % k)] for k in ("g", "u")}
        cg = AR.alloc([512], F32)
        cu = AR.alloc([512], F32)
        Bcg, Bcu = Buf("cg"), Buf("cu")
        ast = [AR.alloc([512], BF16) for _ in range(2)]
        Bast = [Buf("ast0"), Buf("ast1")]
        wv = w_up.rearrange("(kc p) c -> p kc c", p=128)
        nblk = DFF // 128

        def load(i):
            P.dma("pool", wb[i % 2][:, :, 0:128], wv[:, :, i * 128:(i + 1) * 128], writes=[Bwb[i % 2]])
            P.dma("pool", wb[i % 2][:, :, 128:256], wv[:, :, DFF + i * 128:DFF + (i + 1) * 128], writes=[Bwb[i % 2]])

        load(0)
        it = 0
        for i in range(nblk):
            if i + 1 < nblk:
                load(i + 1)
            w_i, Bw = wb[i % 2], Bwb[i % 2]
            for gi, (tk0, ntk) in enumerate(TOKG):
                bk = {}
                for (k, off) in (("g", 0), ("u", 128)):
                    bank, Bb = next_bank()
                    bk[k] = (bank, Bb)
                    for kc in range(32):
                        P.op("pe", lambda e, bank=bank, kc=kc, off=off, tk0=tk0, ntk=ntk, w_i=w_i:
                             e.matmul(bank[:, 0:ntk], lhsT=w_i[:, kc, off:off + 128], rhs=hT[:, kc, tk0:tk0 + ntk],
                                      start=(kc == 0), stop=(kc == 31)), [Bw, B_hT], [Bb])
                zi = gi % 2
                outc = {}
                for (k, cacc, Bc, ch) in (("g", cg, Bcg, i), ("u", cu, Bcu, nblk + i)):
                    bank, Bb = bk[k]
                    z, Bz = zb[k][zi], Bzb[k][zi]
                    zp, Bzp = zb[k][1 - zi], Bzb[k][1 - zi]
                    w0, w1, w2, bia = par[:, ch, 8:9], par[:, ch, 9:10], par[:, ch, 10:11], par[:, ch, 11:12]
                    sample = (gi == 4)
                    if not sample:
                        P.op("act", lambda e, z=z, bank=bank, ntk=ntk: e.activation(out=z[:, 2:2 + ntk], in_=bank[:, 0:ntk],
                                                                                  func=AF.Copy), [Bb], [Bz])
                        if gi == 0:
                            P.op("pool", lambda e, z=z: e.memset(z[:, 0:2], 0.0), [], [Bz])
                        else:
                            P.op("pool", lambda e, z=z, zp=zp: e.tensor_copy(out=z[:, 0:2], in_=zp[:, 512:514]),
                                 [Bzp], [Bz])
                        P.op("act", lambda e, cacc=cacc, bank=bank, ntk=ntk, w2=w2, bia=bia: e.activation(
                            out=cacc[:, 0:ntk], in_=bank[:, 0:ntk], func=AF.Identity, scale=w2, bias=bia),
                            [Bb, Bpar], [Bc])
                        P.op("dve", lambda e, cacc=cacc, z=z, ntk=ntk, w1=w1: e.scalar_tensor_tensor(
                            out=cacc[:, 0:ntk], in0=z[:, 1:1 + ntk], scalar=w1, in1=cacc[:, 0:ntk],
                            op0=ALU.mult, op1=ALU.add), [Bz, Bc, Bpar], [Bc])
                        P.op("dve", lambda e, cacc=cacc, z=z, ntk=ntk, w0=w0: e.scalar_tensor_tensor(
                            out=cacc[:, 0:ntk], in0=z[:, 0:ntk], scalar=w0, in1=cacc[:, 0:ntk],
                            op0=ALU.mult, op1=ALU.add), [Bz, Bc, Bpar], [Bc])
                        if gi == 3:
                            P.op("pool", lambda e, z=z, ch=ch: e.tensor_copy(out=Zl[:, ch, 0:2], in_=z[:, 512:514]),
                                 [Bz], [BZl])
                    else:
                        z3 = z[:, 0:136].rearrange("p (s c) -> p s c", s=4)
                        P.op("act", lambda e, z3=z3, bank=bank: e.activation(
                            out=z3[:, :, 2:34], in_=bank[:, 0:128].rearrange("p (s c) -> p s c", s=4), func=AF.Copy),
                            [Bb], [Bz])
                        P.op("pool", lambda e, z3=z3, ch=ch: e.tensor_copy(
                            out=z3[:, :, 0:2], in_=par[:, ch, 0:8].rearrange("p (s r) -> p s r", s=4)), [Bpar], [Bz])
                        c3 = cacc[:, 0:128].rearrange("p (s c) -> p s c", s=4)
                        P.op("act", lambda e, cacc=cacc, bank=bank, w2=w2, bia=bia: e.activation(
                            out=cacc[:, 0:128], in_=bank[:, 0:128], func=AF.Identity, scale=w2, bias=bia),
                            [Bb, Bpar], [Bc])
                        P.op("dve", lambda e, c3=c3, z3=z3, w1=w1: e.scalar_tensor_tensor(
                            out=c3, in0=z3[:, :, 1:33], scalar=w1, in1=c3, op0=ALU.mult, op1=ALU.add),
                            [Bz, Bc, Bpar], [Bc])
                        P.op("dve", lambda e, c3=c3, z3=z3, w0=w0: e.scalar_tensor_tensor(
                            out=c3, in0=z3[:, :, 0:32], scalar=w0, in1=c3, op0=ALU.mult, op1=ALU.add),
                            [Bz, Bc, Bpar], [Bc])
                        P.op("pool", lambda e, z3=z3, ch=ch: e.tensor_copy(
                            out=Zl[:, ch, 2:10].rearrange("p (s r) -> p s r", s=4), in_=z3[:, :, 32:34]), [Bz], [BZl])
                P.op("act", lambda e, ntk=ntk: e.activation(out=cg[:, 0:ntk], in_=cg[:, 0:ntk], func=AF.Silu),
                     [Bcg], [Bcg])
                k = it % 2
                it += 1
                P.op("pool", lambda e, k=k, ntk=ntk: e.tensor_tensor(out=ast[k][:, 0:ntk], in0=cg[:, 0:ntk],
                                                                     in1=cu[:, 0:ntk], op=ALU.mult),
                     [Bcg, Bcu], [Bast[k]])
                P.dma("sp", actT[i * 128:(i + 1) * 128, tk0:tk0 + ntk], ast[k][:, 0:ntk], reads=[Bast[k]],
                      writes=[B_actT])
        zo = [rowsrc[:, 0:512], rowsrc[:, 512:1024]]
        Bzo = [Brow, Brow]
        for q in range(NCH // 4):
            bank, Bb = next_bank()
            for j in range(4):
                ch = q * 4 + j
                P.op("pe", lambda e, bank=bank, j=j, ch=ch: e.matmul(bank[0:10, j * 128:(j + 1) * 128], lhsT=Zl[:, ch, :],
                                                                   rhs=identf, start=True, stop=True),
                     [BZl, B_ident], [Bb])
            P.op("dve", lambda e, bank=bank, q=q: e.tensor_copy(out=zo[q % 2][0:10, :], in_=bank[0:10, :]),
                 [Bb], [Bzo[q % 2]])
            ob = Buf("ocst")
            out_bufs.append(ob)
            P.dma("sp", o_cst[:, q * 512:(q + 1) * 512], zo[q % 2][0:10, :], reads=[Bzo[q % 2]], writes=[ob])
        P.fence()
        AR.reset(m)

    if want("s8"):
        s8()
    if stop == "s8":
        return finish(nc, P, out_bufs)

    def s9():
        AR.reset(base_mark)
        KC = DFF // 128
        TB = [(0, 768), (768, 768), (1536, 640)]
        acT = AR.alloc([KC, 768], BF16)
        BacTg = [Buf("acT%d" % g) for g in range((KC + 7) // 8)]
        wb = [AR.alloc([KC, 128], BF16) for _ in range(2)]
        Bwb = [Buf("wd0"), Buf("wd1")]
        fT = [AR.alloc([384], BF16) for _ in range(2)]
        BfT = [Buf("fT0"), Buf("fT1")]
        fst = [AR.alloc([6, 512], F32) for _ in range(2)]
        Bfst = [Buf("fst0"), Buf("fst1")]
        acv = actT.rearrange("(kc p) n -> p kc n", p=128)
        wv = w_down.rearrange("(kc p) c -> p kc c", p=128)
        li = [0]

        def load():
            i = li[0]
            c = i % 32
            if i < 32:
                P.dma("pool", wb[i % 2], wv[:, :, c * 128:(c + 1) * 128], writes=[Bwb[i % 2]])
                P.dma("sp", wdc[c], wb[i % 2], reads=[Bwb[i % 2]], writes=[Bwdc[c]])
            else:
                P.dma("sp", wb[i % 2], wdc[c], reads=[Bwdc[c]], writes=[Bwb[i % 2]])
            li[0] += 1

        ci = 0
        fi = 0
        load()
        for bi_, (tb0, ntb) in enumerate(TB):
            for q0 in range(0, KC, 8):
                q1 = min(KC, q0 + 8)
                P.dma("sp", acT[:, q0:q1, 0:ntb], acv[:, q0:q1, tb0:tb0 + ntb], reads=[B_actT], writes=[BacTg[q0 // 8]])
            ntl = ntb // 128
            for c in range(32):
                if li[0] < 32 * len(TB):
                    load()
                w_i, Bw = wb[ci % 2], Bwb[ci % 2]
                ci += 1
                f_s, Bf = fst[(c // 4 + 8 * bi_) % 2], Bfst[(c // 4 + 8 * bi_) % 2]
                for s0 in range(0, ntb, 384):
                    ns = min(384, ntb - s0)
                    bank, Bb = next_bank()
                    for kc in range(KC):
                        P.op("pe", lambda e, bank=bank, kc=kc, s0=s0, ns=ns, w_i=w_i:
                             e.matmul(bank[:, 0:ns], lhsT=w_i[:, kc, :], rhs=acT[:, kc, s0:s0 + ns],
                                      start=(kc == 0), stop=(kc == KC - 1)), [Bw, BacTg[kc // 8]], [Bb])
                    k = fi % 2
                    fi += 1
                    P.op("act", lambda e, k=k, bank=bank, ns=ns: e.activation(out=fT[k][:, 0:ns], in_=bank[:, 0:ns],
                                                                            func=AF.Copy), [Bb], [BfT[k]])
                    bank2, Bb2 = next_bank()
                    pb = bank2[:].bitcast(BF16)
                    for j in range(ns // 128):
                        P.op("pe", lambda e, pb=pb, j=j, k=k: e.transpose(out=pb[:, j * 128:(j + 1) * 128],
                                                                        in_=fT[k][:, j * 128:(j + 1) * 128],
                                                                        identity=identb), [BfT[k], B_ident], [Bb2])
                    tl0 = s0 // 128
                    nj = ns // 128
                    P.op("dve", lambda e, pb=pb, f_s=f_s, tl0=tl0, nj=nj, c=c: e.tensor_copy(
                        out=f_s[:, tl0:tl0 + nj, (c % 4) * 128:(c % 4) * 128 + 128],
                        in_=pb[:, 0:nj * 128].rearrange("p (a b) -> p a b", a=nj)), [Bb2], [Bf])
                if c % 4 == 3:
                    cb = c // 4
                    P.dma("sp", fsc[tb0:tb0 + ntb, cb * 512:(cb + 1) * 512].rearrange("(j p) c -> p j c", p=128),
                          f_s[:, 0:ntl, :], reads=[Bf], writes=[B_fsc])
        P.fence()

    if want("s9"):
        s9()
    if stop == "s9":
        return finish(nc, P, out_bufs)

    def s10():
        AR.reset(base_mark)
        g_bc = AR.alloc([D], F32)
        Bg = Buf("gfin")
        P.dma("sp", g_bc, g_final.partition_broadcast(128), writes=[Bg])
        NB10 = 3
        xa = [AR.alloc([D], F32) for _ in range(NB10)]
        fa = [AR.alloc([D], F32) for _ in range(NB10)]
        Bxa, Bfa = [Buf("xa%d" % i) for i in range(NB10)], [Buf("fa%d" % i) for i in range(NB10)]
        junk = AR.alloc([D], BF16)
        Bjunk = Buf("junk10")
        st = AR.alloc([3 * NT], F32)
        Bst = [Buf("st10_%d" % t) for t in range(NT)]

        def loads(t):
            P.dma("sp", xa[t % NB10], x2[t * 128:(t + 1) * 128, :], reads=[B_x2], writes=[Bxa[t % NB10]])
            P.dma("sp", fa[t % NB10], fsc[t * 128:(t + 1) * 128, :], reads=[B_fsc], writes=[Bfa[t % NB10]])

        loads(0)
        loads(1)
        for t in range(NT):
            if t + 2 < NT:
                loads(t + 2)
            x_t, Bx, f_t, Bf = xa[t % NB10], Bxa[t % NB10], fa[t % NB10], Bfa[t % NB10]
            P.op("pool", lambda e, x_t=x_t, f_t=f_t: e.tensor_tensor(out=x_t, in0=x_t, in1=f_t, op=ALU.add),
                 [Bx, Bf], [Bx])
            ss, sd, rs = st[:, 3 * t:3 * t + 1], st[:, 3 * t + 1:3 * t + 2], st[:, 3 * t + 2:3 * t + 3]
            P.op("act", lambda e, x_t=x_t, ss=ss: e.activation(out=junk, in_=x_t, func=AF.Square, accum_out=ss),
                 [Bx], [Bjunk, Bst[t]])
            P.op("act", lambda e, ss=ss, sd=sd: e.activation(out=sd, in_=ss, func=AF.Sqrt, scale=1.0 / D, bias=EPS),
                 [Bst[t]], [Bst[t]])
            P.op("dve", lambda e, sd=sd, rs=rs: e.reciprocal(out=rs, in_=sd), [Bst[t]], [Bst[t]])
            P.op("dve", lambda e, x_t=x_t, f_t=f_t, rs=rs: e.scalar_tensor_tensor(out=f_t, in0=x_t, scalar=rs, in1=g_bc,
                                                                              op0=ALU.mult, op1=ALU.mult),
                 [Bx, Bst[t], Bg, Bf], [Bf])
            ob = Buf("oy")
            out_bufs.append(ob)
            P.dma("sp", o_y[t * 128:(t + 1) * 128, :], f_t, reads=[Bf], writes=[ob])

    if want("s10"):
        s10()
    return finish(nc, P, out_bufs)


def finish(nc, P, out_bufs):
    P.fence()
    P.emit()
    return nc


def _rope_table(pos, half):
    inv = (np.float32(THETA) ** (-(np.arange(half, dtype=np.float32)) / np.float32(half))).astype(np.float32)
    ang = pos.astype(np.float32)[:, None] * inv[None, :]
    return np.concatenate([np.cos(ang), np.sin(ang)], axis=1).astype(np.float32)


def _consts():
    pos = np.concatenate([np.arange(NP), np.tile(PAST + np.arange(32), 4)]).astype(np.int32)
    bd = np.kron(np.eye(4, dtype=np.float32), np.ones((32, 32), np.float32))
    return {
        "c_ident": np.eye(128, dtype=np.float32),
        "c_csq": _rope_table(pos, 16),
        "c_csi": _rope_table(pos, 8),
        "c_bd": bd,
    }


def make_in_maps(inp, cores=range(8)):
    f = lambda a: np.ascontiguousarray(np.asarray(a, dtype=np.float32))
    shared = {
        "g_attn": f(inp["norm_attn_g"][0]), "w_in": f(inp["w_in"][0]), "g_gmlp": f(inp["gmlp_norm_g"][0]),
        "ws": f(inp["gmlp_ws"][0]), "gbias": f(inp["gmlp_b"][0]), "w_a": f(inp["w_branch_a"][0]),
        "w_b": f(inp["w_branch_b"][0]), "w_o": f(inp["w_out"][0]), "g_ffn": f(inp["norm_ffn_g"][0]),
        "w_up": f(inp["w_up"][0]), "conv_w": f(inp["conv_w"][0]), "conv_b": f(inp["conv_b"][0]),
        "w_down": f(inp["w_down"][0]), "g_final": f(inp["norm_final_g"]),
    }
    shared.update(_consts())
    maps = []
    for c in cores:
        sl = slice(4 * c, 4 * c + 4)
        xs = np.asarray(inp["x_sample"][sl], dtype=np.float32).reshape(NS, D)
        m = dict(shared)
        m["xin"] = np.ascontiguousarray(np.concatenate([np.asarray(inp["x_prompt"][c], dtype=np.float32), xs], axis=0))
        m["ck"] = f(np.asarray(inp["cache_k"][0, sl]).reshape(4, PAST, 512))
        m["cv"] = f(np.asarray(inp["cache_v"][0, sl]).reshape(4, PAST, 512))
        m["cki"] = f(inp["cache_kidx"][0, sl])
        m["cst"] = f(np.asarray(inp["state_ffn_conv"][0, sl]).reshape(8, 2 * DFF))
        maps.append(m)
    return maps


_NC_CACHE = {}


def kernel(**inputs):
    if "nc" not in _NC_CACHE:
        _NC_CACHE["nc"] = build_program()
    nc = _NC_CACHE["nc"]
    maps = make_in_maps(inputs)
    res = run_bass_kernel_spmd(nc, maps, core_ids=list(range(8)))
    R = res.results
    cat = lambda name: [np.asarray(r[name], dtype=np.float32) for r in R]
    y = cat("o_y"); k = cat("o_k"); v = cat("o_v"); ki = cat("o_ki"); cs = cat("o_cst"); vn = cat("o_vn")
    y_prompt = np.stack([a[:NP] for a in y])
    y_sample = np.concatenate([a[NP:].reshape(4, 32, D) for a in y])
    kp = np.stack([a[:NP].reshape(NP, 4, 128) for a in k])[None]
    vp = np.stack([a[:NP].reshape(NP, 4, 128) for a in v])[None]
    kip = np.stack([a[:NP] for a in ki])[None]
    csp = np.stack([a[0:2] for a in cs])[None]
    ks = np.concatenate([a[NP:].reshape(4, 32, 4, 128) for a in k])[None]
    vs = np.concatenate([a[NP:].reshape(4, 32, 4, 128) for a in v])[None]
    kis = np.concatenate([a[NP:].reshape(4, 32, IDD) for a in ki])[None]
    css = np.concatenate([a[2:10].reshape(4, 2, 2 * DFF) for a in cs])[None]
    gv = np.concatenate([a.reshape(4, 32, DA) for a in vn])[None]
    return (y_prompt, y_sample, kp, vp, kip, csp, ks, vs, kis, css, gv)
```

```python
import numpy as np
import ml_dtypes
from contextlib import ExitStack
import concourse.bass as bass
import concourse.mybir as mybir
from concourse.bass_utils import run_bass_kernel_spmd

F32 = mybir.dt.float32
BF16 = mybir.dt.bfloat16
ALU = mybir.AluOpType
AF = mybir.ActivationFunctionType
AX = mybir.AxisListType

ENGS = ("pe", "act", "dve", "pool", "sp")
NDMASEM = 16


class Buf:
    __slots__ = ("name", "w", "r", "rd")

    def __init__(self, name=""):
        self.name = name
        self.w = None
        self.r = {}
        self.rd = []


class Op:
    __slots__ = ("eng", "fn", "deps", "sig", "sem", "val", "dma")

    def __init__(self, eng, fn, dma):
        self.eng = eng
        self.fn = fn
        self.dma = dma
        self.deps = []
        self.sig = dma
        self.sem = None
        self.val = 0


class Prog:
    def __init__(self, nc):
        self.nc = nc
        self.q = {e: [] for e in ENGS}
        self.pend_dma = []

    def op(self, eng, fn, reads=(), writes=(), dma=False):
        o = Op(eng, fn, dma)
        deps = {}
        for b in reads:
            if b.w is not None:
                deps[id(b.w)] = b.w
        for b in writes:
            if b.w is not None:
                deps[id(b.w)] = b.w
            for d in b.r.values():
                deps[id(d)] = d
            for d in b.rd:
                deps[id(d)] = d
        for d in deps.values():
            if d is o:
                continue
            if eng == "pe" and d.eng == "pe" and not d.dma and not dma:
                continue
            d.sig = True
            o.deps.append(d)
        for b in reads:
            if dma:
                b.rd.append(o)
            else:
                b.r[eng] = o
        for b in writes:
            b.w = o
            b.r = {}
            b.rd = []
        self.q[eng].append(o)
        if dma:
            self.pend_dma.append(o)
        return o

    def dma(self, q, out, in_, reads=(), writes=(), **kw):
        return self.op(q, lambda e: e.dma_start(out=out, in_=in_, **kw), reads, writes, dma=True)

    def fence(self):
        lasts = list(self.pend_dma)
        self.pend_dma = []
        for e in ENGS:
            for o in reversed(self.q[e]):
                if o.fn is not None and not o.dma:
                    o.sig = True
                    lasts.append(o)
                    break
        for e in ENGS:
            b = Op(e, None, False)
            b.deps = list(lasts)
            self.q[e].append(b)

    def emit(self):
        nc = self.nc
        with ExitStack() as st:
            esem = {e: st.enter_context(nc.semaphore("s_" + e)) for e in ENGS}
            dsem = {e: [st.enter_context(nc.semaphore("d_%s%d" % (e, i))) for i in range(NDMASEM)]
                    for e in ("sp", "pool", "act")}
            for e in ENGS:
                cnt = 0
                dcnt = [0] * NDMASEM
                di = 0
                for o in self.q[e]:
                    if o.dma:
                        k = di % NDMASEM
                        di += 1
                        dcnt[k] += 16
                        o.sem = dsem[e][k]
                        o.val = dcnt[k]
                    elif o.sig:
                        cnt += 1
                        o.sem = esem[e]
                        o.val = cnt
            block = st.enter_context(nc.Block())

            def run(eng_obj, e):
                waited = {}
                for o in self.q[e]:
                    if o.fn is None:
                        best = {}
                        for d in o.deps:
                            if id(d.sem) not in best or best[id(d.sem)].val < d.val:
                                best[id(d.sem)] = d
                        o.deps = list(best.values())
                    for d in o.deps:
                        key = id(d.sem)
                        if waited.get(key, 0) < d.val:
                            eng_obj.wait_ge(d.sem, d.val)
                            waited[key] = d.val
                    if o.fn is None:
                        continue
                    if o.dma and o.val > 16 and waited.get(id(o.sem), 0) < o.val - 16:
                        eng_obj.wait_ge(o.sem, o.val - 16)
                        waited[id(o.sem)] = o.val - 16
                    ins = o.fn(eng_obj)
                    if o.dma:
                        ins.then_inc(o.sem, 16)
                    elif o.sig:
                        ins.then_inc(o.sem, 1)

            @block.tensor
            def _(t):
                run(t, "pe")

            @block.scalar
            def _(a):
                run(a, "act")

            @block.vector
            def _(v):
                run(v, "dve")

            @block.gpsimd
            def _(g):
                run(g, "pool")

            @block.sync
            def _(s):
                run(s, "sp")


D = 4096
NP = 2048
NS = 128
N = NP + NS
NT = N // 128
DA = 2048
DFF = 11008
NH = 16
NKV = 4
HD = 128
NIH = 16
IDD = 64
PAST = 2048
TOPK = 256
EPS = 1e-6
THETA = 500000.0
U0, VA0, Q0, K0, V0, QI0, KI0, WI0, GA0, GB0, INC = 0, 2048, 4096, 6144, 6656, 7168, 8192, 8256, 8272, 12368, 16464
PJ_VA, PJ_Q, PJ_K, PJ_V, PJ_QI, PJ_KI, PJ_WI, PJW = 0, 2048, 4096, 4608, 5120, 6144, 6208, 6224
ARENA_F32 = 52992
NEG = -1.0e30
TOKG = [(0, 512), (512, 512), (1024, 512), (1536, 512), (2048, 128)]


class Arena:
    def __init__(self, ap):
        self.ap = ap
        self.off = 0

    def mark(self):
        return self.off

    def reset(self, m=0):
        self.off = m

    def alloc(self, shape_free, dtype):
        n = 1
        for s in shape_free:
            n *= s
        bpe = 2 if dtype == BF16 else 4
        nbytes = (n * bpe + 31) // 32 * 32
        assert self.off + nbytes <= ARENA_F32 * 4, ("arena overflow", self.off, nbytes)
        a = self.ap[:, self.off // 4:(self.off + nbytes) // 4]
        self.off += nbytes
        if dtype == BF16:
            a = a.bitcast(BF16)
        a = a[:, 0:n]
        if len(shape_free) == 2:
            a = a.rearrange("p (a b) -> p a b", a=shape_free[0])
        elif len(shape_free) == 3:
            a = a.rearrange("p (a b c) -> p a b c", a=shape_free[0], b=shape_free[1])
        return a


def build_program(dbg=None):
    dbg = dbg or {}
    stop = dbg.get("stop")
    dbg_outs = set(dbg.get("outs", ()))
    only = dbg.get("only")
    cut = dbg.get("cut", 99)

    def want(name):
        return only is None or name in only
    nc = bass.Bass("TRN2", target_bir_lowering=False)
    P = Prog(nc)

    def din(name, shape, dt=F32):
        return nc.dram_tensor(name, list(shape), dt, kind="ExternalInput").ap()

    def dout(name, shape, dt=F32):
        return nc.dram_tensor(name, list(shape), dt, kind="ExternalOutput").ap()

    def dscr(name, shape, dt):
        kind = "ExternalOutput" if name in dbg_outs else "Internal"
        return nc.dram_tensor(name, list(shape), dt, kind=kind).ap()

    xin = din("xin", [N, D])
    ck = din("ck", [4, PAST, 512])
    cv = din("cv", [4, PAST, 512])
    cki = din("cki", [4, PAST, IDD])
    cst = din("cst", [8, 2 * DFF])
    g_attn = din("g_attn", [D])
    w_in = din("w_in", [D, INC])
    g_gmlp = din("g_gmlp", [DA])
    ws = din("ws", [8, 128, 128])
    gbias = din("gbias", [8, 128])
    w_a = din("w_a", [DA, D])
    w_b = din("w_b", [DA, D])
    w_o = din("w_o", [D, D])
    g_ffn = din("g_ffn", [D])
    w_up = din("w_up", [D, 2 * DFF])
    conv_w = din("conv_w", [3, 2 * DFF])
    conv_b = din("conv_b", [2 * DFF])
    w_down = din("w_down", [DFF, D])
    g_final = din("g_final", [D])
    c_ident = din("c_ident", [128, 128])
    c_csq = din("c_csq", [N, 32])
    c_csi = din("c_csi", [N, 16])
    c_bd = din("c_bd", [128, 128])
    o_y = dout("o_y", [N, D])
    o_k = dout("o_k", [N, 512])
    o_v = dout("o_v", [N, 512])
    o_ki = dout("o_ki", [N, IDD])
    o_cst = dout("o_cst", [10, 2 * DFF])
    o_vn = dout("o_vn", [NS, DA])
    out_bufs = []
    projT = dscr("projT", [N, PJW], F32)
    uT = dscr("uT", [DA, N], BF16)
    gaT = dscr("gaT", [D, N], BF16)
    gbT = dscr("gbT", [D, N], BF16)
    yT = dscr("yT", [D, N], BF16)
    x2 = dscr("x2", [N, D], F32)
    actT = dscr("actT", [DFF, N], BF16)
    fsc = dscr("fsc", [N, D], F32)
    wdc_t = dscr("wdc", [32, 128, (DFF // 128) * 128], BF16)
    wdc = [wdc_t[c].rearrange("p (k j) -> p k j", j=128) for c in range(32)]
    Bwdc = [Buf("wdc%d" % c) for c in range(32)]
    aTd = dscr("aTd", [DA, N], BF16)
    bTd = dscr("bTd", [DA, N], BF16)
    B_projT, B_uT, B_gaT, B_gbT, B_yT, B_x2, B_actT, B_fsc, B_aTd, B_bTd = [Buf(n) for n in
        ("projT", "uT", "gaT", "gbT", "yT", "x2", "actT", "fsc", "aTd", "bTd")]

    arena_t = nc.alloc_sbuf_tensor("arena", [128, ARENA_F32], F32)
    AR = Arena(arena_t[:])
    banks = [nc.alloc_psum_tensor("bank%d" % i, [128, 512], F32) for i in range(8)]
    Bbank = [Buf("bank%d" % i) for i in range(8)]
    bankctr = [0]

    def next_bank():
        i = bankctr[0] % 8
        bankctr[0] += 1
        return banks[i], Bbank[i]

    evctr = [0]

    def evac_eng():
        evctr[0] += 1
        return "act" if evctr[0] % 2 else "dve"

    def copy_op(eng, out, in_):
        if eng == "act":
            return lambda e: e.activation(out=out, in_=in_, func=AF.Copy)
        return lambda e: e.tensor_copy(out=out, in_=in_)

    identf = AR.alloc([128], F32)
    identb = AR.alloc([128], BF16)
    B_ident = Buf("ident")
    P.dma("sp", identf, c_ident[:, :], writes=[B_ident])
    P.op("dve", lambda e: e.tensor_copy(out=identb, in_=identf), [B_ident], [B_ident])
    base_mark = AR.mark()

    def norm_transpose(src, Bsrc, gvec, hT, B_hT):
        m = AR.mark()
        g_bc = AR.alloc([D], F32)
        xt = [AR.alloc([D], F32) for _ in range(2)]
        hb = AR.alloc([D], BF16)
        junk = AR.alloc([D], BF16)
        st = AR.alloc([3 * NT], F32)
        Bg, Bxt, Bhb, Bjunk = Buf("g"), [Buf("xt0"), Buf("xt1")], Buf("hb"), Buf("junk")
        Bst = [Buf("st%d" % t) for t in range(NT)]
        P.dma("sp", g_bc, gvec.partition_broadcast(128), writes=[Bg])
        for t in range(NT):
            x_t, Bx = xt[t % 2], Bxt[t % 2]
            P.dma("sp", x_t, src[t * 128:(t + 1) * 128, :], reads=[Bsrc], writes=[Bx])
            ss, sd, rs = st[:, 3 * t:3 * t + 1], st[:, 3 * t + 1:3 * t + 2], st[:, 3 * t + 2:3 * t + 3]
            P.op("act", lambda e, x_t=x_t, ss=ss: e.activation(out=junk, in_=x_t, func=AF.Square, accum_out=ss),
                 [Bx], [Bjunk, Bst[t]])
            P.op("act", lambda e, ss=ss, sd=sd: e.activation(out=sd, in_=ss, func=AF.Sqrt, scale=1.0 / D, bias=EPS),
                 [Bst[t]], [Bst[t]])
            P.op("dve", lambda e, sd=sd, rs=rs: e.reciprocal(out=rs, in_=sd), [Bst[t]], [Bst[t]])
            P.op("dve", lambda e, x_t=x_t, rs=rs: e.scalar_tensor_tensor(out=hb, in0=x_t, scalar=rs, in1=g_bc,
                                                                     op0=ALU.mult, op1=ALU.mult),
                 [Bx, Bst[t], Bg], [Bhb])
            for q4 in range(4):
                bank, Bb = next_bank()
                pb = bank[:].bitcast(BF16)
                for j in range(8):
                    kc = q4 * 8 + j
                    P.op("pe", lambda e, pb=pb, j=j, kc=kc: e.transpose(out=pb[:, j * 128:(j + 1) * 128],
                                                                      in_=hb[:, kc * 128:(kc + 1) * 128],
                                                                      identity=identb),
                         [Bhb, B_ident], [Bb])
                eng = evac_eng()
                o_ap = hT[:, q4 * 8:(q4 + 1) * 8, t * 128:(t + 1) * 128]
                i_ap = pb.rearrange("p (a b) -> p a b", a=8)
                P.op(eng, copy_op(eng, o_ap, i_ap), [Bb], [B_hT])
        P.fence()
        AR.reset(m)

    def gemm(xT, B_xT, KC, W, blocks, epilogue_T=None, epilogue_F=None, tok_groups=TOKG, tiles=range(NT),
             wcols=512):
        m = AR.mark()
        wb = [AR.alloc([KC, wcols], BF16) for _ in range(2)]
        Bwb = [Buf("wb0"), Buf("wb1")]
        Wv = W.rearrange("(kc p) c -> p kc c", p=128)

        def load(i):
            c0, ncols, mode, tag = blocks[i]
            P.dma("pool", wb[i % 2][:, :, 0:ncols], Wv[:, :, c0:c0 + ncols], writes=[Bwb[i % 2]])

        load(0)
        for i, (c0, ncols, mode, tag) in enumerate(blocks):
            if i + 1 < len(blocks):
                load(i + 1)
            w_i, Bw = wb[i % 2], Bwb[i % 2]
            if mode == "T":
                for t in tiles:
                    bank, Bb = next_bank()
                    for kc in range(KC):
                        P.op("pe", lambda e, bank=bank, kc=kc, t=t, w_i=w_i, ncols=ncols:
                             e.matmul(bank[:, 0:ncols], lhsT=xT[:, kc, t * 128:(t + 1) * 128],
                                      rhs=w_i[:, kc, 0:ncols], start=(kc == 0), stop=(kc == KC - 1)),
                             [B_xT, Bw], [Bb])
                    epilogue_T(tag, c0, ncols, t, bank, Bb)
            else:
                for cj in range(ncols // 128):
                    for (tk0, ntk) in tok_groups:
                        bank, Bb = next_bank()
                        for kc in range(KC):
                            P.op("pe", lambda e, bank=bank, kc=kc, cj=cj, tk0=tk0, ntk=ntk, w_i=w_i:
                                 e.matmul(bank[:, 0:ntk], lhsT=w_i[:, kc, cj * 128:(cj + 1) * 128],
                                          rhs=xT[:, kc, tk0:tk0 + ntk], start=(kc == 0), stop=(kc == KC - 1)),
                                 [B_xT, Bw], [Bb])
                        epilogue_F(tag, c0 + cj * 128, tk0, ntk, bank, Bb)
        AR.reset(m)

    hT = AR.alloc([32, N], BF16)
    B_hT = Buf("hT")
    B_xin = Buf("xin")
    if want("s1"):
        norm_transpose(xin, B_xin, g_attn, hT, B_hT)

    def s2():
        m = AR.mark()
        stg = [AR.alloc([512], F32) for _ in range(2)]
        stgb = [AR.alloc([512], BF16) for _ in range(2)]
        Bstg = [Buf("stg0"), Buf("stg1")]
        Bstgb = [Buf("stgb0"), Buf("stgb1")]
        ctr = [0, 0]

        def epi_T(tag, c0, ncols, t, bank, Bb):
            i = ctr[0] % 2
            ctr[0] += 1
            eng = evac_eng()
            P.op(eng, copy_op(eng, stg[i][:, 0:ncols], bank[:, 0:ncols]), [Bb], [Bstg[i]])
            pc = c0 - VA0
            P.dma("sp", projT[t * 128:(t + 1) * 128, pc:pc + ncols], stg[i][:, 0:ncols], reads=[Bstg[i]],
                  writes=[B_projT])

        def epi_F(tag, c, tk0, ntk, bank, Bb):
            i = ctr[1] % 2
            ctr[1] += 1
            if tag == "u":
                eng = evac_eng()
                P.op(eng, copy_op(eng, stgb[i][:, 0:ntk], bank[:, 0:ntk]), [Bb], [Bstgb[i]])
                dst, Bd, r0 = uT, B_uT, c - U0
            else:
                P.op("act", lambda e, i=i, ntk=ntk, bank=bank: e.activation(out=stgb[i][:, 0:ntk], in_=bank[:, 0:ntk],
                                                                          func=AF.Sigmoid), [Bb], [Bstgb[i]])
                if tag == "ga":
                    dst, Bd, r0 = gaT, B_gaT, c - GA0
                else:
                    dst, Bd, r0 = gbT, B_gbT, c - GB0
            P.dma("sp", dst[r0:r0 + 128, tk0:tk0 + ntk], stgb[i][:, 0:ntk], reads=[Bstgb[i]], writes=[Bd])

        blocks = []
        for j in range(4):
            blocks.append((VA0 + 512 * j, 512, "T", "va"))
        for j in range(8):
            blocks.append((Q0 + 512 * j, 512, "T", "qkv"))
        blocks.append((KI0, 80, "T", "kiwi"))
        for j in range(4):
            blocks.append((U0 + 512 * j, 512, "F", "u"))
        for j in range(8):
            blocks.append((GA0 + 512 * j, 512, "F", "ga"))
        for j in range(8):
            blocks.append((GB0 + 512 * j, 512, "F", "gb"))
        gemm(hT, B_hT, 32, w_in, blocks, epi_T, epi_F)
        P.fence()
        AR.reset(m)

    if want("s2"):
        s2()
    if stop == "s2":
        return finish(nc, P, out_bufs)

    def bias4(bias_bc, bi, q4):
        b = bias_bc[:, bi, q4 * 256:(q4 + 1) * 256].rearrange("p (g i) -> p g i", g=2)
        return b.unsqueeze(2).to_broadcast([128, 2, 2, 128])

    def s3():
        AR.reset(base_mark)
        gn_bc = AR.alloc([DA], F32)
        Bgn = Buf("gn")
        P.dma("sp", gn_bc, g_gmlp.partition_broadcast(128), writes=[Bgn])
        wn = AR.alloc([8, 128], F32)
        wns = AR.alloc([8, 128], F32)
        wnb = AR.alloc([8, 128], BF16)
        wnsb = AR.alloc([8, 128], BF16)
        wmT = AR.alloc([8, 128], BF16)
        wmTs = AR.alloc([8, 128], BF16)
        bias_bc = AR.alloc([2, 1024], F32)
        stmp = AR.alloc([512], F32)
        Bstmp = Buf("stmp")
        Bwn, Bwns, Bwmt, Bbr = Buf("wn"), Buf("wns"), Buf("wmT"), Buf("br")
        P.dma("sp", wn, ws.rearrange("g i j -> i g j"), writes=[Bwn])
        P.op("dve", lambda e: e.memset(wn[0:64, :, 64:128], 0.0), [], [Bwn])
        P.op("dve", lambda e: e.tensor_copy(out=wnb, in_=wn), [Bwn], [Bwn])
        P.op("pool", lambda e: e.memset(wns, 0.0), [], [Bwns])
        for s_ in range(4):
            P.dma("sp", wns[32 * s_:32 * s_ + 32, :, 32 * s_:32 * s_ + 32],
                  ws[:, 0:32, 0:32].rearrange("g i j -> i g j"), writes=[Bwns])
        P.op("pool", lambda e: e.tensor_copy(out=wnsb, in_=wns), [Bwns], [Bwns])
        for (src_b, dst, Bs) in ((wnb, wmT, Bwn), (wnsb, wmTs, Bwns)):
            bank, Bb = next_bank()
            pb = bank[:].bitcast(BF16)
            for g in range(8):
                P.op("pe", lambda e, pb=pb, g=g, src_b=src_b: e.transpose(out=pb[:, g * 128:(g + 1) * 128],
                                                                        in_=src_b[:, g, :], identity=identb),
                     [Bs, B_ident], [Bb])
            P.op("dve", lambda e, pb=pb, dst=dst: e.tensor_copy(out=dst, in_=pb.rearrange("p (a b) -> p a b", a=8)),
                 [Bb], [Bwmt])
        P.dma("sp", bias_bc[:, 0, :], gbias.rearrange("g i -> (g i)").partition_broadcast(128), writes=[Bbr])
        for s_ in range(4):
            P.op("dve", lambda e, s_=s_: e.tensor_copy(
                out=bias_bc[:, 1, :].rearrange("p (g i) -> p g i", g=8)[:, :, 32 * s_:32 * s_ + 32],
                in_=bias_bc[:, 0, :].rearrange("p (g i) -> p g i", g=8)[:, :, 0:32]), [Bbr], [Bbr])
        if cut == 1:
            P.fence()
            return

        va = [AR.alloc([DA], F32) for _ in range(2)]
        Bva = [Buf("va0"), Buf("va1")]
        vnb = AR.alloc([DA], BF16)
        vnf = AR.alloc([DA], F32)
        junk = AR.alloc([DA], BF16)
        Bvnb, Bvnf, Bjunk = Buf("vnb"), Buf("vnf"), Buf("junk3")
        st = AR.alloc([3 * NT], F32)
        Bst = [Buf("st3_%d" % t) for t in range(NT)]
        uTs = [AR.alloc([16, 512], BF16) for _ in range(2)]
        BuTs = [Buf("uTs0"), Buf("uTs1")]
        aTg = [AR.alloc([16, 512], BF16) for _ in range(2)]
        BaTg = [Buf("aTg0"), Buf("aTg1")]
        uTv = uT.rearrange("(dc p) n -> p dc n", p=128)
        aTv = aTd.rearrange("(dc p) n -> p dc n", p=128)
        for gi, (tk0, ntk) in enumerate(TOKG):
            u_g, Bu = uTs[gi % 2], BuTs[gi % 2]
            a_g, Ba = aTg[gi % 2], BaTg[gi % 2]
            P.dma("sp", u_g[:, :, 0:ntk], uTv[:, :, tk0:tk0 + ntk], reads=[B_uT], writes=[Bu])
            for tl in range(ntk // 128):
                t = tk0 // 128 + tl
                v_t, Bv = va[t % 2], Bva[t % 2]
                P.dma("sp", v_t, projT[t * 128:(t + 1) * 128, PJ_VA:PJ_VA + DA], reads=[B_projT], writes=[Bv])
                ss, sd, rs = st[:, 3 * t:3 * t + 1], st[:, 3 * t + 1:3 * t + 2], st[:, 3 * t + 2:3 * t + 3]
                P.op("act", lambda e, v_t=v_t, ss=ss: e.activation(out=junk, in_=v_t, func=AF.Square, accum_out=ss),
                     [Bv], [Bjunk, Bst[t]])
                P.op("act", lambda e, ss=ss, sd=sd: e.activation(out=sd, in_=ss, func=AF.Sqrt, scale=1.0 / DA, bias=EPS),
                     [Bst[t]], [Bst[t]])
                P.op("dve", lambda e, sd=sd, rs=rs: e.reciprocal(out=rs, in_=sd), [Bst[t]], [Bst[t]])
                P.op("dve", lambda e, v_t=v_t, rs=rs: e.scalar_tensor_tensor(out=vnb, in0=v_t, scalar=rs, in1=gn_bc,
                                                                         op0=ALU.mult, op1=ALU.mult),
                     [Bv, Bst[t], Bgn], [Bvnb])
                if t == NT - 1:
                    P.op("dve", lambda e, v_t=v_t, rs=rs: e.scalar_tensor_tensor(out=vnf, in0=v_t, scalar=rs, in1=gn_bc,
                                                                             op0=ALU.mult, op1=ALU.mult),
                         [Bv, Bst[t], Bgn], [Bvnf])
                    ob = Buf("o_vn")
                    out_bufs.append(ob)
                    P.dma("sp", o_vn[:, :], vnf, reads=[Bvnf], writes=[ob])
                wm_t = wmTs if t == NT - 1 else wmT
                bi = 1 if t == NT - 1 else 0
                for q4 in range(4 if cut > 2 else 0):
                    bank, Bb = next_bank()
                    for j in range(4):
                        dc = q4 * 4 + j
                        g = dc // 2
                        P.op("pe", lambda e, bank=bank, j=j, dc=dc, g=g, wm_t=wm_t:
                             e.matmul(bank[:, j * 128:(j + 1) * 128], lhsT=vnb[:, dc * 128:(dc + 1) * 128],
                                      rhs=wm_t[:, g, :], start=True, stop=True),
                             [Bvnb, Bwmt], [Bb])
                    if cut == 3:
                        continue
                    P.op("dve", lambda e, bank=bank, q4=q4, bi=bi: e.tensor_tensor(
                        out=stmp.rearrange("p (g r i) -> p g r i", g=2, r=2),
                        in0=bank[:, 0:512].rearrange("p (g r i) -> p g r i", g=2, r=2),
                        in1=bias4(bias_bc, bi, q4), op=ALU.add),
                        [Bb, Bbr], [Bstmp])
                    if cut == 4:
                        continue
                    P.op("dve", lambda e, q4=q4, tl=tl, a_g=a_g, u_g=u_g:
                         e.tensor_tensor(out=a_g[:, q4 * 4:(q4 + 1) * 4, tl * 128:(tl + 1) * 128],
                                         in0=stmp.rearrange("p (a b) -> p a b", a=4),
                                         in1=u_g[:, q4 * 4:(q4 + 1) * 4, tl * 128:(tl + 1) * 128], op=ALU.mult),
                         [Bstmp, Bu], [Ba])
            P.dma("sp", aTv[:, :, tk0:tk0 + ntk], a_g[:, :, 0:ntk], reads=[Ba], writes=[B_aTd])
        P.fence()

    if want("s3"):
        s3()
    if stop == "s3":
        return finish(nc, P, out_bufs)

    def s4():
        AR.reset(base_mark)
        SCALE = float(HD) ** -0.5
        MNEG = -30000.0
        WSC = float(NIH) ** -0.5 * float(IDD) ** -0.5
        RAWW = PJW - PJ_Q
        KT0, Vb0, kiT20 = AR.alloc([4, NP], BF16), AR.alloc([16, 512], BF16), AR.alloc([NP], BF16)
        m_s = AR.mark()
        KT1, Vb1, kiT21 = AR.alloc([4, NP], BF16), AR.alloc([16, 512], BF16), AR.alloc([NP], BF16)
        ckb = AR.alloc([16, 512], BF16)
        ckib = AR.alloc([16, 128], BF16)
        m_e = AR.mark()
        AR.reset(m_s)
        raw_b, qkb_b, qib_b, kib_b, wsm_b = (AR.alloc([RAWW], F32), AR.alloc([2560], BF16), AR.alloc([1024], BF16),
                                             AR.alloc([128], BF16), AR.alloc([16 * 6], F32))
        assert AR.mark() <= m_e
        AR.reset(m_e)
        KT, Vb, kiT2 = [KT0, KT1], [Vb0, Vb1], [kiT20, kiT21]
        BKT = [Buf("KT0"), Buf("KT1")]
        BVb = [Buf("Vb0"), Buf("Vb1")]
        BkiT = [Buf("kiT0"), Buf("kiT1")]
        Bckb, Bckib = Buf("ckb"), Buf("ckib")
        csq = AR.alloc([NT, 32], F32)
        csi = AR.alloc([NT, 16], F32)
        bd = AR.alloc([128], F32)
        Bcs = Buf("cs")
        P.dma("sp", csq, c_csq.rearrange("(t p) c -> p t c", p=128), writes=[Bcs])
        P.dma("sp", csi, c_csi.rearrange("(t p) c -> p t c", p=128), writes=[Bcs])
        P.dma("sp", bd, c_bd[:, :], writes=[Bcs])
        ones_b = AR.alloc([128], BF16)
        diagT = AR.alloc([128], BF16)
        P.op("dve", lambda e: e.memset(ones_b, 1.0), [], [Bcs])
        P.op("dve", lambda e: e.memset(diagT, 0.0), [], [Bcs])
        P.op("dve", lambda e: e.memset(diagT[0:64, 64:128], MNEG), [], [Bcs])
        I4 = AR.alloc([4, 128], BF16)
        for h_ in range(4):
            P.op("dve", lambda e, h_=h_: e.tensor_copy(out=I4[:, h_, :], in_=identb), [B_ident], [Bcs])
        raw2 = [AR.alloc([RAWW], F32), raw_b]
        Braw2 = [Buf("raw0"), Buf("raw1")]
        tmp = [AR.alloc([20 * 16], F32) for _ in range(4)]
        Btmp = Buf("ropetmp")
        qkb2 = [AR.alloc([2560], BF16), qkb_b]
        qib2 = [AR.alloc([1024], BF16), qib_b]
        kib2 = [AR.alloc([128], BF16), kib_b]
        Bqkb2, Bqib2, Bkib2 = [Buf("qkb0"), Buf("qkb1")], [Buf("qib0"), Buf("qib1")], [Buf("kib0"), Buf("kib1")]
        qT2 = [AR.alloc([16, 128], BF16) for _ in range(2)]
        qiT = AR.alloc([8, 128], BF16)
        kTs = AR.alloc([4, 128], BF16)
        Vnew = AR.alloc([512], BF16)
        kiTs = AR.alloc([128], BF16)
        BqT2 = [Buf("qT0"), Buf("qT1")]
        BqiT, BkTs, BVnew, BkiTs = Buf("qiT"), Buf("kTs"), Buf("Vnew"), Buf("kiTs")
        wsm2 = [AR.alloc([16 * 6], F32), wsm_b]
        Bw2 = [Buf("wsm0"), Buf("wsm1")]
        wsm, Bw = wsm2[0], Bw2[0]
        S = AR.alloc([2080], F32)
        mb2 = [AR.alloc([2080], BF16) for _ in range(2)]
        Bmb2 = [Buf("mb0"), Buf("mb1")]
        mx8 = AR.alloc([8], F32)
        BS, Bmx = Buf("S"), Buf("mx8")
        rtmp = [AR.alloc([512], F32) for _ in range(2)]
        Brtmp = [Buf("rtmp%d" % i) for i in range(2)]
        rtb = [AR.alloc([512], BF16) for _ in range(16)]
        Brtb = [Buf("rtb%d" % i) for i in range(16)]
        dg = AR.alloc([16, 128], BF16)
        Bdg = Buf("dg")
        PT = [AR.alloc([512], BF16) for _ in range(5)]
        BPT = [Buf("PT%d" % i) for i in range(5)]
        osb = [AR.alloc([512], F32) for _ in range(2)]
        dsb = [AR.alloc([512], F32) for _ in range(2)]
        Bosb = [Buf("osb0"), Buf("osb1")]
        Bdsb = [Buf("dsb0"), Buf("dsb1")]
        nrm = [0]

        bTg = [AR.alloc([16, 128], BF16) for _ in range(2)]
        BbTg = [Buf("bTg0"), Buf("bTg1")]
        snew = AR.alloc([128], F32)
        snew2 = AR.alloc([128], F32)
        mnew = AR.alloc([128], BF16)
        Bsnew, Bmnew = Buf("snew"), Buf("mnew")
        bTv = bTd.rearrange("(dc p) n -> p dc n", p=128)
        stb = [(banks[i], Bbank[i]) for i in (0, 1, 2, 6, 7)]
        sc_bank, Bsc = banks[3], Bbank[3]
        ob_ = [(banks[4], Bbank[4]), (banks[4], Bbank[4])]
        db_ = [(banks[5], Bbank[5]), (banks[5], Bbank[5])]
        LOOK = 4
        ctr = {"st": 0, "pt": 0, "rt": 0, "mm": 0}

        def nb():
            ctr["st"] += 1
            return stb[ctr["st"] % 5]

        def prep_a(t):
            raw, Braw = raw2[t % 2], Braw2[t % 2]
            qkb, Bqkb, qib, Bqib, kib, Bkib = qkb2[t % 2], Bqkb2[t % 2], qib2[t % 2], Bqib2[t % 2], kib2[t % 2], Bkib2[t % 2]
            wsm, Bw = wsm2[t % 2], Bw2[t % 2]
            P.dma("sp", raw, projT[t * 128:(t + 1) * 128, PJ_Q:PJW], reads=[B_projT], writes=[Braw])
            for (o0, nh, hd, half, cs_t) in ((0, 20, 128, 16, csq), (3072, 17, 64, 8, csi)):
                v3 = raw[:, o0:o0 + nh * hd].rearrange("p (h d) -> p h d", h=nh)
                x1, x2_ = v3[:, :, 0:half], v3[:, :, half:2 * half]
                cos = cs_t[:, t, 0:half].unsqueeze(1).to_broadcast([128, nh, half])
                sin = cs_t[:, t, half:2 * half].unsqueeze(1).to_broadcast([128, nh, half])
                tt = [tm[:, 0:nh * half].rearrange("p (h d) -> p h d", h=nh) for tm in tmp]
                P.op("pool", lambda e, tt=tt, x1=x1, cos=cos: e.tensor_tensor(out=tt[0], in0=x1, in1=cos, op=ALU.mult),
                     [Braw, Bcs], [Btmp])
                P.op("pool", lambda e, tt=tt, x2_=x2_, sin=sin: e.tensor_tensor(out=tt[1], in0=x2_, in1=sin, op=ALU.mult),
                     [Braw, Bcs], [Btmp])
                P.op("pool", lambda e, tt=tt, x2_=x2_, cos=cos: e.tensor_tensor(out=tt[2], in0=x2_, in1=cos, op=ALU.mult),
                     [Braw, Bcs], [Btmp])
                P.op("pool", lambda e, tt=tt, x1=x1, sin=sin: e.tensor_tensor(out=tt[3], in0=x1, in1=sin, op=ALU.mult),
                     [Braw, Bcs], [Btmp])
                P.op("pool", lambda e, tt=tt, x1=x1: e.tensor_tensor(out=x1, in0=tt[0], in1=tt[1], op=ALU.subtract),
                     [Btmp], [Braw])
                P.op("pool", lambda e, tt=tt, x2_=x2_: e.tensor_tensor(out=x2_, in0=tt[2], in1=tt[3], op=ALU.add),
                     [Btmp], [Braw])
            for (dst, c0, w) in ((o_k, 2048, 512), (o_v, 2560, 512), (o_ki, 4096, 64)):
                ob = Buf("ok")
                out_bufs.append(ob)
                P.dma("sp", dst[t * 128:(t + 1) * 128, :], raw[:, c0:c0 + w], reads=[Braw], writes=[ob])
            P.op("act", lambda e: e.activation(out=qkb, in_=raw[:, 0:2560], func=AF.Copy), [Braw], [Bqkb])
            wi = raw[:, 4160:4176]
            P.op("act", lambda e: e.activation(out=wsm[:, 0:16], in_=wi, func=AF.Abs, scale=WSC), [Braw], [Bw])
            P.op("act", lambda e: e.activation(out=wsm[:, 16:32], in_=wi, func=AF.Sign), [Braw], [Bw])
            P.op("pool", lambda e: e.tensor_tensor(out=qib.rearrange("p (h d) -> p h d", h=16),
                                                  in0=raw[:, 3072:4096].rearrange("p (h d) -> p h d", h=16),
                                                  in1=wsm[:, 0:16].unsqueeze(2).to_broadcast([128, 16, 64]),
                                                  op=ALU.mult), [Braw, Bw], [Bqib])
            P.op("pool", lambda e: e.tensor_copy(out=kib.rearrange("p (a d) -> p a d", a=2),
                                                in_=raw[:, 4096:4160].unsqueeze(1).to_broadcast([128, 2, 64])),
                 [Braw], [Bkib])

        def prep_b(t, kv):
            raw, Braw = raw2[t % 2], Braw2[t % 2]
            qkb, Bqkb, qib, Bqib, kib, Bkib = qkb2[t % 2], Bqkb2[t % 2], qib2[t % 2], Bqib2[t % 2], kib2[t % 2], Bkib2[t % 2]
            sample = (t == NT - 1)
            for b0 in (0, 8, 16):
                nb_ = min(8, 20 - b0)
                bank, Bb = nb()
                pb = bank[:].bitcast(BF16)
                for j in range(nb_):
                    hh = b0 + j
                    P.op("pe", lambda e, pb=pb, j=j, hh=hh: e.transpose(out=pb[:, j * 128:(j + 1) * 128],
                                                                      in_=qkb[:, hh * 128:(hh + 1) * 128],
                                                                      identity=identb), [Bqkb, B_ident], [Bb])
                if b0 < 16:
                    P.op("act", lambda e, pb=pb, b0=b0, t=t: e.activation(out=qT2[t % 2][:, b0:b0 + 8, :],
                                                                         in_=pb.rearrange("p (a b) -> p a b", a=8),
                                                                         func=AF.Copy), [Bb], [BqT2[t % 2]])
                else:
                    src3 = pb[:, 0:512].rearrange("p (a b) -> p a b", a=4)
                    if sample:
                        P.op("act", lambda e, src3=src3: e.activation(out=kTs, in_=src3, func=AF.Copy), [Bb], [BkTs])
                    else:
                        P.op("act", lambda e, src3=src3, kv=kv, t=t: e.activation(
                            out=KT[kv][:, :, t * 128:(t + 1) * 128], in_=src3, func=AF.Copy), [Bb], [BKT[kv]])
            bank, Bb = nb()
            pb = bank[:].bitcast(BF16)
            for j in range(8):
                P.op("pe", lambda e, pb=pb, j=j: e.transpose(out=pb[:, j * 128:(j + 1) * 128],
                                                            in_=qib[:, j * 128:(j + 1) * 128], identity=identb),
                     [Bqib, B_ident], [Bb])
            P.op("dve", lambda e, pb=pb: e.tensor_copy(out=qiT, in_=pb.rearrange("p (a b) -> p a b", a=8)),
                 [Bb], [BqiT])
            bank, Bb = nb()
            pb = bank[:].bitcast(BF16)
            P.op("pe", lambda e, pb=pb: e.transpose(out=pb[:, 0:128], in_=kib, identity=identb), [Bkib, B_ident], [Bb])
            if sample:
                P.op("dve", lambda e, pb=pb: e.tensor_copy(out=kiTs, in_=pb[:, 0:128]), [Bb], [BkiTs])
                P.op("dve", lambda e: e.tensor_copy(out=Vnew, in_=raw[:, 2560:3072]), [Braw], [BVnew])
            else:
                P.op("dve", lambda e, pb=pb, kv=kv, t=t: e.tensor_copy(out=kiT2[kv][:, t * 128:(t + 1) * 128],
                                                                     in_=pb[:, 0:128]), [Bb], [BkiT[kv]])
                P.op("dve", lambda e, kv=kv, t=t: e.tensor_copy(out=Vb[kv][:, t, :], in_=raw[:, 2560:3072]),
                     [Braw], [BVb[kv]])


        def prep_tile(t, kv):
            prep_a(t)
            prep_b(t, kv)

        def score_block(kiT_ap, BkiTb, k0, w, sg_ap_fn, first, dst, Bdst, pe_acc=False):
            zb = [None] * 16

            def emit_z(h):
                bank, Bb = nb()
                zb[h] = (bank, Bb)
                r0 = (h % 2) * 64
                P.op("pe", lambda e, bank=bank, h=h, r0=r0, k0=k0, w=w:
                     e.matmul(bank[:, 0:w], lhsT=qiT[r0:r0 + 64, h // 2, :], rhs=kiT_ap[r0:r0 + 64, k0:k0 + w],
                              start=True, stop=True), [BqiT, BkiTb], [Bb])

            LZ = 4 if pe_acc else 3
            for h in range(LZ):
                emit_z(h)
            for h in range(16):
                if h + LZ < 16:
                    emit_z(h + LZ)
                bank, Bb = zb[h]
                if pe_acc:
                    P.op("act", lambda e, bank=bank, h=h, w=w: e.activation(out=rtb[h][:, 0:w], in_=bank[:, 0:w],
                                                                          func=AF.Relu), [Bb], [Brtb[h]])
                    P.op("pe", lambda e, w=w, h=h: e.matmul(sc_bank[:, 0:w], lhsT=dg[:, h, :], rhs=rtb[h][:, 0:w],
                                                           start=(h == 0), stop=(h == 15)), [Bdg, Brtb[h]], [Bsc])
                    continue
                ctr["rt"] += 1
                ri = ctr["rt"] % 2
                P.op("act", lambda e, bank=bank, ri=ri, w=w: e.activation(out=rtmp[ri][:, 0:w], in_=bank[:, 0:w],
                                                                        func=AF.Relu), [Bb], [Brtmp[ri]])
                sg = sg_ap_fn(h)
                if first and h == 0:
                    P.op("dve", lambda e, ri=ri, w=w, sg=sg, dst=dst: e.tensor_scalar(
                        out=dst[:, 0:w], in0=rtmp[ri][:, 0:w], scalar1=sg, scalar2=None, op0=ALU.mult),
                        [Brtmp[ri], Bw], [Bdst])
                else:
                    P.op("dve", lambda e, ri=ri, w=w, sg=sg, dst=dst: e.scalar_tensor_tensor(
                        out=dst[:, 0:w], in0=rtmp[ri][:, 0:w], scalar=sg, in1=dst[:, 0:w], op0=ALU.mult, op1=ALU.add),
                        [Brtmp[ri], Bw, Bdst], [Bdst])
            if pe_acc:
                P.op("act", lambda e, w=w, dst=dst: e.activation(out=dst[:, 0:w], in_=sc_bank[:, 0:w], func=AF.Copy),
                     [Bsc], [Bdst])

        def topk_mask(L, first_chunk_cut, mi):
            mb, Bm = mb2[mi], Bmb2[mi]
            if first_chunk_cut:
                P.op("dve", lambda e: e.memset(S[0:64, L - 64:L], -3.0e30), [], [BS])
            for r in range(TOPK // 8):
                P.op("dve", lambda e: e.max(out=mx8, in_=S[:, 0:L]), [BS], [Bmx])
                P.op("dve", lambda e: e.match_replace(out=S[:, 0:L], in_to_replace=mx8, in_values=S[:, 0:L],
                                                      imm_value=NEG), [BS, Bmx], [BS])
            P.op("dve", lambda e: e.tensor_scalar(out=mb[:, 0:L], in0=S[:, 0:L], scalar1=-5.0e29, scalar2=MNEG,
                                                  op0=ALU.is_gt, op1=ALU.mult), [BS], [Bm])
            if first_chunk_cut:
                P.op("dve", lambda e: e.memset(mb[0:64, L - 64:L], MNEG), [], [Bm])

        mulctr = [0]

        def attn_group(steps, o_flat, d_ap, Bo, Bd, out_ap, Bout):
            nst = len(steps)
            stbank = [None] * nst

            def emit_st(i):
                s_ = steps[i]
                bank, Bb = nb()
                stbank[i] = (bank, Bb)
                hasb = s_["bias"] is not None
                P.op("pe", lambda e, bank=bank, s_=s_, hasb=hasb: e.matmul(s_["st_out"](bank), lhsT=s_["lhsT_k"],
                                                                         rhs=s_["rhs_q"], start=True, stop=not hasb),
                     [s_["Bk"], s_["Bq"]], [Bb])
                if hasb:
                    P.op("pe", lambda e, bank=bank, s_=s_: e.matmul(s_["st_out"](bank), lhsT=s_["bias"],
                                                                  rhs=s_["bias_rhs"], start=False, stop=True),
                         [s_["Bbias"], Bcs], [Bb])

            for i in range(min(LOOK, nst)):
                emit_st(i)
            for i in range(nst):
                if i + LOOK < nst:
                    emit_st(i + LOOK)
                s_ = steps[i]
                bank, Bb = stbank[i]
                w = s_["st_w"]
                ctr["pt"] += 1
                pi = ctr["pt"] % 5
                pt = PT[pi][:, 0:w]
                P.op("act", lambda e, bank=bank, pt=pt, w=w: e.activation(out=pt, in_=bank[:, 0:w], func=AF.Exp,
                                                                        scale=SCALE), [Bb], [BPT[pi]])
                P.op("pe", lambda e, s_=s_, pt=pt, i=i: e.matmul(s_["pv_out"], lhsT=s_["lhsT_v"], rhs=s_["pv_rhs"](pt),
                                                              start=(i == 0), stop=(i == nst - 1)),
                     [s_["Bv"], BPT[pi]], [Bo])
                P.op("pe", lambda e, s_=s_, pt=pt, i=i: e.matmul(s_["pd_out"], lhsT=ones_b, rhs=s_["pv_rhs"](pt),
                                                              start=(i == 0), stop=(i == nst - 1)),
                     [Bcs, BPT[pi]], [Bd])
            wtot = steps[0]["st_w"]
            nrm[0] += 1
            k = nrm[0] % 2
            rv = steps[0]["rec_view"]
            P.op("act", lambda e, k=k: e.activation(out=osb[k][:, 0:wtot], in_=o_flat, func=AF.Copy), [Bo], [Bosb[k]])
            P.op("act", lambda e, k=k: e.activation(out=dsb[k][:, 0:wtot], in_=d_ap, func=AF.Ln), [Bd], [Bdsb[k]])
            P.op("act", lambda e, k=k: e.activation(out=dsb[k][:, 0:wtot], in_=dsb[k][:, 0:wtot], func=AF.Exp,
                                                    scale=-1.0), [Bdsb[k]], [Bdsb[k]])
            P.op("pool", lambda e, k=k: e.tensor_tensor(out=out_ap, in0=rv(osb[k][:, 0:wtot]), in1=rv(dsb[k][:, 0:wtot]),
                                                        op=ALU.mult), [Bosb[k], Bdsb[k]], [Bout])

        def score_topk(t):
            L = 128 * (t + 1)
            if t >= 2:
                wsm_t, Bw_t = wsm2[t % 2], Bw2[t % 2]
                for h in range(16):
                    P.op("pool", lambda e, h=h, wsm_t=wsm_t: e.tensor_scalar(out=dg[:, h, :], in0=identf,
                                                                            scalar1=wsm_t[:, 16 + h:17 + h],
                                                                            scalar2=None, op0=ALU.mult),
                         [Bw_t, B_ident], [Bdg])
                for k0 in range(0, L, 512):
                    w = min(512, L - k0)
                    score_block(kiT2[0], BkiT[0], k0, w, None, True, S[:, k0:k0 + w], BS, pe_acc=True)
                topk_mask(L, True, t % 2)

        def attention_prompt(t):
            use_topk = t >= 2
            b_g, Bbg = bTg[t % 2], BbTg[t % 2]
            for n in range(4):
                (o_bank, Bo), (d_bank, Bd) = ob_[n % 2], db_[n % 2]
                steps = []
                for kt in range(t + 1):
                    if use_topk:
                        mk, Bmk = mb2[t % 2][:, kt * 128:(kt + 1) * 128], Bmb2[t % 2]
                    else:
                        mk, Bmk = (diagT, Bcs) if kt == t else (None, None)
                    steps.append(dict(
                        bias=mk, Bbias=Bmk, bias_rhs=I4,
                        lhsT_k=KT[0][:, n, kt * 128:(kt + 1) * 128], Bk=BKT[0],
                        rhs_q=qT2[t % 2][:, 4 * n:4 * n + 4, :], Bq=BqT2[t % 2], st_w=512,
                        st_out=lambda bank: bank[:, 0:512].rearrange("p (h q) -> p h q", h=4),
                        lhsT_v=Vb[0][:, kt, n * 128:(n + 1) * 128], Bv=BVb[0],
                        pv_out=o_bank[:, 0:512], pd_out=d_bank[:, 0:512], pv_rhs=lambda pt: pt,
                        rec_view=lambda r: r.rearrange("p (h q) -> p h q", h=4)))
                attn_group(steps, o_bank[:, 0:512], d_bank[:, 0:512], Bo, Bd,
                           b_g[:, 4 * n:4 * n + 4, :], Bbg)
            P.dma("sp", bTv[:, :, t * 128:(t + 1) * 128], b_g, reads=[Bbg], writes=[B_bTd])

        prep_a(0)
        prep_a(1)
        prep_b(0, 0)
        for t in range(16):
            if t + 1 < 16:
                prep_b(t + 1, 0)
                score_topk(t + 1)
            if t + 2 < 16:
                prep_a(t + 2)
            attention_prompt(t)

        P.fence()
        t = NT - 1
        prep_tile(t, 0)
        for s_ in range(4):
            P.op("dve", lambda e, s_=s_: e.tensor_scalar(out=wsm[:, 32 + 16 * s_:48 + 16 * s_], in0=wsm[:, 16:32],
                                                         scalar1=bd[:, 32 * s_:32 * s_ + 1], scalar2=None,
                                                         op0=ALU.mult), [Bw, Bcs], [Bw])
        for s_ in range(4):
            kv = s_ % 2
            ckv = cki[s_].rearrange("(kt p) c -> p kt c", p=128)
            P.dma("pool", ckib[:, :, 0:64], ckv, writes=[Bckib])
            P.dma("pool", ckib[:, :, 64:128], ckv, writes=[Bckib])
            for b0 in (0, 8):
                bank, Bb = nb()
                pb = bank[:].bitcast(BF16)
                for j in range(8):
                    P.op("pe", lambda e, pb=pb, j=j, b0=b0: e.transpose(out=pb[:, j * 128:(j + 1) * 128],
                                                                      in_=ckib[:, b0 + j, :], identity=identb),
                         [Bckib, B_ident], [Bb])
                P.op("dve", lambda e, pb=pb, b0=b0, kv=kv: e.tensor_copy(out=kiT2[kv][:, b0 * 128:(b0 + 8) * 128],
                                                                       in_=pb), [Bb], [BkiT[kv]])
            for k0 in range(0, PAST, 512):
                score_block(kiT2[kv], BkiT[kv], k0, 512,
                            lambda h, s_=s_: wsm[:, 32 + 16 * s_ + h:33 + 16 * s_ + h], s_ == 0, S[:, k0:k0 + 512], BS)
        score_block(kiTs, BkiTs, 0, 128, lambda h: wsm[:, 16 + h:17 + h], True, snew, Bsnew)
        P.op("dve", lambda e: e.tensor_tensor(out=snew2, in0=snew, in1=bd, op=ALU.mult), [Bsnew, Bcs], [Bsnew])
        P.op("dve", lambda e: e.tensor_reduce(out=S[:, PAST:PAST + 32], in_=snew2.rearrange("p (s j) -> p j s", s=4),
                                              axis=AX.X, op=ALU.add), [Bsnew], [BS])
        topk_mask(PAST + 32, False, 0)
        mbs, Bmbs = mb2[0], Bmb2[0]
        m3 = mnew.rearrange("p (s j) -> p s j", s=4)
        P.op("dve", lambda e: e.tensor_scalar(out=snew.rearrange("p (s j) -> p s j", s=4),
                                              in0=mbs[:, PAST:PAST + 32].unsqueeze(1).to_broadcast([128, 4, 32]),
                                              scalar1=-MNEG, scalar2=None, op0=ALU.add), [Bmbs], [Bsnew])
        P.op("dve", lambda e: e.tensor_tensor(out=snew2, in0=snew, in1=bd, op=ALU.mult), [Bsnew, Bcs], [Bsnew])
        P.op("dve", lambda e: e.tensor_scalar(out=mnew, in0=snew2, scalar1=MNEG, scalar2=None, op0=ALU.add),
             [Bsnew], [Bmnew])
        b_g, Bbg = bTg[0], BbTg[0]
        for s_ in range(4):
            kv = s_ % 2
            P.dma("pool", ckb, ck[s_].rearrange("(kt p) c -> p kt c", p=128), writes=[Bckb])
            P.dma("pool", Vb[kv], cv[s_].rearrange("(kt p) c -> p kt c", p=128), writes=[BVb[kv]])
            for kt2 in range(8):
                bank, Bb = nb()
                pb = bank[:].bitcast(BF16)
                for j in range(8):
                    ktl, n = j // 4, j % 4
                    kt = kt2 * 2 + ktl
                    P.op("pe", lambda e, pb=pb, j=j, kt=kt, n=n: e.transpose(out=pb[:, j * 128:(j + 1) * 128],
                                                                           in_=ckb[:, kt, n * 128:(n + 1) * 128],
                                                                           identity=identb), [Bckb, B_ident], [Bb])
                eng = evac_eng()
                o_ap = KT[kv][:, :, kt2 * 256:(kt2 + 1) * 256].rearrange("p n (k c) -> p n k c", k=2)
                i_ap = pb.rearrange("p (k n c) -> p n k c", k=2, n=4)
                P.op(eng, copy_op(eng, o_ap, i_ap), [Bb], [BKT[kv]])
            qs = slice(32 * s_, 32 * s_ + 32)
            for n in range(4):
                (o_bank, Bo), (d_bank, Bd) = ob_[n % 2], db_[n % 2]
                o3 = o_bank[:, 0:128].rearrange("p (h q) -> p h q", h=4)
                steps = []
                for kt in range(17):
                    if kt < 16:
                        lk, Bk_, lv, Bv_ = KT[kv][:, n, kt * 128:(kt + 1) * 128], BKT[kv], \
                            Vb[kv][:, kt, n * 128:(n + 1) * 128], BVb[kv]
                    else:
                        lk, Bk_, lv, Bv_ = kTs[:, n, :], BkTs, Vnew[:, n * 128:(n + 1) * 128], BVnew
                    steps.append(dict(
                        lhsT_k=lk, Bk=Bk_, rhs_q=qT2[0][:, 4 * n:4 * n + 4, qs], Bq=BqT2[0], st_w=128,
                        st_out=lambda bank: bank[:, 0:128].rearrange("p (h q) -> p h q", h=4),
                        bias=(mbs[:, kt * 128:(kt + 1) * 128] if kt < 16 else mnew),
                        Bbias=(Bmbs if kt < 16 else Bmnew), bias_rhs=I4[:, :, qs], lhsT_v=lv, Bv=Bv_,
                        pv_out=o_bank[:, 0:128], pd_out=d_bank[:, 0:128], pv_rhs=lambda pt: pt,
                        rec_view=lambda r: r.rearrange("p (h q) -> p h q", h=4)))
                attn_group(steps, o_bank[:, 0:128], d_bank[:, 0:128], Bo, Bd, b_g[:, 4 * n:4 * n + 4, qs], Bbg)
        P.dma("sp", bTv[:, :, NP:N], b_g, reads=[Bbg], writes=[B_bTd])
        P.fence()

    if want("s4"):
        s4()
    if stop == "s4":
        return finish(nc, P, out_bufs)

    def s5():
        AR.reset(base_mark)
        aT = AR.alloc([16, N], BF16)
        bT = AR.alloc([16, N], BF16)
        BaT, BbT = Buf("aT"), Buf("bT")
        aTv = aTd.rearrange("(dc p) n -> p dc n", p=128)
        bTv = bTd.rearrange("(dc p) n -> p dc n", p=128)
        for q in range(4):
            P.dma("sp", aT[:, 4 * q:4 * q + 4, :], aTv[:, 4 * q:4 * q + 4, :], reads=[B_aTd], writes=[BaT])
            P.dma("sp", bT[:, 4 * q:4 * q + 4, :], bTv[:, 4 * q:4 * q + 4, :], reads=[B_bTd], writes=[BbT])
        WC = 256
        wa = [AR.alloc([16, WC], BF16) for _ in range(2)]
        wb_ = [AR.alloc([16, WC], BF16) for _ in range(2)]
        Bwa, Bwb = [Buf("wa0"), Buf("wa1")], [Buf("wbb0"), Buf("wbb1")]
        ga = [AR.alloc([512], BF16) for _ in range(2)]
        gb_ = [AR.alloc([512], BF16) for _ in range(2)]
        Bga, Bgb = [Buf("ga0"), Buf("ga1")], [Buf("gb0"), Buf("gb1")]
        tA = [AR.alloc([512], F32) for _ in range(2)]
        tB = [AR.alloc([512], F32) for _ in range(2)]
        BtA, BtB = [Buf("tA0"), Buf("tA1")], [Buf("tB0"), Buf("tB1")]
        ys = [AR.alloc([512], BF16) for _ in range(2)]
        Bys = [Buf("ys0"), Buf("ys1")]
        wav = w_a.rearrange("(kc p) c -> p kc c", p=128)
        wbv = w_b.rearrange("(kc p) c -> p kc c", p=128)
        nblk = D // WC

        def load(i):
            P.dma("pool", wa[i % 2], wav[:, :, i * WC:(i + 1) * WC], writes=[Bwa[i % 2]])
            P.dma("pool", wb_[i % 2], wbv[:, :, i * WC:(i + 1) * WC], writes=[Bwb[i % 2]])

        load(0)
        it = 0
        for i in range(nblk):
            if i + 1 < nblk:
                load(i + 1)
            for cj in range(WC // 128):
                c = i * WC + cj * 128
                for (tk0, ntk) in TOKG:
                    k = it % 2
                    it += 1
                    P.dma("sp", ga[k][:, 0:ntk], gaT[c:c + 128, tk0:tk0 + ntk], reads=[B_gaT], writes=[Bga[k]])
                    P.dma("sp", gb_[k][:, 0:ntk], gbT[c:c + 128, tk0:tk0 + ntk], reads=[B_gbT], writes=[Bgb[k]])
                    bankA, BbA = next_bank()
                    for kc in range(16):
                        P.op("pe", lambda e, bankA=bankA, kc=kc, cj=cj, tk0=tk0, ntk=ntk, i=i:
                             e.matmul(bankA[:, 0:ntk], lhsT=wa[i % 2][:, kc, cj * 128:(cj + 1) * 128],
                                      rhs=aT[:, kc, tk0:tk0 + ntk], start=(kc == 0), stop=(kc == 15)),
                             [Bwa[i % 2], BaT], [BbA])
                    bankB, BbB = next_bank()
                    for kc in range(16):
                        P.op("pe", lambda e, bankB=bankB, kc=kc, cj=cj, tk0=tk0, ntk=ntk, i=i:
                             e.matmul(bankB[:, 0:ntk], lhsT=wb_[i % 2][:, kc, cj * 128:(cj + 1) * 128],
                                      rhs=bT[:, kc, tk0:tk0 + ntk], start=(kc == 0), stop=(kc == 15)),
                             [Bwb[i % 2], BbT], [BbB])
                    P.op("dve", lambda e, k=k, ntk=ntk, bankA=bankA: e.tensor_tensor(
                        out=tA[k][:, 0:ntk], in0=bankA[:, 0:ntk], in1=ga[k][:, 0:ntk], op=ALU.mult),
                        [BbA, Bga[k]], [BtA[k]])
                    P.op("dve", lambda e, k=k, ntk=ntk, bankB=bankB: e.tensor_tensor(
                        out=tB[k][:, 0:ntk], in0=bankB[:, 0:ntk], in1=gb_[k][:, 0:ntk], op=ALU.mult),
                        [BbB, Bgb[k]], [BtB[k]])
                    P.op("pool", lambda e, k=k, ntk=ntk: e.tensor_tensor(
                        out=ys[k][:, 0:ntk], in0=tA[k][:, 0:ntk], in1=tB[k][:, 0:ntk], op=ALU.add),
                        [BtA[k], BtB[k]], [Bys[k]])
                    P.dma("sp", yT[c:c + 128, tk0:tk0 + ntk], ys[k][:, 0:ntk], reads=[Bys[k]], writes=[B_yT])
        P.fence()

    if want("s5"):
        s5()
    if stop == "s5":
        return finish(nc, P, out_bufs)

    def s6():
        AR.reset(base_mark)
        yTs = AR.alloc([32, N], BF16)
        ByTs = Buf("yTs")
        yTv = yT.rearrange("(kc p) n -> p kc n", p=128)
        for q in range(8):
            P.dma("sp", yTs[:, 4 * q:4 * q + 4, :], yTv[:, 4 * q:4 * q + 4, :], reads=[B_yT], writes=[ByTs])
        xs = [AR.alloc([512], F32) for _ in range(2)]
        Bxs = [Buf("xs0"), Buf("xs1")]
        ctr = [0]

        def epi_T(tag, c0, ncols, t, bank, Bb):
            i = ctr[0] % 2
            ctr[0] += 1
            P.dma("sp", xs[i], xin[t * 128:(t + 1) * 128, c0:c0 + ncols], reads=[B_xin], writes=[Bxs[i]])
            P.op("dve", lambda e, i=i, bank=bank: e.tensor_tensor(out=xs[i], in0=bank[:, 0:512], in1=xs[i], op=ALU.add),
                 [Bb, Bxs[i]], [Bxs[i]])
            P.dma("sp", x2[t * 128:(t + 1) * 128, c0:c0 + ncols], xs[i], reads=[Bxs[i]], writes=[B_x2])

        gemm(yTs, ByTs, 32, w_o, [(512 * j, 512, "T", "o") for j in range(8)], epilogue_T=epi_T)
        P.fence()

    if want("s6"):
        s6()
    if stop == "s6":
        return finish(nc, P, out_bufs)

    AR.reset(base_mark)
    hT = AR.alloc([32, N], BF16)
    B_hT = Buf("hT2")
    if want("s7"):
        norm_transpose(x2, B_x2, g_ffn, hT, B_hT)

    def s8():
        m = AR.mark()
        NCH = 2 * DFF // 128
        par = AR.alloc([NCH, 12], F32)
        Bpar = Buf("par")
        Zl = AR.alloc([NCH, 10], F32)
        BZl = Buf("Zl")
        rowsrc = AR.alloc([2048], F32)
        Brow = Buf("rowsrc")
        for p0 in range(0, 2 * DFF, 2048):
            wdt = min(2048, 2 * DFF - p0)
            P.dma("sp", rowsrc[0:8, 0:wdt], cst[:, p0:p0 + wdt], writes=[Brow])
            P.dma("sp", rowsrc[8:11, 0:wdt], conv_w[:, p0:p0 + wdt], writes=[Brow])
            P.dma("sp", rowsrc[11:12, 0:wdt], conv_b[p0:p0 + wdt].unsqueeze(0), writes=[Brow])
            bank, Bb = next_bank()
            nch = wdt // 128
            for j in range(nch):
                P.op("pe", lambda e, bank=bank, j=j: e.matmul(bank[:, j * 12:(j + 1) * 12],
                                                            lhsT=rowsrc[0:12, j * 128:(j + 1) * 128],
                                                            rhs=identf[0:12, 0:12], start=True, stop=True),
                     [Brow, B_ident], [Bb])
            c0 = p0 // 128
            P.op("dve", lambda e, bank=bank, c0=c0, nch=nch: e.tensor_copy(
                out=par[:, c0:c0 + nch, :], in_=bank[:, 0:nch * 12].rearrange("p (a b) -> p a b", a=nch)),
                [Bb], [Bpar])
        WC = 256
        wb = [AR.alloc([32, WC], BF16) for _ in range(2)]
        Bwb = [Buf("wu0"), Buf("wu1")]
        zb = {k: [AR.alloc([516], F32) for _ in range(2)] for k in ("g", "u")}
        Bzb = {k: [Buf("z%s0" % k), Buf("z%s1" % k)] for k in ("g", "u")}
        cg = AR.alloc([512], F32)
        cu = AR.alloc([512], F32)
        Bcg, Bcu = Buf("cg"), Buf("cu")
        ast = [AR.alloc([512], BF16) for _ in range(2)]
        Bast = [Buf("ast0"), Buf("ast1")]
        wv = w_up.rearrange("(kc p) c -> p kc c", p=128)
        nblk = DFF // 128

        def load(i):
            P.dma("pool", wb[i % 2][:, :, 0:128], wv[:, :, i * 128:(i + 1) * 128], writes=[Bwb[i % 2]])
            P.dma("pool", wb[i % 2][:, :, 128:256], wv[:, :, DFF + i * 128:DFF + (i + 1) * 128], writes=[Bwb[i % 2]])

        load(0)
        it = 0
        for i in range(nblk):
            if i + 1 < nblk:
                load(i + 1)
            w_i, Bw = wb[i % 2], Bwb[i % 2]
            for gi, (tk0, ntk) in enumerate(TOKG):
                bk = {}
                for (k, off) in (("g", 0), ("u", 128)):
                    bank, Bb = next_bank()
                    bk[k] = (bank, Bb)
                    for kc in range(32):
                        P.op("pe", lambda e, bank=bank, kc=kc, off=off, tk0=tk0, ntk=ntk, w_i=w_i:
                             e.matmul(bank[:, 0:ntk], lhsT=w_i[:, kc, off:off + 128], rhs=hT[:, kc, tk0:tk0 + ntk],
                                      start=(kc == 0), stop=(kc == 31)), [Bw, B_hT], [Bb])
                zi = gi % 2
                outc = {}
                for (k, cacc, Bc, ch) in (("g", cg, Bcg, i), ("u", cu, Bcu, nblk + i)):
                    bank, Bb = bk[k]
                    z, Bz = zb[k][zi], Bzb[k][zi]
                    zp, Bzp = zb[k][1 - zi], Bzb[k][1 - zi]
                    w0, w1, w2, bia = par[:, ch, 8:9], par[:, ch, 9:10], par[:, ch, 10:11], par[:, ch, 11:12]
                    sample = (gi == 4)
                    if not sample:
                        P.op("act", lambda e, z=z, bank=bank, ntk=ntk: e.activation(out=z[:, 2:2 + ntk], in_=bank[:, 0:ntk],
                                                                                  func=AF.Copy), [Bb], [Bz])
                        if gi == 0:
                            P.op("pool", lambda e, z=z: e.memset(z[:, 0:2], 0.0), [], [Bz])
                        else:
                            P.op("pool", lambda e, z=z, zp=zp: e.tensor_copy(out=z[:, 0:2], in_=zp[:, 512:514]),
                                 [Bzp], [Bz])
                        P.op("act", lambda e, cacc=cacc, bank=bank, ntk=ntk, w2=w2, bia=bia: e.activation(
                            out=cacc[:, 0:ntk], in_=bank[:, 0:ntk], func=AF.Identity, scale=w2, bias=bia),
                            [Bb, Bpar], [Bc])
                        P.op("dve", lambda e, cacc=cacc, z=z, ntk=ntk, w1=w1: e.scalar_tensor_tensor(
                            out=cacc[:, 0:ntk], in0=z[:, 1:1 + ntk], scalar=w1, in1=cacc[:, 0:ntk],
                            op0=ALU.mult, op1=ALU.add), [Bz, Bc, Bpar], [Bc])
                        P.op("dve", lambda e, cacc=cacc, z=z, ntk=ntk, w0=w0: e.scalar_tensor_tensor(
                            out=cacc[:, 0:ntk], in0=z[:, 0:ntk], scalar=w0, in1=cacc[:, 0:ntk],
                            op0=ALU.mult, op1=ALU.add), [Bz, Bc, Bpar], [Bc])
                        if gi == 3:
                            P.op("pool", lambda e, z=z, ch=ch: e.tensor_copy(out=Zl[:, ch, 0:2], in_=z[:, 512:514]),
                                 [Bz], [BZl])
                    else:
                        z3 = z[:, 0:136].rearrange("p (s c) -> p s c", s=4)
                        P.op("act", lambda e, z3=z3, bank=bank: e.activation(
                            out=z3[:, :, 2:34], in_=bank[:, 0:128].rearrange("p (s c) -> p s c", s=4), func=AF.Copy),
                            [Bb], [Bz])
                        P.op("pool", lambda e, z3=z3, ch=ch: e.tensor_copy(
                            out=z3[:, :, 0:2], in_=par[:, ch, 0:8].rearrange("p (s r) -> p s r", s=4)), [Bpar], [Bz])
                        c3 = cacc[:, 0:128].rearrange("p (s c) -> p s c", s=4)
                        P.op("act", lambda e, cacc=cacc, bank=bank, w2=w2, bia=bia: e.activation(
                            out=cacc[:, 0:128], in_=bank[:, 0:128], func=AF.Identity, scale=w2, bias=bia),
                            [Bb, Bpar], [Bc])
                        P.op("dve", lambda e, c3=c3, z3=z3, w1=w1: e.scalar_tensor_tensor(
                            out=c3, in0=z3[:, :, 1:33], scalar=w1, in1=c3, op0=ALU.mult, op1=ALU.add),
                            [Bz, Bc, Bpar], [Bc])
                        P.op("dve", lambda e, c3=c3, z3=z3, w0=w0: e.scalar_tensor_tensor(
                            out=c3, in0=z3[:, :, 0:32], scalar=w0, in1=c3, op0=ALU.mult, op1=ALU.add),
                            [Bz, Bc, Bpar], [Bc])
                        P.op("pool", lambda e, z3=z3, ch=ch: e.tensor_copy(
                            out=Zl[:, ch, 2:10].rearrange("p (s r) -> p s r", s=4), in_=z3[:, :, 32:34]), [Bz], [BZl])
                P.op("act", lambda e, ntk=ntk: e.activation(out=cg[:, 0:ntk], in_=cg[:, 0:ntk], func=AF.Silu),
                     [Bcg], [Bcg])
                k = it % 2
                it += 1
                P.op("pool", lambda e, k=k, ntk=ntk: e.tensor_tensor(out=ast[k][:, 0:ntk], in0=cg[:, 0:ntk],
                                                                     in1=cu[:, 0:ntk], op=ALU.mult),
                     [Bcg, Bcu], [Bast[k]])
                P.dma("sp", actT[i * 128:(i + 1) * 128, tk0:tk0 + ntk], ast[k][:, 0:ntk], reads=[Bast[k]],
                      writes=[B_actT])
        zo = [rowsrc[:, 0:512], rowsrc[:, 512:1024]]
        Bzo = [Brow, Brow]
        for q in range(NCH // 4):
            bank, Bb = next_bank()
            for j in range(4):
                ch = q * 4 + j
                P.op("pe", lambda e, bank=bank, j=j, ch=ch: e.matmul(bank[0:10, j * 128:(j + 1) * 128], lhsT=Zl[:, ch, :],
                                                                   rhs=identf, start=True, stop=True),
                     [BZl, B_ident], [Bb])
            P.op("dve", lambda e, bank=bank, q=q: e.tensor_copy(out=zo[q % 2][0:10, :], in_=bank[0:10, :]),
                 [Bb], [Bzo[q % 2]])
            ob = Buf("ocst")
            out_bufs.append(ob)
            P.dma("sp", o_cst[:, q * 512:(q + 1) * 512], zo[q % 2][0:10, :], reads=[Bzo[q % 2]], writes=[ob])
        P.fence()
        AR.reset(m)

    if want("s8"):
        s8()
    if stop == "s8":
        return finish(nc, P, out_bufs)

    def s9():
        AR.reset(base_mark)
        KC = DFF // 128
        TB = [(0, 768), (768, 768), (1536, 640)]
        acT = AR.alloc([KC, 768], BF16)
        BacTg = [Buf("acT%d" % g) for g in range((KC + 7) // 8)]
        wb = [AR.alloc([KC, 128], BF16) for _ in range(2)]
        Bwb = [Buf("wd0"), Buf("wd1")]
        fT = [AR.alloc([384], BF16) for _ in range(2)]
        BfT = [Buf("fT0"), Buf("fT1")]
        fst = [AR.alloc([6, 512], F32) for _ in range(2)]
        Bfst = [Buf("fst0"), Buf("fst1")]
        acv = actT.rearrange("(kc p) n -> p kc n", p=128)
        wv = w_down.rearrange("(kc p) c -> p kc c", p=128)
        li = [0]

        def load():
            i = li[0]
            c = i % 32
            if i < 32:
                P.dma("pool", wb[i % 2], wv[:, :, c * 128:(c + 1) * 128], writes=[Bwb[i % 2]])
                P.dma("sp", wdc[c], wb[i % 2], reads=[Bwb[i % 2]], writes=[Bwdc[c]])
            else:
                P.dma("sp", wb[i % 2], wdc[c], reads=[Bwdc[c]], writes=[Bwb[i % 2]])
            li[0] += 1

        ci = 0
        fi = 0
        pend = [None]
        load()
        for bi_, (tb0, ntb) in enumerate(TB):
            for q0 in range(0, KC, 8):
                q1 = min(KC, q0 + 8)
                P.dma("sp", acT[:, q0:q1, 0:ntb], acv[:, q0:q1, tb0:tb0 + ntb], reads=[B_actT], writes=[BacTg[q0 // 8]])
            ntl = ntb // 128
            for c in range(32):
                if li[0] < 32 * len(TB):
                    load()
                w_i, Bw = wb[ci % 2], Bwb[ci % 2]
                ci += 1
                f_s, Bf = fst[(c // 4 + 8 * bi_) % 2], Bfst[(c // 4 + 8 * bi_) % 2]
                for s0 in range(0, ntb, 384):
                    ns = min(384, ntb - s0)
                    bank, Bb = next_bank()
                    for kc in range(KC):
                        P.op("pe", lambda e, bank=bank, kc=kc, s0=s0, ns=ns, w_i=w_i:
                             e.matmul(bank[:, 0:ns], lhsT=w_i[:, kc, :], rhs=acT[:, kc, s0:s0 + ns],
                                      start=(kc == 0), stop=(kc == KC - 1)), [Bw, BacTg[kc // 8]], [Bb])
                    k = fi % 2
                    fi += 1
                    P.op("act", lambda e, k=k, bank=bank, ns=ns: e.activation(out=fT[k][:, 0:ns], in_=bank[:, 0:ns],
                                                                            func=AF.Copy), [Bb], [BfT[k]])
                    last_sub = (s0 + 384 >= ntb)

                    def tail(k=k, ns=ns, s0=s0, c=c, f_s=f_s, Bf=Bf, last_sub=last_sub, tb0=tb0, ntb=ntb, ntl=ntl):
                        bank2, Bb2 = next_bank()
                        pb = bank2[:].bitcast(BF16)
                        for j in range(ns // 128):
                            P.op("pe", lambda e, pb=pb, j=j, k=k: e.transpose(out=pb[:, j * 128:(j + 1) * 128],
                                                                            in_=fT[k][:, j * 128:(j + 1) * 128],
                                                                            identity=identb), [BfT[k], B_ident], [Bb2])
                        tl0 = s0 // 128
                        nj = ns // 128
                        P.op("dve", lambda e, pb=pb, f_s=f_s, tl0=tl0, nj=nj, c=c: e.tensor_copy(
                            out=f_s[:, tl0:tl0 + nj, (c % 4) * 128:(c % 4) * 128 + 128],
                            in_=pb[:, 0:nj * 128].rearrange("p (a b) -> p a b", a=nj)), [Bb2], [Bf])
                        if last_sub and c % 4 == 3:
                            cb = c // 4
                            P.dma("sp", fsc[tb0:tb0 + ntb, cb * 512:(cb + 1) * 512].rearrange("(j p) c -> p j c", p=128),
                                  f_s[:, 0:ntl, :], reads=[Bf], writes=[B_fsc])

                    if pend[0] is not None:
                        pend[0]()
                    pend[0] = tail
            if pend[0] is not None:
                pend[0]()
                pend[0] = None
        P.fence()

    if want("s9"):
        s9()
    if stop == "s9":
        return finish(nc, P, out_bufs)

    def s10():
        AR.reset(base_mark)
        g_bc = AR.alloc([D], F32)
        Bg = Buf("gfin")
        P.dma("sp", g_bc, g_final.partition_broadcast(128), writes=[Bg])
        NB10 = 3
        xa = [AR.alloc([D], F32) for _ in range(NB10)]
        fa = [AR.alloc([D], F32) for _ in range(NB10)]
        Bxa, Bfa = [Buf("xa%d" % i) for i in range(NB10)], [Buf("fa%d" % i) for i in range(NB10)]
        junk = AR.alloc([D], BF16)
        Bjunk = Buf("junk10")
        st = AR.alloc([3 * NT], F32)
        Bst = [Buf("st10_%d" % t) for t in range(NT)]

        def loads(t):
            P.dma("sp", xa[t % NB10], x2[t * 128:(t + 1) * 128, :], reads=[B_x2], writes=[Bxa[t % NB10]])
            P.dma("sp", fa[t % NB10], fsc[t * 128:(t + 1) * 128, :], reads=[B_fsc], writes=[Bfa[t % NB10]])

        loads(0)
        loads(1)
        for t in range(NT):
            if t + 2 < NT:
                loads(t + 2)
            x_t, Bx, f_t, Bf = xa[t % NB10], Bxa[t % NB10], fa[t % NB10], Bfa[t % NB10]
            P.op("pool", lambda e, x_t=x_t, f_t=f_t: e.tensor_tensor(out=x_t, in0=x_t, in1=f_t, op=ALU.add),
                 [Bx, Bf], [Bx])
            ss, sd, rs = st[:, 3 * t:3 * t + 1], st[:, 3 * t + 1:3 * t + 2], st[:, 3 * t + 2:3 * t + 3]
            P.op("act", lambda e, x_t=x_t, ss=ss: e.activation(out=junk, in_=x_t, func=AF.Square, accum_out=ss),
                 [Bx], [Bjunk, Bst[t]])
            P.op("act", lambda e, ss=ss, sd=sd: e.activation(out=sd, in_=ss, func=AF.Sqrt, scale=1.0 / D, bias=EPS),
                 [Bst[t]], [Bst[t]])
            P.op("dve", lambda e, sd=sd, rs=rs: e.reciprocal(out=rs, in_=sd), [Bst[t]], [Bst[t]])
            P.op("dve", lambda e, x_t=x_t, f_t=f_t, rs=rs: e.scalar_tensor_tensor(out=f_t, in0=x_t, scalar=rs, in1=g_bc,
                                                                              op0=ALU.mult, op1=ALU.mult),
                 [Bx, Bst[t], Bg, Bf], [Bf])
            ob = Buf("oy")
            out_bufs.append(ob)
            P.dma("sp", o_y[t * 128:(t + 1) * 128, :], f_t, reads=[Bf], writes=[ob])

    if want("s10"):
        s10()
    return finish(nc, P, out_bufs)


def finish(nc, P, out_bufs):
    P.fence()
    P.emit()
    return nc


def _rope_table(pos, half):
    inv = (np.float32(THETA) ** (-(np.arange(half, dtype=np.float32)) / np.float32(half))).astype(np.float32)
    ang = pos.astype(np.float32)[:, None] * inv[None, :]
    return np.concatenate([np.cos(ang), np.sin(ang)], axis=1).astype(np.float32)


def _consts():
    pos = np.concatenate([np.arange(NP), np.tile(PAST + np.arange(32), 4)]).astype(np.int32)
    bd = np.kron(np.eye(4, dtype=np.float32), np.ones((32, 32), np.float32))
    return {
        "c_ident": np.eye(128, dtype=np.float32),
        "c_csq": _rope_table(pos, 16),
        "c_csi": _rope_table(pos, 8),
        "c_bd": bd,
    }


def make_in_maps(inp, cores=range(8)):
    f = lambda a: np.ascontiguousarray(np.asarray(a, dtype=np.float32))
    shared = {
        "g_attn": f(inp["norm_attn_g"][0]), "w_in": f(inp["w_in"][0]), "g_gmlp": f(inp["gmlp_norm_g"][0]),
        "ws": f(inp["gmlp_ws"][0]), "gbias": f(inp["gmlp_b"][0]), "w_a": f(inp["w_branch_a"][0]),
        "w_b": f(inp["w_branch_b"][0]), "w_o": f(inp["w_out"][0]), "g_ffn": f(inp["norm_ffn_g"][0]),
        "w_up": f(inp["w_up"][0]), "conv_w": f(inp["conv_w"][0]), "conv_b": f(inp["conv_b"][0]),
        "w_down": f(inp["w_down"][0]), "g_final": f(inp["norm_final_g"]),
    }
    shared.update(_consts())
    maps = []
    for c in cores:
        sl = slice(4 * c, 4 * c + 4)
        xs = np.asarray(inp["x_sample"][sl], dtype=np.float32).reshape(NS, D)
        m = dict(shared)
        m["xin"] = np.ascontiguousarray(np.concatenate([np.asarray(inp["x_prompt"][c], dtype=np.float32), xs], axis=0))
        m["ck"] = f(np.asarray(inp["cache_k"][0, sl]).reshape(4, PAST, 512))
        m["cv"] = f(np.asarray(inp["cache_v"][0, sl]).reshape(4, PAST, 512))
        m["cki"] = f(inp["cache_kidx"][0, sl])
        m["cst"] = f(np.asarray(inp["state_ffn_conv"][0, sl]).reshape(8, 2 * DFF))
        maps.append(m)
    return maps


_NC_CACHE = {}


def kernel(**inputs):
    if "nc" not in _NC_CACHE:
        _NC_CACHE["nc"] = build_program()
    nc = _NC_CACHE["nc"]
    maps = make_in_maps(inputs)
    res = run_bass_kernel_spmd(nc, maps, core_ids=list(range(8)))
    R = res.results
    cat = lambda name: [np.asarray(r[name], dtype=np.float32) for r in R]
    y = cat("o_y"); k = cat("o_k"); v = cat("o_v"); ki = cat("o_ki"); cs = cat("o_cst"); vn = cat("o_vn")
    y_prompt = np.stack([a[:NP] for a in y])
    y_sample = np.concatenate([a[NP:].reshape(4, 32, D) for a in y])
    kp = np.stack([a[:NP].reshape(NP, 4, 128) for a in k])[None]
    vp = np.stack([a[:NP].reshape(NP, 4, 128) for a in v])[None]
    kip = np.stack([a[:NP] for a in ki])[None]
    csp = np.stack([a[0:2] for a in cs])[None]
    ks = np.concatenate([a[NP:].reshape(4, 32, 4, 128) for a in k])[None]
    vs = np.concatenate([a[NP:].reshape(4, 32, 4, 128) for a in v])[None]
    kis = np.concatenate([a[NP:].reshape(4, 32, IDD) for a in ki])[None]
    css = np.concatenate([a[2:10].reshape(4, 2, 2 * DFF) for a in cs])[None]
    gv = np.concatenate([a.reshape(4, 32, DA) for a in vn])[None]
    return (y_prompt, y_sample, kp, vp, kip, csp, ks, vs, kis, css, gv)
```

```python
import numpy as np
import ml_dtypes
from contextlib import ExitStack
import concourse.bass as bass
import concourse.mybir as mybir
from concourse.bass_utils import run_bass_kernel_spmd

F32 = mybir.dt.float32
BF16 = mybir.dt.bfloat16
ALU = mybir.AluOpType
AF = mybir.ActivationFunctionType
AX = mybir.AxisListType

ENGS = ("pe", "act", "dve", "pool", "sp")
NDMASEM = 16


class Buf:
    __slots__ = ("name", "w", "r", "rd")

    def __init__(self, name=""):
        self.name = name
        self.w = None
        self.r = {}
        self.rd = []


class Op:
    __slots__ = ("eng", "fn", "deps", "sig", "sem", "val", "dma")

    def __init__(self, eng, fn, dma):
        self.eng = eng
        self.fn = fn
        self.dma = dma
        self.deps = []
        self.sig = dma
        self.sem = None
        self.val = 0


class Prog:
    def __init__(self, nc):
        self.nc = nc
        self.q = {e: [] for e in ENGS}
        self.pend_dma = []

    def op(self, eng, fn, reads=(), writes=(), dma=False):
        o = Op(eng, fn, dma)
        deps = {}
        for b in reads:
            if b.w is not None:
                deps[id(b.w)] = b.w
        for b in writes:
            if b.w is not None:
                deps[id(b.w)] = b.w
            for d in b.r.values():
                deps[id(d)] = d
            for d in b.rd:
                deps[id(d)] = d
        for d in deps.values():
            if d is o:
                continue
            if eng == "pe" and d.eng == "pe" and not d.dma and not dma:
                continue
            d.sig = True
            o.deps.append(d)
        for b in reads:
            if dma:
                b.rd.append(o)
            else:
                b.r[eng] = o
        for b in writes:
            b.w = o
            b.r = {}
            b.rd = []
        self.q[eng].append(o)
        if dma:
            self.pend_dma.append(o)
        return o

    def dma(self, q, out, in_, reads=(), writes=(), **kw):
        return self.op(q, lambda e: e.dma_start(out=out, in_=in_, **kw), reads, writes, dma=True)

    def fence(self):
        lasts = list(self.pend_dma)
        self.pend_dma = []
        for e in ENGS:
            for o in reversed(self.q[e]):
                if o.fn is not None and not o.dma:
                    o.sig = True
                    lasts.append(o)
                    break
        for e in ENGS:
            b = Op(e, None, False)
            b.deps = list(lasts)
            self.q[e].append(b)

    def emit(self):
        nc = self.nc
        with ExitStack() as st:
            esem = {e: st.enter_context(nc.semaphore("s_" + e)) for e in ENGS}
            dsem = {e: [st.enter_context(nc.semaphore("d_%s%d" % (e, i))) for i in range(NDMASEM)]
                    for e in ("sp", "pool", "act")}
            for e in ENGS:
                cnt = 0
                dcnt = [0] * NDMASEM
                di = 0
                for o in self.q[e]:
                    if o.dma:
                        k = di % NDMASEM
                        di += 1
                        dcnt[k] += 16
                        o.sem = dsem[e][k]
                        o.val = dcnt[k]
                    elif o.sig:
                        cnt += 1
                        o.sem = esem[e]
                        o.val = cnt
            block = st.enter_context(nc.Block())

            def run(eng_obj, e):
                waited = {}
                for o in self.q[e]:
                    if o.fn is None:
                        best = {}
                        for d in o.deps:
                            if id(d.sem) not in best or best[id(d.sem)].val < d.val:
                                best[id(d.sem)] = d
                        o.deps = list(best.values())
                    for d in o.deps:
                        key = id(d.sem)
                        if waited.get(key, 0) < d.val:
                            eng_obj.wait_ge(d.sem, d.val)
                            waited[key] = d.val
                    if o.fn is None:
                        continue
                    if o.dma and o.val > 16 and waited.get(id(o.sem), 0) < o.val - 16:
                        eng_obj.wait_ge(o.sem, o.val - 16)
                        waited[id(o.sem)] = o.val - 16
                    ins = o.fn(eng_obj)
                    if o.dma:
                        ins.then_inc(o.sem, 16)
                    elif o.sig:
                        ins.then_inc(o.sem, 1)

            @block.tensor
            def _(t):
                run(t, "pe")

            @block.scalar
            def _(a):
                run(a, "act")

            @block.vector
            def _(v):
                run(v, "dve")

            @block.gpsimd
            def _(g):
                run(g, "pool")

            @block.sync
            def _(s):
                run(s, "sp")


D = 4096
NP = 2048
NS = 128
N = NP + NS
NT = N // 128
DA = 2048
DFF = 11008
NH = 16
NKV = 4
HD = 128
NIH = 16
IDD = 64
PAST = 2048
TOPK = 256
EPS = 1e-6
THETA = 500000.0
U0, VA0, Q0, K0, V0, QI0, KI0, WI0, GA0, GB0, INC = 0, 2048, 4096, 6144, 6656, 7168, 8192, 8256, 8272, 12368, 16464
PJ_VA, PJ_Q, PJ_K, PJ_V, PJ_QI, PJ_KI, PJ_WI, PJW = 0, 2048, 4096, 4608, 5120, 6144, 6208, 6224
ARENA_F32 = 52992
NEG = -1.0e30
TOKG = [(0, 512), (512, 512), (1024, 512), (1536, 512), (2048, 128)]


class Arena:
    def __init__(self, ap):
        self.ap = ap
        self.off = 0

    def mark(self):
        return self.off

    def reset(self, m=0):
        self.off = m

    def alloc(self, shape_free, dtype):
        n = 1
        for s in shape_free:
            n *= s
        bpe = 2 if dtype == BF16 else 4
        nbytes = (n * bpe + 31) // 32 * 32
        assert self.off + nbytes <= ARENA_F32 * 4, ("arena overflow", self.off, nbytes)
        a = self.ap[:, self.off // 4:(self.off + nbytes) // 4]
        self.off += nbytes
        if dtype == BF16:
            a = a.bitcast(BF16)
        a = a[:, 0:n]
        if len(shape_free) == 2:
            a = a.rearrange("p (a b) -> p a b", a=shape_free[0])
        elif len(shape_free) == 3:
            a = a.rearrange("p (a b c) -> p a b c", a=shape_free[0], b=shape_free[1])
        return a


def build_program(dbg=None):
    dbg = dbg or {}
    stop = dbg.get("stop")
    dbg_outs = set(dbg.get("outs", ()))
    only = dbg.get("only")
    cut = dbg.get("cut", 99)

    def want(name):
        return only is None or name in only
    nc = bass.Bass("TRN2", target_bir_lowering=False)
    P = Prog(nc)

    def din(name, shape, dt=F32):
        return nc.dram_tensor(name, list(shape), dt, kind="ExternalInput").ap()

    def dout(name, shape, dt=F32):
        return nc.dram_tensor(name, list(shape), dt, kind="ExternalOutput").ap()

    def dscr(name, shape, dt):
        kind = "ExternalOutput" if name in dbg_outs else "Internal"
        return nc.dram_tensor(name, list(shape), dt, kind=kind).ap()

    xin = din("xin", [N, D])
    ck = din("ck", [4, PAST, 512])
    cv = din("cv", [4, PAST, 512])
    cki = din("cki", [4, PAST, IDD])
    cst = din("cst", [8, 2 * DFF])
    g_attn = din("g_attn", [D])
    w_in = din("w_in", [D, INC])
    g_gmlp = din("g_gmlp", [DA])
    ws = din("ws", [8, 128, 128])
    gbias = din("gbias", [8, 128])
    w_a = din("w_a", [DA, D])
    w_b = din("w_b", [DA, D])
    w_o = din("w_o", [D, D])
    g_ffn = din("g_ffn", [D])
    w_up = din("w_up", [D, 2 * DFF])
    conv_w = din("conv_w", [3, 2 * DFF])
    conv_b = din("conv_b", [2 * DFF])
    w_down = din("w_down", [DFF, D])
    g_final = din("g_final", [D])
    c_ident = din("c_ident", [128, 128])
    c_csq = din("c_csq", [N, 32])
    c_csi = din("c_csi", [N, 16])
    c_bd = din("c_bd", [128, 128])
    o_y = dout("o_y", [N, D])
    o_k = dout("o_k", [N, 512])
    o_v = dout("o_v", [N, 512])
    o_ki = dout("o_ki", [N, IDD])
    o_cst = dout("o_cst", [10, 2 * DFF])
    o_vn = dout("o_vn", [NS, DA])
    out_bufs = []
    projT = dscr("projT", [N, PJW], F32)
    uT = dscr("uT", [DA, N], BF16)
    gaT = dscr("gaT", [D, N], BF16)
    gbT = dscr("gbT", [D, N], BF16)
    yT = dscr("yT", [D, N], BF16)
    x2 = dscr("x2", [N, D], F32)
    actT = dscr("actT", [DFF, N], BF16)
    fsc = dscr("fsc", [N, D], F32)
    wdc_t = dscr("wdc", [32, 128, (DFF // 128) * 128], BF16)
    wdc = [wdc_t[c].rearrange("p (k j) -> p k j", j=128) for c in range(32)]
    Bwdc = [Buf("wdc%d" % c) for c in range(32)]
    aTd = dscr("aTd", [DA, N], BF16)
    bTd = dscr("bTd", [DA, N], BF16)
    B_projT, B_uT, B_gaT, B_gbT, B_yT, B_x2, B_actT, B_fsc, B_aTd, B_bTd = [Buf(n) for n in
        ("projT", "uT", "gaT", "gbT", "yT", "x2", "actT", "fsc", "aTd", "bTd")]

    arena_t = nc.alloc_sbuf_tensor("arena", [128, ARENA_F32], F32)
    AR = Arena(arena_t[:])
    banks = [nc.alloc_psum_tensor("bank%d" % i, [128, 512], F32) for i in range(8)]
    Bbank = [Buf("bank%d" % i) for i in range(8)]
    bankctr = [0]

    def next_bank():
        i = bankctr[0] % 8
        bankctr[0] += 1
        return banks[i], Bbank[i]

    evctr = [0]

    def evac_eng():
        evctr[0] += 1
        return "act" if evctr[0] % 2 else "dve"

    def copy_op(eng, out, in_):
        if eng == "act":
            return lambda e: e.activation(out=out, in_=in_, func=AF.Copy)
        return lambda e: e.tensor_copy(out=out, in_=in_)

    identf = AR.alloc([128], F32)
    identb = AR.alloc([128], BF16)
    B_ident = Buf("ident")
    P.dma("sp", identf, c_ident[:, :], writes=[B_ident])
    P.op("dve", lambda e: e.tensor_copy(out=identb, in_=identf), [B_ident], [B_ident])
    base_mark = AR.mark()

    def norm_transpose(src, Bsrc, gvec, hT, B_hT):
        m = AR.mark()
        g_bc = AR.alloc([D], F32)
        xt = [AR.alloc([D], F32) for _ in range(2)]
        hb = AR.alloc([D], BF16)
        junk = AR.alloc([D], BF16)
        st = AR.alloc([3 * NT], F32)
        Bg, Bxt, Bhb, Bjunk = Buf("g"), [Buf("xt0"), Buf("xt1")], Buf("hb"), Buf("junk")
        Bst = [Buf("st%d" % t) for t in range(NT)]
        P.dma("sp", g_bc, gvec.partition_broadcast(128), writes=[Bg])
        for t in range(NT):
            x_t, Bx = xt[t % 2], Bxt[t % 2]
            P.dma("sp", x_t, src[t * 128:(t + 1) * 128, :], reads=[Bsrc], writes=[Bx])
            ss, sd, rs = st[:, 3 * t:3 * t + 1], st[:, 3 * t + 1:3 * t + 2], st[:, 3 * t + 2:3 * t + 3]
            P.op("act", lambda e, x_t=x_t, ss=ss: e.activation(out=junk, in_=x_t, func=AF.Square, accum_out=ss),
                 [Bx], [Bjunk, Bst[t]])
            P.op("act", lambda e, ss=ss, sd=sd: e.activation(out=sd, in_=ss, func=AF.Sqrt, scale=1.0 / D, bias=EPS),
                 [Bst[t]], [Bst[t]])
            P.op("dve", lambda e, sd=sd, rs=rs: e.reciprocal(out=rs, in_=sd), [Bst[t]], [Bst[t]])
            P.op("dve", lambda e, x_t=x_t, rs=rs: e.scalar_tensor_tensor(out=hb, in0=x_t, scalar=rs, in1=g_bc,
                                                                     op0=ALU.mult, op1=ALU.mult),
                 [Bx, Bst[t], Bg], [Bhb])
            for q4 in range(4):
                bank, Bb = next_bank()
                pb = bank[:].bitcast(BF16)
                for j in range(8):
                    kc = q4 * 8 + j
                    P.op("pe", lambda e, pb=pb, j=j, kc=kc: e.transpose(out=pb[:, j * 128:(j + 1) * 128],
                                                                      in_=hb[:, kc * 128:(kc + 1) * 128],
                                                                      identity=identb),
                         [Bhb, B_ident], [Bb])
                eng = evac_eng()
                o_ap = hT[:, q4 * 8:(q4 + 1) * 8, t * 128:(t + 1) * 128]
                i_ap = pb.rearrange("p (a b) -> p a b", a=8)
                P.op(eng, copy_op(eng, o_ap, i_ap), [Bb], [B_hT])
        P.fence()
        AR.reset(m)

    def gemm(xT, B_xT, KC, W, blocks, epilogue_T=None, epilogue_F=None, tok_groups=TOKG, tiles=range(NT),
             wcols=512):
        m = AR.mark()
        wb = [AR.alloc([KC, wcols], BF16) for _ in range(2)]
        Bwb = [Buf("wb0"), Buf("wb1")]
        Wv = W.rearrange("(kc p) c -> p kc c", p=128)

        def load(i):
            c0, ncols, mode, tag = blocks[i]
            P.dma("pool", wb[i % 2][:, :, 0:ncols], Wv[:, :, c0:c0 + ncols], writes=[Bwb[i % 2]])

        load(0)
        for i, (c0, ncols, mode, tag) in enumerate(blocks):
            if i + 1 < len(blocks):
                load(i + 1)
            w_i, Bw = wb[i % 2], Bwb[i % 2]
            if mode == "T":
                for t in tiles:
                    bank, Bb = next_bank()
                    for kc in range(KC):
                        P.op("pe", lambda e, bank=bank, kc=kc, t=t, w_i=w_i, ncols=ncols:
                             e.matmul(bank[:, 0:ncols], lhsT=xT[:, kc, t * 128:(t + 1) * 128],
                                      rhs=w_i[:, kc, 0:ncols], start=(kc == 0), stop=(kc == KC - 1)),
                             [B_xT, Bw], [Bb])
                    epilogue_T(tag, c0, ncols, t, bank, Bb)
            else:
                for cj in range(ncols // 128):
                    for (tk0, ntk) in tok_groups:
                        bank, Bb = next_bank()
                        for kc in range(KC):
                            P.op("pe", lambda e, bank=bank, kc=kc, cj=cj, tk0=tk0, ntk=ntk, w_i=w_i:
                                 e.matmul(bank[:, 0:ntk], lhsT=w_i[:, kc, cj * 128:(cj + 1) * 128],
                                          rhs=xT[:, kc, tk0:tk0 + ntk], start=(kc == 0), stop=(kc == KC - 1)),
                                 [B_xT, Bw], [Bb])
                        epilogue_F(tag, c0 + cj * 128, tk0, ntk, bank, Bb)
        AR.reset(m)

    hT = AR.alloc([32, N], BF16)
    B_hT = Buf("hT")
    B_xin = Buf("xin")
    if want("s1"):
        norm_transpose(xin, B_xin, g_attn, hT, B_hT)

    def s2():
        m = AR.mark()
        stg = [AR.alloc([512], F32) for _ in range(2)]
        stgb = [AR.alloc([512], BF16) for _ in range(2)]
        Bstg = [Buf("stg0"), Buf("stg1")]
        Bstgb = [Buf("stgb0"), Buf("stgb1")]
        ctr = [0, 0]

        def epi_T(tag, c0, ncols, t, bank, Bb):
            i = ctr[0] % 2
            ctr[0] += 1
            eng = evac_eng()
            P.op(eng, copy_op(eng, stg[i][:, 0:ncols], bank[:, 0:ncols]), [Bb], [Bstg[i]])
            pc = c0 - VA0
            P.dma("sp", projT[t * 128:(t + 1) * 128, pc:pc + ncols], stg[i][:, 0:ncols], reads=[Bstg[i]],
                  writes=[B_projT])

        def epi_F(tag, c, tk0, ntk, bank, Bb):
            i = ctr[1] % 2
            ctr[1] += 1
            if tag == "u":
                eng = evac_eng()
                P.op(eng, copy_op(eng, stgb[i][:, 0:ntk], bank[:, 0:ntk]), [Bb], [Bstgb[i]])
                dst, Bd, r0 = uT, B_uT, c - U0
            else:
                P.op("act", lambda e, i=i, ntk=ntk, bank=bank: e.activation(out=stgb[i][:, 0:ntk], in_=bank[:, 0:ntk],
                                                                          func=AF.Sigmoid), [Bb], [Bstgb[i]])
                if tag == "ga":
                    dst, Bd, r0 = gaT, B_gaT, c - GA0
                else:
                    dst, Bd, r0 = gbT, B_gbT, c - GB0
            P.dma("sp", dst[r0:r0 + 128, tk0:tk0 + ntk], stgb[i][:, 0:ntk], reads=[Bstgb[i]], writes=[Bd])

        blocks = []
        for j in range(4):
            blocks.append((VA0 + 512 * j, 512, "T", "va"))
        for j in range(8):
            blocks.append((Q0 + 512 * j, 512, "T", "qkv"))
        blocks.append((KI0, 80, "T", "kiwi"))
        for j in range(4):
            blocks.append((U0 + 512 * j, 512, "F", "u"))
        for j in range(8):
            blocks.append((GA0 + 512 * j, 512, "F", "ga"))
        for j in range(8):
            blocks.append((GB0 + 512 * j, 512, "F", "gb"))
        gemm(hT, B_hT, 32, w_in, blocks, epi_T, epi_F)
        P.fence()
        AR.reset(m)

    if want("s2"):
        s2()
    if stop == "s2":
        return finish(nc, P, out_bufs)

    def bias4(bias_bc, bi, q4):
        b = bias_bc[:, bi, q4 * 256:(q4 + 1) * 256].rearrange("p (g i) -> p g i", g=2)
        return b.unsqueeze(2).to_broadcast([128, 2, 2, 128])

    def s3():
        AR.reset(base_mark)
        gn_bc = AR.alloc([DA], F32)
        Bgn = Buf("gn")
        P.dma("sp", gn_bc, g_gmlp.partition_broadcast(128), writes=[Bgn])
        wn = AR.alloc([8, 128], F32)
        wns = AR.alloc([8, 128], F32)
        wnb = AR.alloc([8, 128], BF16)
        wnsb = AR.alloc([8, 128], BF16)
        wmT = AR.alloc([8, 128], BF16)
        wmTs = AR.alloc([8, 128], BF16)
        bias_bc = AR.alloc([2, 1024], F32)
        stmp = AR.alloc([512], F32)
        Bstmp = Buf("stmp")
        Bwn, Bwns, Bwmt, Bbr = Buf("wn"), Buf("wns"), Buf("wmT"), Buf("br")
        P.dma("sp", wn, ws.rearrange("g i j -> i g j"), writes=[Bwn])
        P.op("dve", lambda e: e.memset(wn[0:64, :, 64:128], 0.0), [], [Bwn])
        P.op("dve", lambda e: e.tensor_copy(out=wnb, in_=wn), [Bwn], [Bwn])
        P.op("pool", lambda e: e.memset(wns, 0.0), [], [Bwns])
        for s_ in range(4):
            P.dma("sp", wns[32 * s_:32 * s_ + 32, :, 32 * s_:32 * s_ + 32],
                  ws[:, 0:32, 0:32].rearrange("g i j -> i g j"), writes=[Bwns])
        P.op("pool", lambda e: e.tensor_copy(out=wnsb, in_=wns), [Bwns], [Bwns])
        for (src_b, dst, Bs) in ((wnb, wmT, Bwn), (wnsb, wmTs, Bwns)):
            bank, Bb = next_bank()
            pb = bank[:].bitcast(BF16)
            for g in range(8):
                P.op("pe", lambda e, pb=pb, g=g, src_b=src_b: e.transpose(out=pb[:, g * 128:(g + 1) * 128],
                                                                        in_=src_b[:, g, :], identity=identb),
                     [Bs, B_ident], [Bb])
            P.op("dve", lambda e, pb=pb, dst=dst: e.tensor_copy(out=dst, in_=pb.rearrange("p (a b) -> p a b", a=8)),
                 [Bb], [Bwmt])
        P.dma("sp", bias_bc[:, 0, :], gbias.rearrange("g i -> (g i)").partition_broadcast(128), writes=[Bbr])
        for s_ in range(4):
            P.op("dve", lambda e, s_=s_: e.tensor_copy(
                out=bias_bc[:, 1, :].rearrange("p (g i) -> p g i", g=8)[:, :, 32 * s_:32 * s_ + 32],
                in_=bias_bc[:, 0, :].rearrange("p (g i) -> p g i", g=8)[:, :, 0:32]), [Bbr], [Bbr])
        if cut == 1:
            P.fence()
            return

        va = [AR.alloc([DA], F32) for _ in range(2)]
        Bva = [Buf("va0"), Buf("va1")]
        vnb = AR.alloc([DA], BF16)
        vnf = AR.alloc([DA], F32)
        junk = AR.alloc([DA], BF16)
        Bvnb, Bvnf, Bjunk = Buf("vnb"), Buf("vnf"), Buf("junk3")
        st = AR.alloc([3 * NT], F32)
        Bst = [Buf("st3_%d" % t) for t in range(NT)]
        uTs = [AR.alloc([16, 512], BF16) for _ in range(2)]
        BuTs = [Buf("uTs0"), Buf("uTs1")]
        aTg = [AR.alloc([16, 512], BF16) for _ in range(2)]
        BaTg = [Buf("aTg0"), Buf("aTg1")]
        uTv = uT.rearrange("(dc p) n -> p dc n", p=128)
        aTv = aTd.rearrange("(dc p) n -> p dc n", p=128)
        for gi, (tk0, ntk) in enumerate(TOKG):
            u_g, Bu = uTs[gi % 2], BuTs[gi % 2]
            a_g, Ba = aTg[gi % 2], BaTg[gi % 2]
            P.dma("sp", u_g[:, :, 0:ntk], uTv[:, :, tk0:tk0 + ntk], reads=[B_uT], writes=[Bu])
            for tl in range(ntk // 128):
                t = tk0 // 128 + tl
                v_t, Bv = va[t % 2], Bva[t % 2]
                P.dma("sp", v_t, projT[t * 128:(t + 1) * 128, PJ_VA:PJ_VA + DA], reads=[B_projT], writes=[Bv])
                ss, sd, rs = st[:, 3 * t:3 * t + 1], st[:, 3 * t + 1:3 * t + 2], st[:, 3 * t + 2:3 * t + 3]
                P.op("act", lambda e, v_t=v_t, ss=ss: e.activation(out=junk, in_=v_t, func=AF.Square, accum_out=ss),
                     [Bv], [Bjunk, Bst[t]])
                P.op("act", lambda e, ss=ss, sd=sd: e.activation(out=sd, in_=ss, func=AF.Sqrt, scale=1.0 / DA, bias=EPS),
                     [Bst[t]], [Bst[t]])
                P.op("dve", lambda e, sd=sd, rs=rs: e.reciprocal(out=rs, in_=sd), [Bst[t]], [Bst[t]])
                P.op("dve", lambda e, v_t=v_t, rs=rs: e.scalar_tensor_tensor(out=vnb, in0=v_t, scalar=rs, in1=gn_bc,
                                                                         op0=ALU.mult, op1=ALU.mult),
                     [Bv, Bst[t], Bgn], [Bvnb])
                if t == NT - 1:
                    P.op("dve", lambda e, v_t=v_t, rs=rs: e.scalar_tensor_tensor(out=vnf, in0=v_t, scalar=rs, in1=gn_bc,
                                                                             op0=ALU.mult, op1=ALU.mult),
                         [Bv, Bst[t], Bgn], [Bvnf])
                    ob = Buf("o_vn")
                    out_bufs.append(ob)
                    P.dma("sp", o_vn[:, :], vnf, reads=[Bvnf], writes=[ob])
                wm_t = wmTs if t == NT - 1 else wmT
                bi = 1 if t == NT - 1 else 0
                for q4 in range(4 if cut > 2 else 0):
                    bank, Bb = next_bank()
                    for j in range(4):
                        dc = q4 * 4 + j
                        g = dc // 2
                        P.op("pe", lambda e, bank=bank, j=j, dc=dc, g=g, wm_t=wm_t:
                             e.matmul(bank[:, j * 128:(j + 1) * 128], lhsT=vnb[:, dc * 128:(dc + 1) * 128],
                                      rhs=wm_t[:, g, :], start=True, stop=True),
                             [Bvnb, Bwmt], [Bb])
                    if cut == 3:
                        continue
                    P.op("dve", lambda e, bank=bank, q4=q4, bi=bi: e.tensor_tensor(
                        out=stmp.rearrange("p (g r i) -> p g r i", g=2, r=2),
                        in0=bank[:, 0:512].rearrange("p (g r i) -> p g r i", g=2, r=2),
                        in1=bias4(bias_bc, bi, q4), op=ALU.add),
                        [Bb, Bbr], [Bstmp])
                    if cut == 4:
                        continue
                    P.op("dve", lambda e, q4=q4, tl=tl, a_g=a_g, u_g=u_g:
                         e.tensor_tensor(out=a_g[:, q4 * 4:(q4 + 1) * 4, tl * 128:(tl + 1) * 128],
                                         in0=stmp.rearrange("p (a b) -> p a b", a=4),
                                         in1=u_g[:, q4 * 4:(q4 + 1) * 4, tl * 128:(tl + 1) * 128], op=ALU.mult),
                         [Bstmp, Bu], [Ba])
            P.dma("sp", aTv[:, :, tk0:tk0 + ntk], a_g[:, :, 0:ntk], reads=[Ba], writes=[B_aTd])
        P.fence()

    if want("s3"):
        s3()
    if stop == "s3":
        return finish(nc, P, out_bufs)

    def s4():
        AR.reset(base_mark)
        SCALE = float(HD) ** -0.5
        MNEG = -30000.0
        WSC = float(NIH) ** -0.5 * float(IDD) ** -0.5
        RAWW = PJW - PJ_Q
        KT0, Vb0, kiT20 = AR.alloc([4, NP], BF16), AR.alloc([16, 512], BF16), AR.alloc([NP], BF16)
        m_s = AR.mark()
        KT1, Vb1, kiT21 = AR.alloc([4, NP], BF16), AR.alloc([16, 512], BF16), AR.alloc([NP], BF16)
        ckb = AR.alloc([16, 512], BF16)
        ckib = AR.alloc([16, 128], BF16)
        m_e = AR.mark()
        AR.reset(m_s)
        raw_b, qkb_b, qib_b, kib_b, wsm_b = (AR.alloc([RAWW], F32), AR.alloc([2560], BF16), AR.alloc([1024], BF16),
                                             AR.alloc([128], BF16), AR.alloc([16 * 6], F32))
        assert AR.mark() <= m_e
        AR.reset(m_e)
        KT, Vb, kiT2 = [KT0, KT1], [Vb0, Vb1], [kiT20, kiT21]
        BKT = [Buf("KT0"), Buf("KT1")]
        BVb = [Buf("Vb0"), Buf("Vb1")]
        BkiT = [Buf("kiT0"), Buf("kiT1")]
        Bckb, Bckib = Buf("ckb"), Buf("ckib")
        csq = AR.alloc([NT, 32], F32)
        csi = AR.alloc([NT, 16], F32)
        bd = AR.alloc([128], F32)
        Bcs = Buf("cs")
        P.dma("sp", csq, c_csq.rearrange("(t p) c -> p t c", p=128), writes=[Bcs])
        P.dma("sp", csi, c_csi.rearrange("(t p) c -> p t c", p=128), writes=[Bcs])
        P.dma("sp", bd, c_bd[:, :], writes=[Bcs])
        ones_b = AR.alloc([128], BF16)
        diagT = AR.alloc([128], BF16)
        P.op("dve", lambda e: e.memset(ones_b, 1.0), [], [Bcs])
        P.op("dve", lambda e: e.memset(diagT, 0.0), [], [Bcs])
        P.op("dve", lambda e: e.memset(diagT[0:64, 64:128], MNEG), [], [Bcs])
        I4 = AR.alloc([4, 128], BF16)
        for h_ in range(4):
            P.op("dve", lambda e, h_=h_: e.tensor_copy(out=I4[:, h_, :], in_=identb), [B_ident], [Bcs])
        raw2 = [AR.alloc([RAWW], F32), raw_b]
        Braw2 = [Buf("raw0"), Buf("raw1")]
        tmp = [AR.alloc([20 * 16], F32) for _ in range(4)]
        Btmp = Buf("ropetmp")
        qkb2 = [AR.alloc([2560], BF16), qkb_b]
        qib2 = [AR.alloc([1024], BF16), qib_b]
        kib2 = [AR.alloc([128], BF16), kib_b]
        Bqkb2, Bqib2, Bkib2 = [Buf("qkb0"), Buf("qkb1")], [Buf("qib0"), Buf("qib1")], [Buf("kib0"), Buf("kib1")]
        qT2 = [AR.alloc([16, 128], BF16) for _ in range(2)]
        qiT = AR.alloc([8, 128], BF16)
        kTs = AR.alloc([4, 128], BF16)
        Vnew = AR.alloc([512], BF16)
        kiTs = AR.alloc([128], BF16)
        BqT2 = [Buf("qT0"), Buf("qT1")]
        BqiT, BkTs, BVnew, BkiTs = Buf("qiT"), Buf("kTs"), Buf("Vnew"), Buf("kiTs")
        wsm2 = [AR.alloc([16 * 6], F32), wsm_b]
        Bw2 = [Buf("wsm0"), Buf("wsm1")]
        wsm, Bw = wsm2[0], Bw2[0]
        S2 = [AR.alloc([2080], F32) for _ in range(2)]
        S = S2[0]
        mb2 = [AR.alloc([2080], BF16) for _ in range(2)]
        Bmb2 = [Buf("mb0"), Buf("mb1")]
        mx8 = AR.alloc([8], F32)
        BS2 = [Buf("S0"), Buf("S1")]
        BS, Bmx = BS2[0], Buf("mx8")
        rtmp = [AR.alloc([512], F32) for _ in range(2)]
        Brtmp = [Buf("rtmp%d" % i) for i in range(2)]
        rtb = [AR.alloc([512], BF16) for _ in range(8)]
        Brtb = [Buf("rtb%d" % i) for i in range(8)]
        dg = AR.alloc([16, 128], BF16)
        Bdg = Buf("dg")
        PT = [AR.alloc([512], BF16) for _ in range(5)]
        BPT = [Buf("PT%d" % i) for i in range(5)]
        osb = [AR.alloc([512], F32) for _ in range(2)]
        dsb = [AR.alloc([512], F32) for _ in range(2)]
        Bosb = [Buf("osb0"), Buf("osb1")]
        Bdsb = [Buf("dsb0"), Buf("dsb1")]
        nrm = [0]

        bTg = [AR.alloc([16, 128], BF16) for _ in range(2)]
        BbTg = [Buf("bTg0"), Buf("bTg1")]
        snew = AR.alloc([128], F32)
        snew2 = AR.alloc([128], F32)
        mnew = AR.alloc([128], BF16)
        Bsnew, Bmnew = Buf("snew"), Buf("mnew")
        bTv = bTd.rearrange("(dc p) n -> p dc n", p=128)
        stb = [(banks[i], Bbank[i]) for i in (0, 1, 2, 6, 7)]
        sc_bank, Bsc = banks[3], Bbank[3]
        ob_ = [(banks[4], Bbank[4]), (banks[4], Bbank[4])]
        db_ = [(banks[5], Bbank[5]), (banks[5], Bbank[5])]
        LOOK = 4
        ctr = {"st": 0, "pt": 0, "rt": 0, "mm": 0}

        def nb():
            ctr["st"] += 1
            return stb[ctr["st"] % 5]

        def prep_a(t):
            raw, Braw = raw2[t % 2], Braw2[t % 2]
            qkb, Bqkb, qib, Bqib, kib, Bkib = qkb2[t % 2], Bqkb2[t % 2], qib2[t % 2], Bqib2[t % 2], kib2[t % 2], Bkib2[t % 2]
            wsm, Bw = wsm2[t % 2], Bw2[t % 2]
            P.dma("sp", raw, projT[t * 128:(t + 1) * 128, PJ_Q:PJW], reads=[B_projT], writes=[Braw])
            for (o0, nh, hd, half, cs_t) in ((0, 20, 128, 16, csq), (3072, 17, 64, 8, csi)):
                v3 = raw[:, o0:o0 + nh * hd].rearrange("p (h d) -> p h d", h=nh)
                x1, x2_ = v3[:, :, 0:half], v3[:, :, half:2 * half]
                cos = cs_t[:, t, 0:half].unsqueeze(1).to_broadcast([128, nh, half])
                sin = cs_t[:, t, half:2 * half].unsqueeze(1).to_broadcast([128, nh, half])
                tt = [tm[:, 0:nh * half].rearrange("p (h d) -> p h d", h=nh) for tm in tmp]
                P.op("pool", lambda e, tt=tt, x1=x1, cos=cos: e.tensor_tensor(out=tt[0], in0=x1, in1=cos, op=ALU.mult),
                     [Braw, Bcs], [Btmp])
                P.op("pool", lambda e, tt=tt, x2_=x2_, sin=sin: e.tensor_tensor(out=tt[1], in0=x2_, in1=sin, op=ALU.mult),
                     [Braw, Bcs], [Btmp])
                P.op("pool", lambda e, tt=tt, x2_=x2_, cos=cos: e.tensor_tensor(out=tt[2], in0=x2_, in1=cos, op=ALU.mult),
                     [Braw, Bcs], [Btmp])
                P.op("pool", lambda e, tt=tt, x1=x1, sin=sin: e.tensor_tensor(out=tt[3], in0=x1, in1=sin, op=ALU.mult),
                     [Braw, Bcs], [Btmp])
                P.op("pool", lambda e, tt=tt, x1=x1: e.tensor_tensor(out=x1, in0=tt[0], in1=tt[1], op=ALU.subtract),
                     [Btmp], [Braw])
                P.op("pool", lambda e, tt=tt, x2_=x2_: e.tensor_tensor(out=x2_, in0=tt[2], in1=tt[3], op=ALU.add),
                     [Btmp], [Braw])
            for (dst, c0, w) in ((o_k, 2048, 512), (o_v, 2560, 512), (o_ki, 4096, 64)):
                ob = Buf("ok")
                out_bufs.append(ob)
                P.dma("sp", dst[t * 128:(t + 1) * 128, :], raw[:, c0:c0 + w], reads=[Braw], writes=[ob])
            P.op("act", lambda e: e.activation(out=qkb, in_=raw[:, 0:2560], func=AF.Copy), [Braw], [Bqkb])
            wi = raw[:, 4160:4176]
            P.op("act", lambda e: e.activation(out=wsm[:, 0:16], in_=wi, func=AF.Abs, scale=WSC), [Braw], [Bw])
            P.op("act", lambda e: e.activation(out=wsm[:, 16:32], in_=wi, func=AF.Sign), [Braw], [Bw])
            P.op("pool", lambda e: e.tensor_tensor(out=qib.rearrange("p (h d) -> p h d", h=16),
                                                  in0=raw[:, 3072:4096].rearrange("p (h d) -> p h d", h=16),
                                                  in1=wsm[:, 0:16].unsqueeze(2).to_broadcast([128, 16, 64]),
                                                  op=ALU.mult), [Braw, Bw], [Bqib])
            P.op("pool", lambda e: e.tensor_copy(out=kib.rearrange("p (a d) -> p a d", a=2),
                                                in_=raw[:, 4096:4160].unsqueeze(1).to_broadcast([128, 2, 64])),
                 [Braw], [Bkib])

        def prep_b(t, kv):
            raw, Braw = raw2[t % 2], Braw2[t % 2]
            qkb, Bqkb, qib, Bqib, kib, Bkib = qkb2[t % 2], Bqkb2[t % 2], qib2[t % 2], Bqib2[t % 2], kib2[t % 2], Bkib2[t % 2]
            sample = (t == NT - 1)
            for b0 in (0, 8, 16):
                nb_ = min(8, 20 - b0)
                bank, Bb = nb()
                pb = bank[:].bitcast(BF16)
                for j in range(nb_):
                    hh = b0 + j
                    P.op("pe", lambda e, pb=pb, j=j, hh=hh: e.transpose(out=pb[:, j * 128:(j + 1) * 128],
                                                                      in_=qkb[:, hh * 128:(hh + 1) * 128],
                                                                      identity=identb), [Bqkb, B_ident], [Bb])
                if b0 < 16:
                    P.op("act", lambda e, pb=pb, b0=b0, t=t: e.activation(out=qT2[t % 2][:, b0:b0 + 8, :],
                                                                         in_=pb.rearrange("p (a b) -> p a b", a=8),
                                                                         func=AF.Copy), [Bb], [BqT2[t % 2]])
                else:
                    src3 = pb[:, 0:512].rearrange("p (a b) -> p a b", a=4)
                    if sample:
                        P.op("act", lambda e, src3=src3: e.activation(out=kTs, in_=src3, func=AF.Copy), [Bb], [BkTs])
                    else:
                        P.op("act", lambda e, src3=src3, kv=kv, t=t: e.activation(
                            out=KT[kv][:, :, t * 128:(t + 1) * 128], in_=src3, func=AF.Copy), [Bb], [BKT[kv]])
            bank, Bb = nb()
            pb = bank[:].bitcast(BF16)
            for j in range(8):
                P.op("pe", lambda e, pb=pb, j=j: e.transpose(out=pb[:, j * 128:(j + 1) * 128],
                                                            in_=qib[:, j * 128:(j + 1) * 128], identity=identb),
                     [Bqib, B_ident], [Bb])
            P.op("dve", lambda e, pb=pb: e.tensor_copy(out=qiT, in_=pb.rearrange("p (a b) -> p a b", a=8)),
                 [Bb], [BqiT])
            bank, Bb = nb()
            pb = bank[:].bitcast(BF16)
            P.op("pe", lambda e, pb=pb: e.transpose(out=pb[:, 0:128], in_=kib, identity=identb), [Bkib, B_ident], [Bb])
            if sample:
                P.op("dve", lambda e, pb=pb: e.tensor_copy(out=kiTs, in_=pb[:, 0:128]), [Bb], [BkiTs])
                P.op("dve", lambda e: e.tensor_copy(out=Vnew, in_=raw[:, 2560:3072]), [Braw], [BVnew])
            else:
                P.op("dve", lambda e, pb=pb, kv=kv, t=t: e.tensor_copy(out=kiT2[kv][:, t * 128:(t + 1) * 128],
                                                                     in_=pb[:, 0:128]), [Bb], [BkiT[kv]])
                P.op("dve", lambda e, kv=kv, t=t: e.tensor_copy(out=Vb[kv][:, t, :], in_=raw[:, 2560:3072]),
                     [Braw], [BVb[kv]])


        def prep_tile(t, kv):
            prep_a(t)
            prep_b(t, kv)

        def score_block(kiT_ap, BkiTb, k0, w, sg_ap_fn, first, dst, Bdst, pe_acc=False):
            zb = [None] * 16

            def emit_z(h):
                bank, Bb = nb()
                zb[h] = (bank, Bb)
                r0 = (h % 2) * 64
                P.op("pe", lambda e, bank=bank, h=h, r0=r0, k0=k0, w=w:
                     e.matmul(bank[:, 0:w], lhsT=qiT[r0:r0 + 64, h // 2, :], rhs=kiT_ap[r0:r0 + 64, k0:k0 + w],
                              start=True, stop=True), [BqiT, BkiTb], [Bb])

            LZ = 4 if pe_acc else 3
            for h in range(LZ):
                emit_z(h)
            for h in range(16):
                if h + LZ < 16:
                    emit_z(h + LZ)
                bank, Bb = zb[h]
                if pe_acc:
                    P.op("act", lambda e, bank=bank, h=h, w=w: e.activation(out=rtb[h % 8][:, 0:w], in_=bank[:, 0:w],
                                                                          func=AF.Relu), [Bb], [Brtb[h % 8]])
                    P.op("pe", lambda e, w=w, h=h: e.matmul(sc_bank[:, 0:w], lhsT=dg[:, h, :], rhs=rtb[h % 8][:, 0:w],
                                                           start=(h == 0), stop=(h == 15)), [Bdg, Brtb[h % 8]], [Bsc])
                    continue
                ctr["rt"] += 1
                ri = ctr["rt"] % 2
                P.op("act", lambda e, bank=bank, ri=ri, w=w: e.activation(out=rtmp[ri][:, 0:w], in_=bank[:, 0:w],
                                                                        func=AF.Relu), [Bb], [Brtmp[ri]])
                sg = sg_ap_fn(h)
                if first and h == 0:
                    P.op("dve", lambda e, ri=ri, w=w, sg=sg, dst=dst: e.tensor_scalar(
                        out=dst[:, 0:w], in0=rtmp[ri][:, 0:w], scalar1=sg, scalar2=None, op0=ALU.mult),
                        [Brtmp[ri], Bw], [Bdst])
                else:
                    P.op("dve", lambda e, ri=ri, w=w, sg=sg, dst=dst: e.scalar_tensor_tensor(
                        out=dst[:, 0:w], in0=rtmp[ri][:, 0:w], scalar=sg, in1=dst[:, 0:w], op0=ALU.mult, op1=ALU.add),
                        [Brtmp[ri], Bw, Bdst], [Bdst])
            if pe_acc:
                P.op("act", lambda e, w=w, dst=dst: e.activation(out=dst[:, 0:w], in_=sc_bank[:, 0:w], func=AF.Copy),
                     [Bsc], [Bdst])

        def topk_mask(L, first_chunk_cut, mi, S=None, BS=None):
            mb, Bm = mb2[mi], Bmb2[mi]
            if S is None:
                S, BS = S2[0], BS2[0]
            if first_chunk_cut:
                P.op("dve", lambda e: e.memset(S[0:64, L - 64:L], -3.0e30), [], [BS])
            for r in range(TOPK // 8):
                P.op("dve", lambda e: e.max(out=mx8, in_=S[:, 0:L]), [BS], [Bmx])
                P.op("dve", lambda e: e.match_replace(out=S[:, 0:L], in_to_replace=mx8, in_values=S[:, 0:L],
                                                      imm_value=NEG), [BS, Bmx], [BS])
            P.op("dve", lambda e: e.tensor_scalar(out=mb[:, 0:L], in0=S[:, 0:L], scalar1=-5.0e29, scalar2=MNEG,
                                                  op0=ALU.is_gt, op1=ALU.mult), [BS], [Bm])
            if first_chunk_cut:
                P.op("dve", lambda e: e.memset(mb[0:64, L - 64:L], MNEG), [], [Bm])

        mulctr = [0]

        def attn_group(steps, o_flat, d_ap, Bo, Bd, out_ap, Bout):
            nst = len(steps)
            stbank = [None] * nst

            def emit_st(i):
                s_ = steps[i]
                bank, Bb = nb()
                stbank[i] = (bank, Bb)
                hasb = s_["bias"] is not None
                P.op("pe", lambda e, bank=bank, s_=s_, hasb=hasb: e.matmul(s_["st_out"](bank), lhsT=s_["lhsT_k"],
                                                                         rhs=s_["rhs_q"], start=True, stop=not hasb),
                     [s_["Bk"], s_["Bq"]], [Bb])
                if hasb:
                    P.op("pe", lambda e, bank=bank, s_=s_: e.matmul(s_["st_out"](bank), lhsT=s_["bias"],
                                                                  rhs=s_["bias_rhs"], start=False, stop=True),
                         [s_["Bbias"], Bcs], [Bb])

            for i in range(min(LOOK, nst)):
                emit_st(i)
            for i in range(nst):
                if i + LOOK < nst:
                    emit_st(i + LOOK)
                s_ = steps[i]
                bank, Bb = stbank[i]
                w = s_["st_w"]
                ctr["pt"] += 1
                pi = ctr["pt"] % 5
                pt = PT[pi][:, 0:w]
                P.op("act", lambda e, bank=bank, pt=pt, w=w: e.activation(out=pt, in_=bank[:, 0:w], func=AF.Exp,
                                                                        scale=SCALE), [Bb], [BPT[pi]])
                P.op("pe", lambda e, s_=s_, pt=pt, i=i: e.matmul(s_["pv_out"], lhsT=s_["lhsT_v"], rhs=s_["pv_rhs"](pt),
                                                              start=(i == 0), stop=(i == nst - 1)),
                     [s_["Bv"], BPT[pi]], [Bo])
                P.op("pe", lambda e, s_=s_, pt=pt, i=i: e.matmul(s_["pd_out"], lhsT=ones_b, rhs=s_["pv_rhs"](pt),
                                                              start=(i == 0), stop=(i == nst - 1)),
                     [Bcs, BPT[pi]], [Bd])
            wtot = steps[0]["st_w"]
            nrm[0] += 1
            k = nrm[0] % 2
            rv = steps[0]["rec_view"]
            P.op("act", lambda e, k=k: e.activation(out=osb[k][:, 0:wtot], in_=o_flat, func=AF.Copy), [Bo], [Bosb[k]])
            P.op("act", lambda e, k=k: e.activation(out=dsb[k][:, 0:wtot], in_=d_ap, func=AF.Ln), [Bd], [Bdsb[k]])
            P.op("act", lambda e, k=k: e.activation(out=dsb[k][:, 0:wtot], in_=dsb[k][:, 0:wtot], func=AF.Exp,
                                                    scale=-1.0), [Bdsb[k]], [Bdsb[k]])
            P.op("pool", lambda e, k=k: e.tensor_tensor(out=out_ap, in0=rv(osb[k][:, 0:wtot]), in1=rv(dsb[k][:, 0:wtot]),
                                                        op=ALU.mult), [Bosb[k], Bdsb[k]], [Bout])

        def score_topk(t):
            L = 128 * (t + 1)
            if t >= 2:
                wsm_t, Bw_t = wsm2[t % 2], Bw2[t % 2]
                for h in range(16):
                    P.op("pool", lambda e, h=h, wsm_t=wsm_t: e.tensor_scalar(out=dg[:, h, :], in0=identf,
                                                                            scalar1=wsm_t[:, 16 + h:17 + h],
                                                                            scalar2=None, op0=ALU.mult),
                         [Bw_t, B_ident], [Bdg])
                for k0 in range(0, L, 512):
                    w = min(512, L - k0)
                    score_block(kiT2[0], BkiT[0], k0, w, None, True, S2[t % 2][:, k0:k0 + w], BS2[t % 2], pe_acc=True)
                topk_mask(L, True, t % 2, S2[t % 2], BS2[t % 2])

        def attention_prompt(t):
            use_topk = t >= 2
            b_g, Bbg = bTg[t % 2], BbTg[t % 2]
            for n in range(4):
                (o_bank, Bo), (d_bank, Bd) = ob_[n % 2], db_[n % 2]
                steps = []
                for kt in range(t + 1):
                    if use_topk:
                        mk, Bmk = mb2[t % 2][:, kt * 128:(kt + 1) * 128], Bmb2[t % 2]
                    else:
                        mk, Bmk = (diagT, Bcs) if kt == t else (None, None)
                    steps.append(dict(
                        bias=mk, Bbias=Bmk, bias_rhs=I4,
                        lhsT_k=KT[0][:, n, kt * 128:(kt + 1) * 128], Bk=BKT[0],
                        rhs_q=qT2[t % 2][:, 4 * n:4 * n + 4, :], Bq=BqT2[t % 2], st_w=512,
                        st_out=lambda bank: bank[:, 0:512].rearrange("p (h q) -> p h q", h=4),
                        lhsT_v=Vb[0][:, kt, n * 128:(n + 1) * 128], Bv=BVb[0],
                        pv_out=o_bank[:, 0:512], pd_out=d_bank[:, 0:512], pv_rhs=lambda pt: pt,
                        rec_view=lambda r: r.rearrange("p (h q) -> p h q", h=4)))
                attn_group(steps, o_bank[:, 0:512], d_bank[:, 0:512], Bo, Bd,
                           b_g[:, 4 * n:4 * n + 4, :], Bbg)
            P.dma("sp", bTv[:, :, t * 128:(t + 1) * 128], b_g, reads=[Bbg], writes=[B_bTd])

        prep_a(0)
        prep_a(1)
        prep_b(0, 0)
        for t in range(16):
            if t + 1 < 16:
                prep_b(t + 1, 0)
                score_topk(t + 1)
            if t + 2 < 16:
                prep_a(t + 2)
            attention_prompt(t)

        P.fence()
        t = NT - 1
        prep_tile(t, 0)
        for s_ in range(4):
            P.op("dve", lambda e, s_=s_: e.tensor_scalar(out=wsm[:, 32 + 16 * s_:48 + 16 * s_], in0=wsm[:, 16:32],
                                                         scalar1=bd[:, 32 * s_:32 * s_ + 1], scalar2=None,
                                                         op0=ALU.mult), [Bw, Bcs], [Bw])
        for s_ in range(4):
            kv = s_ % 2
            ckv = cki[s_].rearrange("(kt p) c -> p kt c", p=128)
            P.dma("pool", ckib[:, :, 0:64], ckv, writes=[Bckib])
            P.dma("pool", ckib[:, :, 64:128], ckv, writes=[Bckib])
            for b0 in (0, 8):
                bank, Bb = nb()
                pb = bank[:].bitcast(BF16)
                for j in range(8):
                    P.op("pe", lambda e, pb=pb, j=j, b0=b0: e.transpose(out=pb[:, j * 128:(j + 1) * 128],
                                                                      in_=ckib[:, b0 + j, :], identity=identb),
                         [Bckib, B_ident], [Bb])
                P.op("dve", lambda e, pb=pb, b0=b0, kv=kv: e.tensor_copy(out=kiT2[kv][:, b0 * 128:(b0 + 8) * 128],
                                                                       in_=pb), [Bb], [BkiT[kv]])
            for k0 in range(0, PAST, 512):
                score_block(kiT2[kv], BkiT[kv], k0, 512,
                            lambda h, s_=s_: wsm[:, 32 + 16 * s_ + h:33 + 16 * s_ + h], s_ == 0, S[:, k0:k0 + 512], BS)
        score_block(kiTs, BkiTs, 0, 128, lambda h: wsm[:, 16 + h:17 + h], True, snew, Bsnew)
        P.op("dve", lambda e: e.tensor_tensor(out=snew2, in0=snew, in1=bd, op=ALU.mult), [Bsnew, Bcs], [Bsnew])
        P.op("dve", lambda e: e.tensor_reduce(out=S[:, PAST:PAST + 32], in_=snew2.rearrange("p (s j) -> p j s", s=4),
                                              axis=AX.X, op=ALU.add), [Bsnew], [BS])
        topk_mask(PAST + 32, False, 0)
        mbs, Bmbs = mb2[0], Bmb2[0]
        m3 = mnew.rearrange("p (s j) -> p s j", s=4)
        P.op("dve", lambda e: e.tensor_scalar(out=snew.rearrange("p (s j) -> p s j", s=4),
                                              in0=mbs[:, PAST:PAST + 32].unsqueeze(1).to_broadcast([128, 4, 32]),
                                              scalar1=-MNEG, scalar2=None, op0=ALU.add), [Bmbs], [Bsnew])
        P.op("dve", lambda e: e.tensor_tensor(out=snew2, in0=snew, in1=bd, op=ALU.mult), [Bsnew, Bcs], [Bsnew])
        P.op("dve", lambda e: e.tensor_scalar(out=mnew, in0=snew2, scalar1=MNEG, scalar2=None, op0=ALU.add),
             [Bsnew], [Bmnew])
        b_g, Bbg = bTg[0], BbTg[0]
        for s_ in range(4):
            kv = s_ % 2
            P.dma("pool", ckb, ck[s_].rearrange("(kt p) c -> p kt c", p=128), writes=[Bckb])
            P.dma("pool", Vb[kv], cv[s_].rearrange("(kt p) c -> p kt c", p=128), writes=[BVb[kv]])
            for kt2 in range(8):
                bank, Bb = nb()
                pb = bank[:].bitcast(BF16)
                for j in range(8):
                    ktl, n = j // 4, j % 4
                    kt = kt2 * 2 + ktl
                    P.op("pe", lambda e, pb=pb, j=j, kt=kt, n=n: e.transpose(out=pb[:, j * 128:(j + 1) * 128],
                                                                           in_=ckb[:, kt, n * 128:(n + 1) * 128],
                                                                           identity=identb), [Bckb, B_ident], [Bb])
                eng = evac_eng()
                o_ap = KT[kv][:, :, kt2 * 256:(kt2 + 1) * 256].rearrange("p n (k c) -> p n k c", k=2)
                i_ap = pb.rearrange("p (k n c) -> p n k c", k=2, n=4)
                P.op(eng, copy_op(eng, o_ap, i_ap), [Bb], [BKT[kv]])
            qs = slice(32 * s_, 32 * s_ + 32)
            for n in range(4):
                (o_bank, Bo), (d_bank, Bd) = ob_[n % 2], db_[n % 2]
                o3 = o_bank[:, 0:128].rearrange("p (h q) -> p h q", h=4)
                steps = []
                for kt in range(17):
                    if kt < 16:
                        lk, Bk_, lv, Bv_ = KT[kv][:, n, kt * 128:(kt + 1) * 128], BKT[kv], \
                            Vb[kv][:, kt, n * 128:(n + 1) * 128], BVb[kv]
                    else:
                        lk, Bk_, lv, Bv_ = kTs[:, n, :], BkTs, Vnew[:, n * 128:(n + 1) * 128], BVnew
                    steps.append(dict(
                        lhsT_k=lk, Bk=Bk_, rhs_q=qT2[0][:, 4 * n:4 * n + 4, qs], Bq=BqT2[0], st_w=128,
                        st_out=lambda bank: bank[:, 0:128].rearrange("p (h q) -> p h q", h=4),
                        bias=(mbs[:, kt * 128:(kt + 1) * 128] if kt < 16 else mnew),
                        Bbias=(Bmbs if kt < 16 else Bmnew), bias_rhs=I4[:, :, qs], lhsT_v=lv, Bv=Bv_,
                        pv_out=o_bank[:, 0:128], pd_out=d_bank[:, 0:128], pv_rhs=lambda pt: pt,
                        rec_view=lambda r: r.rearrange("p (h q) -> p h q", h=4)))
                attn_group(steps, o_bank[:, 0:128], d_bank[:, 0:128], Bo, Bd, b_g[:, 4 * n:4 * n + 4, qs], Bbg)
        P.dma("sp", bTv[:, :, NP:N], b_g, reads=[Bbg], writes=[B_bTd])
        P.fence()

    if want("s4"):
        s4()
    if stop == "s4":
        return finish(nc, P, out_bufs)

    def s5():
        AR.reset(base_mark)
        aT = AR.alloc([16, N], BF16)
        bT = AR.alloc([16, N], BF16)
        BaT, BbT = Buf("aT"), Buf("bT")
        aTv = aTd.rearrange("(dc p) n -> p dc n", p=128)
        bTv = bTd.rearrange("(dc p) n -> p dc n", p=128)
        for q in range(4):
            P.dma("sp", aT[:, 4 * q:4 * q + 4, :], aTv[:, 4 * q:4 * q + 4, :], reads=[B_aTd], writes=[BaT])
            P.dma("sp", bT[:, 4 * q:4 * q + 4, :], bTv[:, 4 * q:4 * q + 4, :], reads=[B_bTd], writes=[BbT])
        WC = 256
        wa = [AR.alloc([16, WC], BF16) for _ in range(2)]
        wb_ = [AR.alloc([16, WC], BF16) for _ in range(2)]
        Bwa, Bwb = [Buf("wa0"), Buf("wa1")], [Buf("wbb0"), Buf("wbb1")]
        ga = [AR.alloc([512], BF16) for _ in range(2)]
        gb_ = [AR.alloc([512], BF16) for _ in range(2)]
        Bga, Bgb = [Buf("ga0"), Buf("ga1")], [Buf("gb0"), Buf("gb1")]
        tA = [AR.alloc([512], F32) for _ in range(2)]
        tB = [AR.alloc([512], F32) for _ in range(2)]
        BtA, BtB = [Buf("tA0"), Buf("tA1")], [Buf("tB0"), Buf("tB1")]
        ys = [AR.alloc([512], BF16) for _ in range(2)]
        Bys = [Buf("ys0"), Buf("ys1")]
        wav = w_a.rearrange("(kc p) c -> p kc c", p=128)
        wbv = w_b.rearrange("(kc p) c -> p kc c", p=128)
        nblk = D // WC

        def load(i):
            P.dma("pool", wa[i % 2], wav[:, :, i * WC:(i + 1) * WC], writes=[Bwa[i % 2]])
            P.dma("pool", wb_[i % 2], wbv[:, :, i * WC:(i + 1) * WC], writes=[Bwb[i % 2]])

        load(0)
        it = 0
        for i in range(nblk):
            if i + 1 < nblk:
                load(i + 1)
            for cj in range(WC // 128):
                c = i * WC + cj * 128
                for (tk0, ntk) in TOKG:
                    k = it % 2
                    it += 1
                    P.dma("sp", ga[k][:, 0:ntk], gaT[c:c + 128, tk0:tk0 + ntk], reads=[B_gaT], writes=[Bga[k]])
                    P.dma("sp", gb_[k][:, 0:ntk], gbT[c:c + 128, tk0:tk0 + ntk], reads=[B_gbT], writes=[Bgb[k]])
                    bankA, BbA = next_bank()
                    for kc in range(16):
                        P.op("pe", lambda e, bankA=bankA, kc=kc, cj=cj, tk0=tk0, ntk=ntk, i=i:
                             e.matmul(bankA[:, 0:ntk], lhsT=wa[i % 2][:, kc, cj * 128:(cj + 1) * 128],
                                      rhs=aT[:, kc, tk0:tk0 + ntk], start=(kc == 0), stop=(kc == 15)),
                             [Bwa[i % 2], BaT], [BbA])
                    bankB, BbB = next_bank()
                    for kc in range(16):
                        P.op("pe", lambda e, bankB=bankB, kc=kc, cj=cj, tk0=tk0, ntk=ntk, i=i:
                             e.matmul(bankB[:, 0:ntk], lhsT=wb_[i % 2][:, kc, cj * 128:(cj + 1) * 128],
                                      rhs=bT[:, kc, tk0:tk0 + ntk], start=(kc == 0), stop=(kc == 15)),
                             [Bwb[i % 2], BbT], [BbB])
                    P.op("dve", lambda e, k=k, ntk=ntk, bankA=bankA: e.tensor_tensor(
                        out=tA[k][:, 0:ntk], in0=bankA[:, 0:ntk], in1=ga[k][:, 0:ntk], op=ALU.mult),
                        [BbA, Bga[k]], [BtA[k]])
                    P.op("dve", lambda e, k=k, ntk=ntk, bankB=bankB: e.tensor_tensor(
                        out=tB[k][:, 0:ntk], in0=bankB[:, 0:ntk], in1=gb_[k][:, 0:ntk], op=ALU.mult),
                        [BbB, Bgb[k]], [BtB[k]])
                    P.op("pool", lambda e, k=k, ntk=ntk: e.tensor_tensor(
                        out=ys[k][:, 0:ntk], in0=tA[k][:, 0:ntk], in1=tB[k][:, 0:ntk], op=ALU.add),
                        [BtA[k], BtB[k]], [Bys[k]])
                    P.dma("sp", yT[c:c + 128, tk0:tk0 + ntk], ys[k][:, 0:ntk], reads=[Bys[k]], writes=[B_yT])
        P.fence()

    if want("s5"):
        s5()
    if stop == "s5":
        return finish(nc, P, out_bufs)

    def s6():
        AR.reset(base_mark)
        yTs = AR.alloc([32, N], BF16)
        ByTs = Buf("yTs")
        yTv = yT.rearrange("(kc p) n -> p kc n", p=128)
        for q in range(8):
            P.dma("sp", yTs[:, 4 * q:4 * q + 4, :], yTv[:, 4 * q:4 * q + 4, :], reads=[B_yT], writes=[ByTs])
        xs = [AR.alloc([512], F32) for _ in range(2)]
        Bxs = [Buf("xs0"), Buf("xs1")]
        ctr = [0]

        def epi_T(tag, c0, ncols, t, bank, Bb):
            i = ctr[0] % 2
            ctr[0] += 1
            P.dma("sp", xs[i], xin[t * 128:(t + 1) * 128, c0:c0 + ncols], reads=[B_xin], writes=[Bxs[i]])
            P.op("dve", lambda e, i=i, bank=bank: e.tensor_tensor(out=xs[i], in0=bank[:, 0:512], in1=xs[i], op=ALU.add),
                 [Bb, Bxs[i]], [Bxs[i]])
            P.dma("sp", x2[t * 128:(t + 1) * 128, c0:c0 + ncols], xs[i], reads=[Bxs[i]], writes=[B_x2])

        gemm(yTs, ByTs, 32, w_o, [(512 * j, 512, "T", "o") for j in range(8)], epilogue_T=epi_T)
        P.fence()

    if want("s6"):
        s6()
    if stop == "s6":
        return finish(nc, P, out_bufs)

    AR.reset(base_mark)
    hT = AR.alloc([32, N], BF16)
    B_hT = Buf("hT2")
    if want("s7"):
        norm_transpose(x2, B_x2, g_ffn, hT, B_hT)

    def s8():
        m = AR.mark()
        NCH = 2 * DFF // 128
        par = AR.alloc([NCH, 12], F32)
        Bpar = Buf("par")
        Zl = AR.alloc([NCH, 10], F32)
        BZl = Buf("Zl")
        rowsrc = AR.alloc([2048], F32)
        Brow = Buf("rowsrc")
        for p0 in range(0, 2 * DFF, 2048):
            wdt = min(2048, 2 * DFF - p0)
            P.dma("sp", rowsrc[0:8, 0:wdt], cst[:, p0:p0 + wdt], writes=[Brow])
            P.dma("sp", rowsrc[8:11, 0:wdt], conv_w[:, p0:p0 + wdt], writes=[Brow])
            P.dma("sp", rowsrc[11:12, 0:wdt], conv_b[p0:p0 + wdt].unsqueeze(0), writes=[Brow])
            bank, Bb = next_bank()
            nch = wdt // 128
            for j in range(nch):
                P.op("pe", lambda e, bank=bank, j=j: e.matmul(bank[:, j * 12:(j + 1) * 12],
                                                            lhsT=rowsrc[0:12, j * 128:(j + 1) * 128],
                                                            rhs=identf[0:12, 0:12], start=True, stop=True),
                     [Brow, B_ident], [Bb])
            c0 = p0 // 128
            P.op("dve", lambda e, bank=bank, c0=c0, nch=nch: e.tensor_copy(
                out=par[:, c0:c0 + nch, :], in_=bank[:, 0:nch * 12].rearrange("p (a b) -> p a b", a=nch)),
                [Bb], [Bpar])
        WC = 256
        wb = [AR.alloc([32, WC], BF16) for _ in range(2)]
        Bwb = [Buf("wu0"), Buf("wu1")]
        zb = {k: [AR.alloc([516], F32) for _ in range(2)] for k in ("g", "u")}
        Bzb = {k: [Buf("z%s0" % k), Buf("z%s1" % k)] for k in ("g", "u")}
        cg = AR.alloc([512], F32)
        cu = AR.alloc([512], F32)
        Bcg, Bcu = Buf("cg"), Buf("cu")
        ast = [AR.alloc([512], BF16) for _ in range(2)]
        Bast = [Buf("ast0"), Buf("ast1")]
        wv = w_up.rearrange("(kc p) c -> p kc c", p=128)
        nblk = DFF // 128

        def load(i):
            P.dma("pool", wb[i % 2][:, :, 0:128], wv[:, :, i * 128:(i + 1) * 128], writes=[Bwb[i % 2]])
            P.dma("pool", wb[i % 2][:, :, 128:256], wv[:, :, DFF + i * 128:DFF + (i + 1) * 128], writes=[Bwb[i % 2]])

        load(0)
        it = 0
        for i in range(nblk):
            if i + 1 < nblk:
                load(i + 1)
            w_i, Bw = wb[i % 2], Bwb[i % 2]
            for gi, (tk0, ntk) in enumerate(TOKG):
                bk = {}
                for (k, off) in (("g", 0), ("u", 128)):
                    bank, Bb = next_bank()
                    bk[k] = (bank, Bb)
                    for kc in range(32):
                        P.op("pe", lambda e, bank=bank, kc=kc, off=off, tk0=tk0, ntk=ntk, w_i=w_i:
                             e.matmul(bank[:, 0:ntk], lhsT=w_i[:, kc, off:off + 128], rhs=hT[:, kc, tk0:tk0 + ntk],
                                      start=(kc == 0), stop=(kc == 31)), [Bw, B_hT], [Bb])
                zi = gi % 2
                outc = {}
                for (k, cacc, Bc, ch) in (("g", cg, Bcg, i), ("u", cu, Bcu, nblk + i)):
                    bank, Bb = bk[k]
                    z, Bz = zb[k][zi], Bzb[k][zi]
                    zp, Bzp = zb[k][1 - zi], Bzb[k][1 - zi]
                    w0, w1, w2, bia = par[:, ch, 8:9], par[:, ch, 9:10], par[:, ch, 10:11], par[:, ch, 11:12]
                    sample = (gi == 4)
                    if not sample:
                        P.op("act", lambda e, z=z, bank=bank, ntk=ntk: e.activation(out=z[:, 2:2 + ntk], in_=bank[:, 0:ntk],
                                                                                  func=AF.Copy), [Bb], [Bz])
                        if gi == 0:
                            P.op("pool", lambda e, z=z: e.memset(z[:, 0:2], 0.0), [], [Bz])
                        else:
                            P.op("pool", lambda e, z=z, zp=zp: e.tensor_copy(out=z[:, 0:2], in_=zp[:, 512:514]),
                                 [Bzp], [Bz])
                        P.op("act", lambda e, cacc=cacc, bank=bank, ntk=ntk, w2=w2, bia=bia: e.activation(
                            out=cacc[:, 0:ntk], in_=bank[:, 0:ntk], func=AF.Identity, scale=w2, bias=bia),
                            [Bb, Bpar], [Bc])
                        P.op("dve", lambda e, cacc=cacc, z=z, ntk=ntk, w1=w1: e.scalar_tensor_tensor(
                            out=cacc[:, 0:ntk], in0=z[:, 1:1 + ntk], scalar=w1, in1=cacc[:, 0:ntk],
                            op0=ALU.mult, op1=ALU.add), [Bz, Bc, Bpar], [Bc])
                        P.op("dve", lambda e, cacc=cacc, z=z, ntk=ntk, w0=w0: e.scalar_tensor_tensor(
                            out=cacc[:, 0:ntk], in0=z[:, 0:ntk], scalar=w0, in1=cacc[:, 0:ntk],
                            op0=ALU.mult, op1=ALU.add), [Bz, Bc, Bpar], [Bc])
                        if gi == 3:
                            P.op("pool", lambda e, z=z, ch=ch: e.tensor_copy(out=Zl[:, ch, 0:2], in_=z[:, 512:514]),
                                 [Bz], [BZl])
                    else:
                        z3 = z[:, 0:136].rearrange("p (s c) -> p s c", s=4)
                        P.op("act", lambda e, z3=z3, bank=bank: e.activation(
                            out=z3[:, :, 2:34], in_=bank[:, 0:128].rearrange("p (s c) -> p s c", s=4), func=AF.Copy),
                            [Bb], [Bz])
                        P.op("pool", lambda e, z3=z3, ch=ch: e.tensor_copy(
                            out=z3[:, :, 0:2], in_=par[:, ch, 0:8].rearrange("p (s r) -> p s r", s=4)), [Bpar], [Bz])
                        c3 = cacc[:, 0:128].rearrange("p (s c) -> p s c", s=4)
                        P.op("act", lambda e, cacc=cacc, bank=bank, w2=w2, bia=bia: e.activation(
                            out=cacc[:, 0:128], in_=bank[:, 0:128], func=AF.Identity, scale=w2, bias=bia),
                            [Bb, Bpar], [Bc])
                        P.op("dve", lambda e, c3=c3, z3=z3, w1=w1: e.scalar_tensor_tensor(
                            out=c3, in0=z3[:, :, 1:33], scalar=w1, in1=c3, op0=ALU.mult, op1=ALU.add),
                            [Bz, Bc, Bpar], [Bc])
                        P.op("dve", lambda e, c3=c3, z3=z3, w0=w0: e.scalar_tensor_tensor(
                            out=c3, in0=z3[:, :, 0:32], scalar=w0, in1=c3, op0=ALU.mult, op1=ALU.add),
                            [Bz, Bc, Bpar], [Bc])
                        P.op("pool", lambda e, z3=z3, ch=ch: e.tensor_copy(
                            out=Zl[:, ch, 2:10].rearrange("p (s r) -> p s r", s=4), in_=z3[:, :, 32:34]), [Bz], [BZl])
                P.op("act", lambda e, ntk=ntk: e.activation(out=cg[:, 0:ntk], in_=cg[:, 0:ntk], func=AF.Silu),
                     [Bcg], [Bcg])
                k = it % 2
                it += 1
                P.op("pool", lambda e, k=k, ntk=ntk: e.tensor_tensor(out=ast[k][:, 0:ntk], in0=cg[:, 0:ntk],
                                                                     in1=cu[:, 0:ntk], op=ALU.mult),
                     [Bcg, Bcu], [Bast[k]])
                P.dma("sp", actT[i * 128:(i + 1) * 128, tk0:tk0 + ntk], ast[k][:, 0:ntk], reads=[Bast[k]],
                      writes=[B_actT])
        zo = [rowsrc[:, 0:512], rowsrc[:, 512:1024]]
        Bzo = [Brow, Brow]
        for q in range(NCH // 4):
            bank, Bb = next_bank()
            for j in range(4):
                ch = q * 4 + j
                P.op("pe", lambda e, bank=bank, j=j, ch=ch: e.matmul(bank[0:10, j * 128:(j + 1) * 128], lhsT=Zl[:, ch, :],
                                                                   rhs=identf, start=True, stop=True),
                     [BZl, B_ident], [Bb])
            P.op("dve", lambda e, bank=bank, q=q: e.tensor_copy(out=zo[q % 2][0:10, :], in_=bank[0:10, :]),
                 [Bb], [Bzo[q % 2]])
            ob = Buf("ocst")
            out_bufs.append(ob)
            P.dma("sp", o_cst[:, q * 512:(q + 1) * 512], zo[q % 2][0:10, :], reads=[Bzo[q % 2]], writes=[ob])
        P.fence()
        AR.reset(m)

    if want("s8"):
        s8()
    if stop == "s8":
        return finish(nc, P, out_bufs)

    def s9():
        AR.reset(base_mark)
        KC = DFF // 128
        TB = [(0, 768), (768, 768), (1536, 640)]
        acT = AR.alloc([KC, 768], BF16)
        BacTg = [Buf("acT%d" % g) for g in range((KC + 7) // 8)]
        wb = [AR.alloc([KC, 128], BF16) for _ in range(2)]
        Bwb = [Buf("wd0"), Buf("wd1")]
        fT = [AR.alloc([384], BF16) for _ in range(2)]
        BfT = [Buf("fT0"), Buf("fT1")]
        fst = [AR.alloc([6, 512], F32) for _ in range(2)]
        Bfst = [Buf("fst0"), Buf("fst1")]
        acv = actT.rearrange("(kc p) n -> p kc n", p=128)
        wv = w_down.rearrange("(kc p) c -> p kc c", p=128)
        li = [0]

        def load():
            i = li[0]
            c = i % 32
            if i < 32:
                P.dma("pool", wb[i % 2], wv[:, :, c * 128:(c + 1) * 128], writes=[Bwb[i % 2]])
                P.dma("sp", wdc[c], wb[i % 2], reads=[Bwb[i % 2]], writes=[Bwdc[c]])
            else:
                P.dma("sp", wb[i % 2], wdc[c], reads=[Bwdc[c]], writes=[Bwb[i % 2]])
            li[0] += 1

        ci = 0
        fi = 0
        pend = [None]
        load()
        for bi_, (tb0, ntb) in enumerate(TB):
            for q0 in range(0, KC, 8):
                q1 = min(KC, q0 + 8)
                P.dma("sp", acT[:, q0:q1, 0:ntb], acv[:, q0:q1, tb0:tb0 + ntb], reads=[B_actT], writes=[BacTg[q0 // 8]])
            ntl = ntb // 128
            for c in range(32):
                if li[0] < 32 * len(TB):
                    load()
                w_i, Bw = wb[ci % 2], Bwb[ci % 2]
                ci += 1
                f_s, Bf = fst[(c // 4 + 8 * bi_) % 2], Bfst[(c // 4 + 8 * bi_) % 2]
                for s0 in range(0, ntb, 384):
                    ns = min(384, ntb - s0)
                    bank, Bb = next_bank()
                    for kc in range(KC):
                        P.op("pe", lambda e, bank=bank, kc=kc, s0=s0, ns=ns, w_i=w_i:
                             e.matmul(bank[:, 0:ns], lhsT=w_i[:, kc, :], rhs=acT[:, kc, s0:s0 + ns],
                                      start=(kc == 0), stop=(kc == KC - 1)), [Bw, BacTg[kc // 8]], [Bb])
                    k = fi % 2
                    fi += 1
                    P.op("act", lambda e, k=k, bank=bank, ns=ns: e.activation(out=fT[k][:, 0:ns], in_=bank[:, 0:ns],
                                                                            func=AF.Copy), [Bb], [BfT[k]])
                    last_sub = (s0 + 384 >= ntb)

                    def tail(k=k, ns=ns, s0=s0, c=c, f_s=f_s, Bf=Bf, last_sub=last_sub, tb0=tb0, ntb=ntb, ntl=ntl):
                        bank2, Bb2 = next_bank()
                        pb = bank2[:].bitcast(BF16)
                        for j in range(ns // 128):
                            P.op("pe", lambda e, pb=pb, j=j, k=k: e.transpose(out=pb[:, j * 128:(j + 1) * 128],
                                                                            in_=fT[k][:, j * 128:(j + 1) * 128],
                                                                            identity=identb), [BfT[k], B_ident], [Bb2])
                        tl0 = s0 // 128
                        nj = ns // 128
                        P.op("dve", lambda e, pb=pb, f_s=f_s, tl0=tl0, nj=nj, c=c: e.tensor_copy(
                            out=f_s[:, tl0:tl0 + nj, (c % 4) * 128:(c % 4) * 128 + 128],
                            in_=pb[:, 0:nj * 128].rearrange("p (a b) -> p a b", a=nj)), [Bb2], [Bf])
                        if last_sub and c % 4 == 3:
                            cb = c // 4
                            P.dma("sp", fsc[tb0:tb0 + ntb, cb * 512:(cb + 1) * 512].rearrange("(j p) c -> p j c", p=128),
                                  f_s[:, 0:ntl, :], reads=[Bf], writes=[B_fsc])

                    if pend[0] is not None:
                        pend[0]()
                    pend[0] = tail
            if pend[0] is not None:
                pend[0]()
                pend[0] = None
        P.fence()

    if want("s9"):
        s9()
    if stop == "s9":
        return finish(nc, P, out_bufs)

    def s10():
        AR.reset(base_mark)
        g_bc = AR.alloc([D], F32)
        Bg = Buf("gfin")
        P.dma("sp", g_bc, g_final.partition_broadcast(128), writes=[Bg])
        NB10 = 3
        xa = [AR.alloc([D], F32) for _ in range(NB10)]
        fa = [AR.alloc([D], F32) for _ in range(NB10)]
        Bxa, Bfa = [Buf("xa%d" % i) for i in range(NB10)], [Buf("fa%d" % i) for i in range(NB10)]
        junk = AR.alloc([D], BF16)
        Bjunk = Buf("junk10")
        st = AR.alloc([3 * NT], F32)
        Bst = [Buf("st10_%d" % t) for t in range(NT)]

        def loads(t):
            P.dma("sp", xa[t % NB10], x2[t * 128:(t + 1) * 128, :], reads=[B_x2], writes=[Bxa[t % NB10]])
            P.dma("sp", fa[t % NB10], fsc[t * 128:(t + 1) * 128, :], reads=[B_fsc], writes=[Bfa[t % NB10]])

        loads(0)
        loads(1)
        for t in range(NT):
            if t + 2 < NT:
                loads(t + 2)
            x_t, Bx, f_t, Bf = xa[t % NB10], Bxa[t % NB10], fa[t % NB10], Bfa[t % NB10]
            P.op("pool", lambda e, x_t=x_t, f_t=f_t: e.tensor_tensor(out=x_t, in0=x_t, in1=f_t, op=ALU.add),
                 [Bx, Bf], [Bx])
            ss, sd, rs = st[:, 3 * t:3 * t + 1], st[:, 3 * t + 1:3 * t + 2], st[:, 3 * t + 2:3 * t + 3]
            P.op("act", lambda e, x_t=x_t, ss=ss: e.activation(out=junk, in_=x_t, func=AF.Square, accum_out=ss),
                 [Bx], [Bjunk, Bst[t]])
            P.op("act", lambda e, ss=ss, sd=sd: e.activation(out=sd, in_=ss, func=AF.Sqrt, scale=1.0 / D, bias=EPS),
                 [Bst[t]], [Bst[t]])
            P.op("dve", lambda e, sd=sd, rs=rs: e.reciprocal(out=rs, in_=sd), [Bst[t]], [Bst[t]])
            P.op("dve", lambda e, x_t=x_t, f_t=f_t, rs=rs: e.scalar_tensor_tensor(out=f_t, in0=x_t, scalar=rs, in1=g_bc,
                                                                              op0=ALU.mult, op1=ALU.mult),
                 [Bx, Bst[t], Bg, Bf], [Bf])
            ob = Buf("oy")
            out_bufs.append(ob)
            P.dma("sp", o_y[t * 128:(t + 1) * 128, :], f_t, reads=[Bf], writes=[ob])

    if want("s10"):
        s10()
    return finish(nc, P, out_bufs)


def finish(nc, P, out_bufs):
    P.fence()
    P.emit()
    return nc


def _rope_table(pos, half):
    inv = (np.float32(THETA) ** (-(np.arange(half, dtype=np.float32)) / np.float32(half))).astype(np.float32)
    ang = pos.astype(np.float32)[:, None] * inv[None, :]
    return np.concatenate([np.cos(ang), np.sin(ang)], axis=1).astype(np.float32)


def _consts():
    pos = np.concatenate([np.arange(NP), np.tile(PAST + np.arange(32), 4)]).astype(np.int32)
    bd = np.kron(np.eye(4, dtype=np.float32), np.ones((32, 32), np.float32))
    return {
        "c_ident": np.eye(128, dtype=np.float32),
        "c_csq": _rope_table(pos, 16),
        "c_csi": _rope_table(pos, 8),
        "c_bd": bd,
    }


def make_in_maps(inp, cores=range(8)):
    f = lambda a: np.ascontiguousarray(np.asarray(a, dtype=np.float32))
    shared = {
        "g_attn": f(inp["norm_attn_g"][0]), "w_in": f(inp["w_in"][0]), "g_gmlp": f(inp["gmlp_norm_g"][0]),
        "ws": f(inp["gmlp_ws"][0]), "gbias": f(inp["gmlp_b"][0]), "w_a": f(inp["w_branch_a"][0]),
        "w_b": f(inp["w_branch_b"][0]), "w_o": f(inp["w_out"][0]), "g_ffn": f(inp["norm_ffn_g"][0]),
        "w_up": f(inp["w_up"][0]), "conv_w": f(inp["conv_w"][0]), "conv_b": f(inp["conv_b"][0]),
        "w_down": f(inp["w_down"][0]), "g_final": f(inp["norm_final_g"]),
    }
    shared.update(_consts())
    maps = []
    for c in cores:
        sl = slice(4 * c, 4 * c + 4)
        xs = np.asarray(inp["x_sample"][sl], dtype=np.float32).reshape(NS, D)
        m = dict(shared)
        m["xin"] = np.ascontiguousarray(np.concatenate([np.asarray(inp["x_prompt"][c], dtype=np.float32), xs], axis=0))
        m["ck"] = f(np.asarray(inp["cache_k"][0, sl]).reshape(4, PAST, 512))
        m["cv"] = f(np.asarray(inp["cache_v"][0, sl]).reshape(4, PAST, 512))
        m["cki"] = f(inp["cache_kidx"][0, sl])
        m["cst"] = f(np.asarray(inp["state_ffn_conv"][0, sl]).reshape(8, 2 * DFF))
        maps.append(m)
    return maps


_NC_CACHE = {}


def kernel(**inputs):
    if "nc" not in _NC_CACHE:
        _NC_CACHE["nc"] = build_program()
    nc = _NC_CACHE["nc"]
    maps = make_in_maps(inputs)
    res = run_bass_kernel_spmd(nc, maps, core_ids=list(range(8)))
    R = res.results
    cat = lambda name: [np.asarray(r[name], dtype=np.float32) for r in R]
    y = cat("o_y"); k = cat("o_k"); v = cat("o_v"); ki = cat("o_ki"); cs = cat("o_cst"); vn = cat("o_vn")
    y_prompt = np.stack([a[:NP] for a in y])
    y_sample = np.concatenate([a[NP:].reshape(4, 32, D) for a in y])
    kp = np.stack([a[:NP].reshape(NP, 4, 128) for a in k])[None]
    vp = np.stack([a[:NP].reshape(NP, 4, 128) for a in v])[None]
    kip = np.stack([a[:NP] for a in ki])[None]
    csp = np.stack([a[0:2] for a in cs])[None]
    ks = np.concatenate([a[NP:].reshape(4, 32, 4, 128) for a in k])[None]
    vs = np.concatenate([a[NP:].reshape(4, 32, 4, 128) for a in v])[None]
    kis = np.concatenate([a[NP:].reshape(4, 32, IDD) for a in ki])[None]
    css = np.concatenate([a[2:10].reshape(4, 2, 2 * DFF) for a in cs])[None]
    gv = np.concatenate([a.reshape(4, 32, DA) for a in vn])[None]
    return (y_prompt, y_sample, kp, vp, kip, csp, ks, vs, kis, css, gv)
```
